# Optimizing a Trainium2 kernel written in Bass

```python
import math
import jax
import jax.numpy as jnp
from jax import lax
import numpy as np

D_MODEL = 1024
BATCH = 4
SEQ = 4096
DEPTH = 2

CTX_LEN = 256
GRID_W = 64

D_FF = 2816
MACARON_WEIGHT = 0.5
N_MOD = 9
EPS = 1e-6

GLA_HEADS = 4
GLA_DK = 128
GLA_DV = 256
GLA_LOWRANK = 16
GLA_TAU = 16.0
GLA_CHUNK = 64

HY_WIDTH = 1024
HY_ORDER = 2
HY_SHORT = 3
HY_EMB = 33
HY_HIDDEN = 64
HY_FAST_DECAY = 0.3
HY_SLOW_DECAY = 1.5
HY_DECAY_TARGET = 1e-2

KEY_W = GLA_HEADS * GLA_DK
VAL_W = GLA_HEADS * GLA_DV
ALPHA_W = 2 * GLA_LOWRANK
OFF_K = 0
OFF_V = OFF_K + KEY_W
OFF_A = OFF_V + VAL_W
OFF_Q = OFF_A + ALPHA_W
OFF_R = OFF_Q + KEY_W
OFF_H = OFF_R + VAL_W
OFF_G = OFF_H + (HY_ORDER + 1) * HY_WIDTH
IN_W = OFF_G + 2 * D_MODEL

kernel_name = 'hybrid_gla_hyena_prefix_dit'


def rms_norm(x, g):
    xf = x.astype(jnp.float32)
    y = xf * lax.rsqrt(jnp.mean(xf * xf, axis=-1, keepdims=True) + EPS)
    return (y * g.astype(jnp.float32)).astype(x.dtype)


def modulate(h, shift, scale):
    return h * (1.0 + scale) + shift


def swiglu(h, w_gate, w_up, w_down):
    return (jax.nn.silu(h @ w_gate) * (h @ w_up)) @ w_down


def ffn_sublayer(x, mod, norm_g, w_gate, w_up, w_down):
    shift, scale, gate = mod
    y = swiglu(modulate(rms_norm(x, norm_g), shift, scale), w_gate, w_up, w_down)
    return x + MACARON_WEIGHT * gate * y


def gla_zero_state(batch):
    return jnp.zeros((batch, GLA_HEADS, GLA_DK, GLA_DV), jnp.float32)


def gla_inputs(p, wa_f, ba_f, wa_b, ba_b):
    bsz, L, _ = p.shape
    k = p[..., OFF_K:OFF_V].reshape(bsz, L, GLA_HEADS, GLA_DK)
    v = p[..., OFF_V:OFF_A].reshape(bsz, L, GLA_HEADS, GLA_DV)
    a = p[..., OFF_A:OFF_Q].astype(jnp.float32)
    la_f = jax.nn.log_sigmoid(a[..., :GLA_LOWRANK] @ wa_f.astype(jnp.float32) + ba_f.astype(jnp.float32)) / GLA_TAU
    la_b = jax.nn.log_sigmoid(a[..., GLA_LOWRANK:] @ wa_b.astype(jnp.float32) + ba_b.astype(jnp.float32)) / GLA_TAU
    shape = (bsz, L, GLA_HEADS, GLA_DK)
    return k, v, la_f.reshape(shape), la_b.reshape(shape)


def gla_query(p):
    bsz, L, _ = p.shape
    return p[..., OFF_Q:OFF_R].reshape(bsz, L, GLA_HEADS, GLA_DK)


def gla_chunk_states(k, v, log_a, s0):
    bsz, L, H, DK = k.shape
    n = L // GLA_CHUNK
    resh = lambda t: t.reshape(bsz, n, GLA_CHUNK, H, t.shape[-1]).astype(jnp.float32)
    kc, vc, la = resh(k), resh(v), resh(log_a)
    b = jnp.cumsum(la, axis=2)
    b_last = b[:, :, -1]
    k_end = kc * jnp.exp(b_last[:, :, None] - b)
    du = jnp.einsum('bnchk,bnchv->nbhkv', k_end, vc)
    decay = jnp.moveaxis(jnp.exp(b_last), 1, 0)

    def step(s, inp):
        d, u = inp
        return d[..., None] * s + u, s

    s_fin, s_start = lax.scan(step, s0.astype(jnp.float32), (decay, du))
    return b, kc, vc, s_start, s_fin


def gla_scan(q, k, v, log_a, s0):
    bsz, L, H, DK = q.shape
    b, kc, vc, s_start, s_fin = gla_chunk_states(k, v, log_a, s0)
    qc = q.reshape(bsz, L // GLA_CHUNK, GLA_CHUNK, H, DK).astype(jnp.float32) * (DK ** -0.5)
    q_dec = qc * jnp.exp(b)
    o_inter = jnp.einsum('bnchk,nbhkv->bnchv', q_dec, s_start)
    att = jnp.einsum('bnihk,bnjhk->bnhij', q_dec, kc * jnp.exp(-b))
    lower = jnp.tril(jnp.ones((GLA_CHUNK, GLA_CHUNK), dtype=bool))
    att = jnp.where(lower, att, 0.0)
    o_intra = jnp.einsum('bnhij,bnjhv->bnihv', att, vc)
    o = (o_inter + o_intra).reshape(bsz, L, H, v.shape[-1]).astype(v.dtype)
    return o, s_fin


def flip_seq(t):
    return jnp.flip(t, axis=1)


def gla_bidir(q, k, v, la_f, la_b, s_f0, s_b0):
    o_f, s_f = gla_scan(q, k, v, la_f, s_f0)
    o_b, s_b = gla_scan(flip_seq(q), flip_seq(k), flip_seq(v), flip_seq(la_b), s_b0)
    return o_f + flip_seq(o_b), s_f, s_b


def gla_context_states(k, v, la_f, la_b):
    s0 = gla_zero_state(k.shape[0])
    s_f = gla_chunk_states(k, v, la_f, s0)[-1]
    s_b = gla_chunk_states(flip_seq(k), flip_seq(v), flip_seq(la_b), s0)[-1]
    return s_f, s_b


def short_conv(u, w, b, rows):
    bsz, L, C = u.shape
    seqs = u.reshape(bsz * rows, L // rows, C) if rows is not None else u
    pad = HY_SHORT // 2
    y = lax.conv_general_dilated(seqs, w[:, None, :].astype(u.dtype), window_strides=(1,),
                                 padding=[(pad, pad)], dimension_numbers=('NWC', 'WIO', 'NWC'),
                                 feature_group_count=C)
    return y.reshape(bsz, L, C) + b


def hyena_filters(L, f_w1, f_b1, f_fr1, f_w2, f_b2, f_fr2, f_w3):
    bands = (HY_EMB - 1) // 2
    t = jnp.linspace(0.0, 1.0, L, dtype=jnp.float32)[:, None]
    w = (2.0 * math.pi / L) * jnp.arange(L, dtype=jnp.float32)[:, None]
    f = jnp.linspace(1e-4, bands - 1, bands, dtype=jnp.float32)[None, :]
    z = jnp.concatenate([t, jnp.cos(f * w), -jnp.sin(f * w)], axis=-1)
    h = jnp.sin(f_fr1 * (z @ f_w1 + f_b1))
    h = jnp.sin(f_fr2 * (h @ f_w2 + f_b2))
    h = (h @ f_w3).astype(jnp.float32).reshape(L, 2, HY_ORDER, HY_WIDTH)
    max_decay = math.log(HY_DECAY_TARGET) / HY_FAST_DECAY
    min_decay = math.log(HY_DECAY_TARGET) / HY_SLOW_DECAY
    deltas = jnp.abs(jnp.linspace(min_decay, max_decay, HY_WIDTH, dtype=jnp.float32))
    window = jnp.exp(-t * deltas[None, :])
    return h * window[:, None, None, :]


def long_conv(z, h_fwd, h_bwd, bias):
    bsz, L, W = z.shape
    taps = jnp.concatenate([h_fwd, jnp.zeros((1, W), h_fwd.dtype), jnp.flip(h_bwd[1:], axis=0)], axis=0)
    taps = taps / (jnp.sum(jnp.abs(taps), axis=0, keepdims=True) + EPS)
    zf = z.astype(jnp.float32)
    spec = jnp.fft.rfft(zf, n=2 * L, axis=1) * jnp.fft.rfft(taps, axis=0)[None]
    y = jnp.fft.irfft(spec, n=2 * L, axis=1)[:, :L]
    return (y + bias.astype(jnp.float32) * zf).astype(z.dtype)


def hyena(u, rows, conv_w, conv_b, f_w1, f_b1, f_fr1, f_w2, f_b2, f_fr2, f_w3, bias):
    L = u.shape[1]
    u = short_conv(u, conv_w, conv_b, rows)
    v, x1, x2 = jnp.split(u, HY_ORDER + 1, axis=-1)
    h = hyena_filters(L, f_w1, f_b1, f_fr1, f_w2, f_b2, f_fr2, f_w3)
    z = v
    for o, gate in enumerate((x1, x2)):
        z = gate * long_conv(z, h[:, 0, o], h[:, 1, o], bias[o])
    return z


def mix_output(p, o_gla, rows, gla_norm_g, w_gla_out, hy_params, w_hy_out, w_out):
    bsz, L, _ = p.shape
    o = o_gla.astype(jnp.float32)
    o = o * lax.rsqrt(jnp.mean(o * o, axis=-1, keepdims=True) + EPS)
    o = (o.reshape(bsz, L, VAL_W) * gla_norm_g.astype(jnp.float32)).astype(p.dtype)
    y_gla = (o * jax.nn.silu(p[..., OFF_R:OFF_H])) @ w_gla_out
    y_hy = hyena(p[..., OFF_H:OFF_G], rows, *hy_params) @ w_hy_out
    gates = jax.nn.sigmoid(p[..., OFF_G:IN_W])
    return (gates[..., :D_MODEL] * y_gla + gates[..., D_MODEL:] * y_hy) @ w_out


def setup_inputs(seed: int = 0) -> dict:
    key = jax.random.key(seed)
    ks = iter(jax.random.split(key, 48))
    nrm = lambda shape, s: jax.random.normal(next(ks), shape, jnp.float32) * s
    D, Lr = D_MODEL, GLA_LOWRANK
    return {
        'x': nrm((BATCH, SEQ, D), 1.0),
        'c': nrm((BATCH, D), 1.0),
        'ctx': nrm((BATCH, CTX_LEN, D), 1.0),
        'c_ctx': nrm((D,), 1.0),
        'ada_w': nrm((DEPTH, D, N_MOD * D), 0.5 * D ** -0.5),
        'ada_b': nrm((DEPTH, N_MOD * D), 0.02),
        'ffn1_norm_g': 1.0 + nrm((DEPTH, D), 0.02),
        'ffn1_w_gate': nrm((DEPTH, D, D_FF), D ** -0.5),
        'ffn1_w_up': nrm((DEPTH, D, D_FF), D ** -0.5),
        'ffn1_w_down': nrm((DEPTH, D_FF, D), D_FF ** -0.5),
        'mix_norm_g': 1.0 + nrm((DEPTH, D), 0.02),
        'w_in': nrm((DEPTH, D, IN_W), D ** -0.5),
        'gla_wa_f': nrm((DEPTH, Lr, KEY_W), Lr ** -0.5),
        'gla_ba_f': nrm((DEPTH, KEY_W), 0.1),
        'gla_wa_b': nrm((DEPTH, Lr, KEY_W), Lr ** -0.5),
        'gla_ba_b': nrm((DEPTH, KEY_W), 0.1),
        'gla_norm_g': 1.0 + nrm((DEPTH, VAL_W), 0.02),
        'w_gla_out': nrm((DEPTH, VAL_W, D), VAL_W ** -0.5),
        'hy_conv_w': nrm((DEPTH, HY_SHORT, (HY_ORDER + 1) * HY_WIDTH), HY_SHORT ** -0.5),
        'hy_conv_b': nrm((DEPTH, (HY_ORDER + 1) * HY_WIDTH), 0.02),
        'hy_f_w1': nrm((DEPTH, HY_EMB, HY_HIDDEN), HY_EMB ** -0.5),
        'hy_f_b1': nrm((DEPTH, HY_HIDDEN), 0.1),
        'hy_f_freq1': 1.0 + nrm((DEPTH, HY_HIDDEN), 0.1),
        'hy_f_w2': nrm((DEPTH, HY_HIDDEN, HY_HIDDEN), HY_HIDDEN ** -0.5),
        'hy_f_b2': nrm((DEPTH, HY_HIDDEN), 0.1),
        'hy_f_freq2': 1.0 + nrm((DEPTH, HY_HIDDEN), 0.1),
        'hy_f_w3': nrm((DEPTH, HY_HIDDEN, 2 * HY_ORDER * HY_WIDTH), HY_HIDDEN ** -0.5),
        'hy_bias': nrm((DEPTH, HY_ORDER, HY_WIDTH), 0.5),
        'w_hy_out': nrm((DEPTH, HY_WIDTH, D), HY_WIDTH ** -0.5),
        'w_out': nrm((DEPTH, D, D), D ** -0.5),
        'ffn2_norm_g': 1.0 + nrm((DEPTH, D), 0.02),
        'ffn2_w_gate': nrm((DEPTH, D, D_FF), D ** -0.5),
        'ffn2_w_up': nrm((DEPTH, D, D_FF), D ** -0.5),
        'ffn2_w_down': nrm((DEPTH, D_FF, D), D_FF ** -0.5),
        'final_norm_g': 1.0 + nrm((D,), 0.02),
    }


def reference(x, c, ctx, c_ctx, ada_w, ada_b, ffn1_norm_g, ffn1_w_gate, ffn1_w_up, ffn1_w_down,
              mix_norm_g, w_in, gla_wa_f, gla_ba_f, gla_wa_b, gla_ba_b, gla_norm_g, w_gla_out,
              hy_conv_w, hy_conv_b, hy_f_w1, hy_f_b1, hy_f_freq1, hy_f_w2, hy_f_b2, hy_f_freq2, hy_f_w3,
              hy_bias, w_hy_out, w_out, ffn2_norm_g, ffn2_w_gate, ffn2_w_up, ffn2_w_down, final_norm_g):
    rows = x.shape[1] // GRID_W
    xl, xc = x, ctx
    for i in range(DEPTH):
        last = i == DEPTH - 1
        ml = jnp.split((jax.nn.silu(c) @ ada_w[i] + ada_b[i])[:, None, :], N_MOD, axis=-1)
        mc = jnp.split((jax.nn.silu(c_ctx) @ ada_w[i] + ada_b[i])[None, None, :], N_MOD, axis=-1)
        ffn1 = (ffn1_norm_g[i], ffn1_w_gate[i], ffn1_w_up[i], ffn1_w_down[i])
        ffn2 = (ffn2_norm_g[i], ffn2_w_gate[i], ffn2_w_up[i], ffn2_w_down[i])
        gla_p = (gla_wa_f[i], gla_ba_f[i], gla_wa_b[i], gla_ba_b[i])
        hy_p = (hy_conv_w[i], hy_conv_b[i], hy_f_w1[i], hy_f_b1[i], hy_f_freq1[i], hy_f_w2[i],
                hy_f_b2[i], hy_f_freq2[i], hy_f_w3[i], hy_bias[i])

        xl = ffn_sublayer(xl, ml[0:3], *ffn1)
        xc = ffn_sublayer(xc, mc[0:3], *ffn1)

        hl = modulate(rms_norm(xl, mix_norm_g[i]), ml[3], ml[4])
        hc = modulate(rms_norm(xc, mix_norm_g[i]), mc[3], mc[4])
        pl = hl @ w_in[i]
        if last:
            kc, vc, laf_c, lab_c = gla_inputs(hc @ w_in[i][:, :OFF_Q], *gla_p)
            s_f, s_b = gla_context_states(kc, vc, laf_c, lab_c)
        else:
            pc = hc @ w_in[i]
            kc, vc, laf_c, lab_c = gla_inputs(pc, *gla_p)
            zero = gla_zero_state(xc.shape[0])
            o_c, s_f, s_b = gla_bidir(gla_query(pc), kc, vc, laf_c, lab_c, zero, zero)
            xc = xc + mc[5] * mix_output(pc, o_c, None, gla_norm_g[i], w_gla_out[i], hy_p, w_hy_out[i], w_out[i])
            xc = ffn_sublayer(xc, mc[6:9], *ffn2)
        kl, vl, laf_l, lab_l = gla_inputs(pl, *gla_p)
        o_l, _, _ = gla_bidir(gla_query(pl), kl, vl, laf_l, lab_l, s_f, s_b)
        xl = xl + ml[5] * mix_output(pl, o_l, rows, gla_norm_g[i], w_gla_out[i], hy_p, w_hy_out[i], w_out[i])

        xl = ffn_sublayer(xl, ml[6:9], *ffn2)
    return rms_norm(xl, final_norm_g)
```

```python
import numpy as np
import ml_dtypes
import concourse.bass as bass
import concourse.mybir as mybir
from concourse.bass_utils import run_bass_kernel_spmd

F32 = mybir.dt.float32
BF16 = mybir.dt.bfloat16
AF = mybir.ActivationFunctionType
ALU = mybir.AluOpType
AX = mybir.AxisListType

SEM_EPOCH = 30000
NCORES = 8
D = 1024
DFF = 2816
NFF = DFF // 128
EPS = 1e-6
SEQ = 4096
CTX = 256
NB = 4
TL = SEQ // 2
TC = CTX // 2
TT = TL + TC


class Buf:
    __slots__ = ("w", "r", "name", "excl")

    def __init__(self, name="", excl=False):
        self.w = None
        self.r = []
        self.name = name
        self.excl = excl


class Ctx:
    ENG = ("pe", "dve", "act", "pool", "sp")

    def __init__(self, nc, n_dma_sems=24):
        self.nc = nc
        self.prog = {e: [] for e in self.ENG}
        self.sems = {}
        self.cnt = {e: 0 for e in self.ENG}
        self.seen = {e: {} for e in self.ENG}
        self.n_dma_sems = n_dma_sems
        self.n_hw = (2 * n_dma_sems) // 3
        self.n_dma = {"hw": 0, "sw": 0}
        self.dma_cnt = [0] * n_dma_sems
        self._sid = 0
        self._uid = 0

    def uid(self, p="t"):
        self._uid += 1
        return f"{p}{self._uid}"

    def _sem(self, key):
        if key not in self.sems:
            self._sid += 1
            self.sems[key] = self.nc.alloc_semaphore(f"s{self._sid}")
        return self.sems[key]

    def _next_event(self, eng):
        c = self.cnt[eng]
        self.cnt[eng] = c + 1
        return ((eng, c // SEM_EPOCH), c % SEM_EPOCH + 1, eng)

    def _collect(self, eng, reads, writes, extra=()):
        deps = {}

        def add(ev):
            if ev is None:
                return
            key, val, src = ev
            if src == "pe" and eng == "pe":
                return
            if deps.get(key, 0) < val:
                deps[key] = val
        for b in reads:
            add(b.w)
        for b in writes:
            add(b.w)
            for ev in b.r:
                add(ev)
        for ev in extra:
            add(ev)
        waits = []
        seen = self.seen[eng]
        for key, val in deps.items():
            if seen.get(key, 0) < val:
                seen[key] = val
                waits.append((self._sem(key), val))
        return waits

    def _mark(self, ev, reads, writes):
        for b in writes:
            b.w = ev
            b.r = []
        for b in reads:
            if b in writes:
                continue
            b.r.append(ev)
            if len(b.r) > 48:
                best = {}
                for e in b.r:
                    if best.get(e[0], (0,))[0] < e[1]:
                        best[e[0]] = (e[1], e)
                b.r = [v[1] for v in best.values()]

    def op(self, eng, fn, reads=(), writes=()):
        reads = list(reads)
        writes = list(writes)
        ex = [b for b in reads if b.excl and b not in writes]
        if ex:
            writes = writes + ex
            reads = [b for b in reads if not b.excl]
        waits = self._collect(eng, reads, writes)
        ev = self._next_event(eng)
        sem = self._sem(ev[0])
        self.prog[eng].append((waits, fn, sem, 1))
        self._mark(ev, reads, writes)
        return ev

    def dma(self, q, out, in_, reads=(), writes=(), **kw):
        reads = list(reads)
        writes = list(writes)
        if q == "st":
            prod = {b.w[2] for b in reads if b.w is not None}
            q = "act" if prod == {"act"} else "sp"
        kind = "sw" if q == "pool" else "hw"
        if kind == "hw":
            i = self.n_dma["hw"] % self.n_hw
        else:
            i = self.n_hw + self.n_dma["sw"] % (self.n_dma_sems - self.n_hw)
        self.n_dma[kind] += 1
        c = self.dma_cnt[i]
        per = SEM_EPOCH // 16
        ep, k = divmod(c, per)
        key = ("dma", i, ep)
        prev = []
        if k > 0:
            prev.append((key, 16 * k, "dma"))
        elif ep > 0:
            prev.append((("dma", i, ep - 1), 16 * per, "dma"))
        self.dma_cnt[i] = c + 1
        waits = self._collect(q, reads, writes, extra=prev)
        ev = (key, 16 * (k + 1), "dma")
        sem = self._sem(key)
        fn = lambda e, out=out, in_=in_, kw=kw: e.dma_start(out=out, in_=in_, **kw)
        self.prog[q].append((waits, fn, sem, 16))
        self._mark(ev, reads, writes)
        return ev

    def barrier(self):
        evs = []
        for e in self.ENG:
            cnt = self.cnt[e]
            if cnt > 0:
                cc = cnt - 1
                evs.append(((e, cc // SEM_EPOCH), cc % SEM_EPOCH + 1))
        per = SEM_EPOCH // 16
        for i in range(self.n_dma_sems):
            cnt = self.dma_cnt[i]
            if cnt > 0:
                ep, k = divmod(cnt - 1, per)
                evs.append((("dma", i, ep), 16 * (k + 1)))
        for e in self.ENG:
            waits = []
            for key, val in evs:
                if self.seen[e].get(key, 0) < val:
                    self.seen[e][key] = val
                    waits.append((self._sem(key), val))
            self.prog[e].append((waits, None, None, 0))

    def wait_all(self, eng, bufs):
        waits = self._collect(eng, list(bufs), [])
        self.prog[eng].append((waits, None, None, 0))

    def emit(self):
        nc = self.nc
        eobj = {"pe": "tensor", "dve": "vector", "act": "scalar", "pool": "gpsimd", "sp": "sync"}
        with nc.Block() as block:
            def mk(name):
                def body(e):
                    for waits, fn, sem, inc in self.prog[name]:
                        for s, v in waits:
                            e.wait_ge(s, v)
                        if fn is not None:
                            fn(e).then_inc(sem, inc)
                return body
            for name in self.ENG:
                getattr(block, eobj[name])(mk(name))


class FM:
    def __init__(self, nc, name, nch, T, dtype, blocks):
        self.t = sb_alloc(nc, name, [128, nch, T], dtype)
        self.nch = nch
        self.T = T
        self.blocks = blocks
        self.b = [[Buf(f"{name}{k}_{i}") for i in range(len(blocks))] for k in range(nch)]

    def ap(self, k, bi):
        t0, t1 = self.blocks[bi][0], self.blocks[bi][1]
        return self.t[:, k, t0:t1]


class Rot:
    def __init__(self, nc, name, n, shape, dtype, psum=False):
        self.tiles = []
        for i in range(n):
            if psum:
                t = nc.alloc_psum_tensor(f"{name}{i}", shape, dtype)
            else:
                t = sb_alloc(nc, f"{name}{i}", shape, dtype)
            self.tiles.append((t, Buf(f"{name}{i}")))
        self.i = 0

    def next(self):
        r = self.tiles[self.i % len(self.tiles)]
        self.i += 1
        return r


import contextlib

_STACK = [None]


_NAME_CTR = [0]


def sb_alloc(nc, name, shape, dtype):
    _NAME_CTR[0] += 1
    name = f"{name}_u{_NAME_CTR[0]}"
    if _STACK[0] is not None:
        return _STACK[0].enter_context(nc.sbuf_tensor(name, shape, dtype))
    return nc.alloc_sbuf_tensor(name, shape, dtype)


@contextlib.contextmanager
def phase_stack():
    es = contextlib.ExitStack()
    prev = _STACK[0]
    _STACK[0] = es
    try:
        yield es
    finally:
        _STACK[0] = prev
        es.close()


class View:
    def __init__(self, t, off, width):
        self.t, self.off, self.width = t, off, width

    def __getitem__(self, idx):
        if not isinstance(idx, tuple):
            idx = (idx, slice(None))
        ps, fs = idx
        a = 0 if fs.start is None else fs.start
        b = self.width if fs.stop is None else fs.stop
        return self.t[ps, self.off + a:self.off + b]


class RotPsumHalf:
    def __init__(self, nc, name, nbanks, width=256):
        self.tiles = []
        per = 512 // width
        banks = []
        for i in range(nbanks):
            t = nc.alloc_psum_tensor(f"{name}{i}", [128, 512], F32)
            banks.append((t, Buf(f"{name}{i}", excl=True)))
        for h in range(per):
            for t, b in banks:
                self.tiles.append((View(t, h * width, width), b))
        self.i = 0

    def next(self):
        r = self.tiles[self.i % len(self.tiles)]
        self.i += 1
        return r


def make_psum(nc, name="psg"):
    r = Rot(nc, name, 8, [128, 512], F32, psum=True)
    for t, b in r.tiles:
        b.excl = True
    return r


class PsumHalves:
    def __init__(self, rot, nbanks=7, width=256):
        self.tiles = []
        per = 512 // width
        banks = rot.tiles[:nbanks]
        for h in range(per):
            for t, b in banks:
                self.tiles.append((View(t, h * width, width), b))
        self.i = 0

    def next(self):
        r = self.tiles[self.i % len(self.tiles)]
        self.i += 1
        return r


class RotSlice:
    def __init__(self, rot, idx):
        self.tiles = [rot.tiles[i] for i in idx]
        self.i = 0

    def next(self):
        r = self.tiles[self.i % len(self.tiles)]
        self.i += 1
        return r


def make_blocks(segs):
    out = []
    for s, e, sid in segs:
        t = s
        while t < e:
            n = min(512, e - t)
            out.append((t, t + n, sid))
            t += n
    return out


class Env:
    pass


def setup_common(nc, c, psum=None):
    env = Env()
    env.nc = nc
    env.c = c
    env.ones = sb_alloc(nc, "ones_bf", [128, 128], BF16)
    env.b_ones = Buf("ones")
    c.op("pool", lambda e: e.memset(env.ones[:], 1.0), writes=[env.b_ones])
    if psum is None:
        env.ps_mm = Rot(nc, "psmm", 4, [128, 512], F32, psum=True)
        env.ps_d = Rot(nc, "psd", 2, [128, 512], F32, psum=True)
        env.ps_ss = Rot(nc, "psss", 1, [128, 512], F32, psum=True)
    else:
        env.ps_mm = RotSlice(psum, [0, 1, 2, 3])
        env.ps_d = RotSlice(psum, [4, 5])
        env.ps_ss = RotSlice(psum, [6])
    env.sq = Rot(nc, "sq", 2, [128, 512], BF16)
    env.tmpf = Rot(nc, "tmpf", 3, [128, 512], F32)
    env.tmpr = Rot(nc, "tmpr", 2, [128, 512], F32)
    env.tmpb = Rot(nc, "tmpb", 3, [128, 512], BF16)
    env.epsb = sb_alloc(nc, "epsb", [128, 1], F32)
    env.b_eps = Buf("eps")
    c.op("pool", lambda e: e.memset(env.epsb[:], EPS), writes=[env.b_eps])
    return env


class FMRot:
    def __init__(self, nc, name, nch, dtype, blocks, n=2):
        self.tiles = [sb_alloc(nc, f"{name}{i}", [128, nch, 512], dtype) for i in range(n)]
        self.bufs = [[Buf() for _ in range(nch)] for i in range(n)]
        self.n = n
        self.nch = nch
        self.blocks = blocks
        self.bo = Buf()

        class _B:
            def __init__(s2, outer):
                s2.o = outer

            def __getitem__(s2, k):
                class _R:
                    def __getitem__(s3, bi):
                        return s2.o.bufs[bi % s2.o.n][k]
                return _R()
        self.b = _B(self)

    def ap(self, k, bi):
        t0, t1 = self.blocks[bi][0], self.blocks[bi][1]
        return self.tiles[bi % self.n][:, k, 0:t1 - t0]

    def flush_to(self, c, dst_of, bi, q="sp"):
        t0, t1 = self.blocks[bi][0], self.blocks[bi][1]
        c.dma(q, dst_of(t0, t1), self.tiles[bi % self.n][:, :, 0:t1 - t0],
              reads=self.bufs[bi % self.n], writes=[self.bo])

    def flush(self, c, dst, bi, q="sp"):
        t0, t1 = self.blocks[bi][0], self.blocks[bi][1]
        c.dma(q, dst[:, :, t0:t1], self.tiles[bi % self.n][:, :, 0:t1 - t0],
              reads=self.bufs[bi % self.n], writes=[self.bo])


def emit_norm_mod(env, X, OUT, A_ap, S_ap, b_AS, nfeat=D, out_cb=None):
    c = env.c
    nch = X.nch
    for bi, (t0, t1, sid) in enumerate(X.blocks):
        n = t1 - t0
        ps, bps = env.ps_ss.next()
        for k in range(nch):
            sq, bsq = env.sq.next()
            c.op("act", lambda e, sq=sq, k=k, bi=bi, n=n: e.activation(out=sq[:, :n], in_=X.ap(k, bi), func=AF.Square),
                 reads=[X.b[k][bi]], writes=[bsq])
            c.op("pe", lambda e, ps=ps, sq=sq, k=k, n=n: e.matmul(ps[:, :n], lhsT=env.ones[:], rhs=sq[:, :n],
                                                                   start=(k == 0), stop=(k == nch - 1)),
                 reads=[bsq, env.b_ones], writes=[bps])
        ln, bln = env.tmpr.next()
        c.op("act", lambda e, ln=ln, ps=ps, n=n: e.activation(out=ln[:, :n], in_=ps[:, :n], func=AF.Ln,
                                                               bias=env.epsb[:, 0:1], scale=1.0 / nfeat),
             reads=[bps, env.b_eps], writes=[bln])
        rs, brs = env.tmpr.next()
        c.op("act", lambda e, ln=ln, rs=rs, n=n: e.activation(out=rs[:, :n], in_=ln[:, :n], func=AF.Exp, scale=-0.5),
             reads=[bln], writes=[brs])
        for k in range(nch):
            t2, bt2 = env.tmpf.next()
            c.op("dve", lambda e, t2=t2, rs=rs, k=k, bi=bi, n=n, sid=sid: e.scalar_tensor_tensor(
                out=t2[:, :n], in0=X.ap(k, bi), scalar=A_ap(k, sid), in1=rs[:, :n], op0=ALU.mult, op1=ALU.mult),
                reads=[X.b[k][bi], brs, b_AS], writes=[bt2])
            if S_ap is None:
                c.op("act", lambda e, t2=t2, k=k, bi=bi, n=n: e.copy(out=OUT.ap(k, bi), in_=t2[:, :n]),
                     reads=[bt2], writes=[OUT.b[k][bi]])
            else:
                c.op("act", lambda e, t2=t2, k=k, bi=bi, n=n, sid=sid: e.activation(
                    out=OUT.ap(k, bi), in_=t2[:, :n], func=AF.Identity, bias=S_ap(k, sid), scale=1.0),
                    reads=[bt2, b_AS], writes=[OUT.b[k][bi]])
        if out_cb is not None:
            out_cb(bi)


def load_w_panel(env, rot, w_ap, r0, nrow, c0, ncol, q="pool"):
    t, b = rot.next()
    kc = nrow // 128
    src = w_ap[r0:r0 + nrow, c0:c0 + ncol].rearrange("(k p) n -> p k n", p=128)
    env.c.dma(q, t[:, 0:kc, 0:ncol], src, writes=[b])
    return t, b


def emit_ffn(env, H, X, U, wg, wu, wd, CG_ap, b_CG, wpan, wdpan):
    c = env.c
    nhalf = U.nch
    j0 = 0
    while j0 < NFF:
        jn = min(nhalf, NFF - j0)
        wds = []
        for jl in range(jn):
            j = j0 + jl
            tg, bg = load_w_panel(env, wpan, wg, 0, D, j * 128, 128)
            tu, bu = load_w_panel(env, wpan, wu, 0, D, j * 128, 128)
            td, bd = load_w_panel(env, wdpan, wd, j * 128, 128, 0, D)
            wds.append((td, bd))
            for bi, (t0, t1, sid) in enumerate(H.blocks):
                n = t1 - t0
                pg, bpg = env.ps_mm.next()
                pu, bpu = env.ps_mm.next()
                for k in range(8):
                    c.op("pe", lambda e, pg=pg, tg=tg, k=k, bi=bi, n=n: e.matmul(
                        pg[:, :n], lhsT=tg[:, k, :], rhs=H.ap(k, bi), start=(k == 0), stop=(k == 7)),
                        reads=[bg, H.b[k][bi]], writes=[bpg])
                for k in range(8):
                    c.op("pe", lambda e, pu=pu, tu=tu, k=k, bi=bi, n=n: e.matmul(
                        pu[:, :n], lhsT=tu[:, k, :], rhs=H.ap(k, bi), start=(k == 0), stop=(k == 7)),
                        reads=[bu, H.b[k][bi]], writes=[bpu])
                sg, bsg = env.tmpb.next()
                c.op("act", lambda e, sg=sg, pg=pg, n=n: e.activation(out=sg[:, :n], in_=pg[:, :n], func=AF.Silu),
                     reads=[bpg], writes=[bsg])
                c.op("dve", lambda e, sg=sg, pu=pu, jl=jl, bi=bi, n=n: e.tensor_tensor(
                    out=U.ap(jl, bi), in0=sg[:, :n], in1=pu[:, :n], op=ALU.mult),
                    reads=[bsg, bpu], writes=[U.b[jl][bi]])
        for dk in range(8):
            for bi, (t0, t1, sid) in enumerate(X.blocks):
                n = t1 - t0
                pd, bpd = env.ps_d.next()
                for jl in range(jn):
                    td, bd = wds[jl]
                    c.op("pe", lambda e, pd=pd, td=td, jl=jl, dk=dk, bi=bi, n=n: e.matmul(
                        pd[:, :n], lhsT=td[:, 0, dk * 128:(dk + 1) * 128], rhs=U.ap(jl, bi),
                        start=(jl == 0), stop=(jl == jn - 1)),
                        reads=[bd, U.b[jl][bi]], writes=[bpd])
                c.op("dve", lambda e, pd=pd, dk=dk, bi=bi, n=n, sid=sid: e.scalar_tensor_tensor(
                    out=X.ap(dk, bi), in0=pd[:, :n], scalar=CG_ap(dk, sid), in1=X.ap(dk, bi),
                    op0=ALU.mult, op1=ALU.add),
                    reads=[bpd, b_CG, X.b[dk][bi]], writes=[X.b[dk][bi]])
        j0 += jn


ADA_COLS = 2 * 9 * D // NCORES


def build_ada():
    nc = bass.Bass("TRN2", target_bir_lowering=False)
    cs = nc.dram_tensor("cs", [128, 8, 8], F32, kind="ExternalInput").ap()
    w = nc.dram_tensor("w", [D, ADA_COLS], F32, kind="ExternalInput").ap()
    bb = nc.dram_tensor("bb", [8, ADA_COLS], F32, kind="ExternalInput").ap()
    out = nc.dram_tensor("out", [8, ADA_COLS], F32, kind="ExternalOutput").ap()
    c = Ctx(nc)
    cst = sb_alloc(nc, "cst", [128, 8, 8], F32)
    sct = sb_alloc(nc, "sct", [128, 8, 8], F32)
    bt = sb_alloc(nc, "bt", [8, ADA_COLS], F32)
    ot = sb_alloc(nc, "ot", [8, ADA_COLS], F32)
    b_cs, b_sc, b_bt, b_ot, b_out = Buf(), Buf(), Buf(), Buf(), Buf()
    c.dma("sp", cst[:], cs, writes=[b_cs])
    c.dma("sp", bt[:], bb, writes=[b_bt])
    c.op("act", lambda e: e.activation(out=sct[:], in_=cst[:], func=AF.Silu), reads=[b_cs], writes=[b_sc])
    wrot = Rot(nc, "wp", 3, [128, 8, 512], F32)
    prot = Rot(nc, "pp", 2, [8, 512], F32, psum=True)
    col = 0
    while col < ADA_COLS:
        n = min(512, ADA_COLS - col)
        wt, bw = wrot.next()
        c.dma("sp", wt[:, :, 0:n], w[:, col:col + n].rearrange("(k p) n -> p k n", p=128), writes=[bw])
        ps, bps = prot.next()
        for k in range(8):
            c.op("pe", lambda e, ps=ps, wt=wt, k=k, n=n: e.matmul(ps[:, :n], lhsT=sct[:, k, :], rhs=wt[:, k, 0:n],
                                                                   start=(k == 0), stop=(k == 7)),
                 reads=[b_sc, bw], writes=[bps])
        c.op("dve", lambda e, ps=ps, col=col, n=n: e.tensor_tensor(out=ot[:, col:col + n], in0=ps[:, :n],
                                                                    in1=bt[:, col:col + n], op=ALU.add),
             reads=[bps, b_bt], writes=[b_ot])
        col += n
    c.dma("sp", out, ot[:], reads=[b_ot], writes=[b_out])
    c.wait_all("sp", [b_out])
    c.emit()
    return nc


def ffn_dram_weights(nc, n_ffn, tag=""):
    ws = []
    for i in range(n_ffn):
        ws.append((nc.dram_tensor(f"wg{i}{tag}", [D, DFF], F32, kind="ExternalInput").ap(),
                   nc.dram_tensor(f"wu{i}{tag}", [D, DFF], F32, kind="ExternalInput").ap(),
                   nc.dram_tensor(f"wd{i}{tag}", [DFF, D], F32, kind="ExternalInput").ap()))
    return ws


def emit_ffn_chain(nc, c, env, n_ffn, tail, T, segs, xsrc_of, mt, b_m, gt, b_g, ws, xdst_of, hdsts=()):
    blocks = make_blocks(segs)
    X = FM(nc, "X", 8, T, F32, blocks)
    H = FM(nc, "H", 8, T, BF16, blocks)
    U = FM(nc, "U", 11, T, BF16, blocks)
    wpan = Rot(nc, "wpan", 4, [128, 8, 128], BF16)
    wdpan = Rot(nc, "wdpan", 12, [128, 1, D], BF16)
    for bi, (t0, t1, sid) in enumerate(blocks):
        c.dma("sp", X.t[:, :, t0:t1], xsrc_of(t0, t1), writes=[X.b[k][bi] for k in range(8)])
    der = sb_alloc(nc, "der", [128, 8, 2, 2 * (n_ffn + 1)], F32)
    b_der = Buf()
    nst = n_ffn + (1 if tail == "mixnorm" else 0)
    for i in range(n_ffn + 1):
        for sid in range(2):
            if i < nst:
                sc_col = 3 * i + 1
                c.op("dve", lambda e, i=i, sid=sid, sc_col=sc_col: e.scalar_tensor_tensor(
                    out=der[:, :, sid, 2 * i], in0=mt[:, :, sid, sc_col], scalar=1.0, in1=gt[:, :, i],
                    op0=ALU.add, op1=ALU.mult), reads=[b_m, b_g], writes=[b_der])
            else:
                c.op("dve", lambda e, i=i, sid=sid: e.tensor_copy(out=der[:, :, sid, 2 * i], in_=gt[:, :, i]),
                     reads=[b_g], writes=[b_der])
            if i < n_ffn:
                c.op("dve", lambda e, i=i, sid=sid: e.tensor_scalar(
                    out=der[:, :, sid, 2 * i + 1], in0=mt[:, :, sid, 3 * i + 2], scalar1=0.5, scalar2=None,
                    op0=ALU.mult), reads=[b_m], writes=[b_der])
    for i in range(n_ffn):
        emit_norm_mod(env, X, H, lambda k, sid, i=i: der[:, k, sid, 2 * i:2 * i + 1],
                      lambda k, sid, i=i: mt[:, k, sid, 3 * i:3 * i + 1], b_der)
        emit_ffn(env, H, X, U, ws[i][0], ws[i][1], ws[i][2],
                 lambda k, sid, i=i: der[:, k, sid, 2 * i + 1:2 * i + 2], b_der, wpan, wdpan)
    i = n_ffn
    bo = Buf()
    if tail == "mixnorm":
        emit_norm_mod(env, X, H, lambda k, sid: der[:, k, sid, 2 * i:2 * i + 1],
                      lambda k, sid: mt[:, k, sid, 3 * i:3 * i + 1], b_der)
        for bi, (t0, t1, sid) in enumerate(blocks):
            for hdst_of in hdsts:
                c.dma("sp", hdst_of(t0, t1), H.t[:, :, t0:t1], reads=[H.b[k][bi] for k in range(8)], writes=[bo])
            c.dma("sp", xdst_of(t0, t1), X.t[:, :, t0:t1], reads=[X.b[k][bi] for k in range(8)], writes=[bo])
    else:
        O = FMRot(nc, "O", 8, F32, blocks, n=1)
        O.bo = bo
        emit_norm_mod(env, X, O, lambda k, sid: der[:, k, sid, 2 * i:2 * i + 1], None, b_der,
                      out_cb=lambda bi: O.flush_to(c, xdst_of, bi))
    return bo


def build_ffn_prog(n_ffn, tail, T, segs):
    nc = bass.Bass("TRN2", target_bir_lowering=False)
    nm = 3 * n_ffn + 2
    xT = nc.dram_tensor("xT", [D, T], F32, kind="ExternalInput").ap()
    mods = nc.dram_tensor("mods", [128, 8, 2, nm], F32, kind="ExternalInput").ap()
    gv = nc.dram_tensor("gv", [128, 8, n_ffn + 1], F32, kind="ExternalInput").ap()
    ws = ffn_dram_weights(nc, n_ffn)
    xo = nc.dram_tensor("xo", [D, T], F32, kind="ExternalOutput").ap()
    hdsts = []
    if tail == "mixnorm":
        ho = nc.dram_tensor("ho", [D, T], BF16, kind="ExternalOutput").ap()
        hd = ho.rearrange("(k p) t -> p k t", p=128)
        hdsts.append(lambda t0, t1: hd[:, :, t0:t1])
    c = Ctx(nc)
    env = setup_common(nc, c)
    mt = sb_alloc(nc, "mt", [128, 8, 2, nm], F32)
    gt = sb_alloc(nc, "gt", [128, 8, n_ffn + 1], F32)
    b_m, b_g = Buf(), Buf()
    c.dma("sp", mt[:], mods, writes=[b_m])
    c.dma("sp", gt[:], gv, writes=[b_g])
    xs = xT.rearrange("(k p) t -> p k t", p=128)
    xd = xo.rearrange("(k p) t -> p k t", p=128)
    bo = emit_ffn_chain(nc, c, env, n_ffn, tail, T, segs, lambda t0, t1: xs[:, :, t0:t1], mt, b_m, gt, b_g, ws,
                        lambda t0, t1: xd[:, :, t0:t1], hdsts)
    c.wait_all("sp", [bo])
    c.emit()
    return nc


def fmv(v):
    v = np.asarray(v, np.float32)
    lead = v.shape[:-1]
    a = v.reshape(lead + (v.shape[-1] // 128, 128))
    a = np.moveaxis(a, [-1, -2], [0, 1])
    return np.ascontiguousarray(a)


_PROGS = {}


def get_prog(key, builder):
    if key not in _PROGS:
        _PROGS[key] = builder()
    return _PROGS[key]


def run(nc, in_maps):
    res = run_bass_kernel_spmd(nc, in_maps, core_ids=list(range(NCORES)))
    return res.results


def run_ada(inp):
    cs = np.zeros((8, D), np.float32)
    cs[0:4] = inp["c"]
    cs[4] = inp["c_ctx"]
    csf = fmv(cs.T.reshape(D, 8).T) if False else np.ascontiguousarray(
        cs.reshape(8, 8, 128).transpose(2, 1, 0))
    W = np.concatenate([inp["ada_w"][0], inp["ada_w"][1]], axis=1)
    bvec = np.concatenate([inp["ada_b"][0], inp["ada_b"][1]])
    nc = get_prog("ada", build_ada)
    maps = []
    for i in range(NCORES):
        sl = slice(i * ADA_COLS, (i + 1) * ADA_COLS)
        maps.append({"cs": csf, "w": np.ascontiguousarray(W[:, sl]),
                     "bb": np.ascontiguousarray(np.broadcast_to(bvec[sl], (8, ADA_COLS)))})
    res = run(nc, maps)
    full = np.concatenate([r["out"] for r in res], axis=1)[:5]
    return np.ascontiguousarray(full.reshape(5, 2, 9, D).transpose(1, 0, 2, 3))


def core_tokens_T(xl, xc, core, with_ctx=True):
    b, hf = divmod(core, 2)
    parts = []
    if with_ctx:
        parts.append(xc[b, hf * TC:(hf + 1) * TC])
    parts.append(xl[b, hf * TL:(hf + 1) * TL])
    return np.ascontiguousarray(np.concatenate(parts, axis=0).T)


def mods_table(mods_l, core, cols):
    b = core // 2
    out = np.zeros((128, 8, 2, len(cols)), np.float32)
    for sid, row in enumerate((4, b)):
        for j, m in enumerate(cols):
            if m is not None:
                out[:, :, sid, j] = mods_l[row, m].reshape(8, 128).T
    return out


def mix_dram_weights(nc, tag=""):
    return {"wgla": nc.dram_tensor(f"wgla{tag}", [D, D], F32, kind="ExternalInput").ap(),
            "why": nc.dram_tensor(f"why{tag}", [D, D], F32, kind="ExternalInput").ap(),
            "wgt": nc.dram_tensor(f"wgt{tag}", [D, 2 * D], F32, kind="ExternalInput").ap(),
            "wout": nc.dram_tensor(f"wout{tag}", [D, D], F32, kind="ExternalInput").ap()}


def emit_mix(nc, c, env, T, segs, src_of, xdst_of, mt, b_m, wd, z_tm_of=None, in_bufs=()):
    blocks = make_blocks(segs)
    in_bufs = list(in_bufs)
    W = {}
    for name, key, ncol in (("gla", "wgla", D), ("hy", "why", D), ("g", "wgt", 2 * D), ("out", "wout", D)):
        t = sb_alloc(nc, f"w_{name}", [128, 8, ncol], BF16)
        bw = Buf()
        for c0 in range(0, ncol, 512):
            c.dma("pool", t[:, :, c0:c0 + 512], wd[key][:, c0:c0 + 512].rearrange("(k p) n -> p k n", p=128), writes=[bw])
        W[name] = (t, bw)
    inb = {n: Rot(nc, f"in_{n}", 2, [128, 8, 512], BF16) for n in ("h", "og", "z")}
    xin = Rot(nc, "in_x", 2, [128, 8, 512], F32)
    mrot = Rot(nc, "mblk", 2, [128, 8, 512], BF16)
    if z_tm_of is not None:
        ztm = Rot(nc, "ztm", 2, [128, D], BF16)
    bo = Buf()
    for bi, (t0, t1, sid) in enumerate(blocks):
        n = t1 - t0
        cur = {}
        for nm in ("h", "og", "z"):
            t, bt = inb[nm].next()
            if nm == "z" and z_tm_of is not None:
                for j in range(0, n, 128):
                    zt, bzt = ztm.next()
                    c.dma("sp", zt[:], z_tm_of(t0 + j, t0 + j + 128), reads=in_bufs, writes=[bzt])
                    for kc in range(8):
                        p, bp = env.ps_d.next()
                        c.op("pe", lambda e, p=p, zt=zt, kc=kc: e.matmul(
                            p[:, 0:128], lhsT=zt[:, kc * 128:(kc + 1) * 128], rhs=env.ident[:], start=True, stop=True),
                            reads=[bzt, env.b_ident], writes=[bp])
                        c.op("act", lambda e, p=p, t=t, kc=kc, j=j: e.copy(out=t[:, kc, j:j + 128], in_=p[:, 0:128]),
                             reads=[bp], writes=[bt])
            else:
                c.dma("sp", t[:, :, 0:n], src_of[nm](t0, t1), reads=in_bufs, writes=[bt])
            cur[nm] = (t, bt)
        xt, bx = xin.next()
        c.dma("sp", xt[:, :, 0:n], src_of["x"](t0, t1), reads=in_bufs, writes=[bx])
        mt_, bmb = mrot.next()
        for dk in range(8):
            acc = []
            for wname, inname, coff in (("gla", "og", 0), ("hy", "z", 0), ("g", "h", 0), ("g", "h", D)):
                ps, bps = env.ps_mm.next()
                wt, bw = W[wname]
                it, ib = cur[inname]
                for k in range(8):
                    c.op("pe", lambda e, ps=ps, wt=wt, it=it, k=k, dk=dk, coff=coff, n=n: e.matmul(
                        ps[:, :n], lhsT=wt[:, k, coff + dk * 128:coff + (dk + 1) * 128], rhs=it[:, k, 0:n],
                        start=(k == 0), stop=(k == 7)), reads=[bw, ib], writes=[bps])
                acc.append((ps, bps))
            prods = []
            for (py, bpy), (pg, bpg) in ((acc[0], acc[2]), (acc[1], acc[3])):
                sg, bsg = env.tmpf.next()
                c.op("act", lambda e, sg=sg, pg=pg, n=n: e.activation(out=sg[:, :n], in_=pg[:, :n], func=AF.Sigmoid),
                     reads=[bpg], writes=[bsg])
                c.op("dve", lambda e, sg=sg, py=py, n=n: e.tensor_tensor(out=sg[:, :n], in0=sg[:, :n], in1=py[:, :n],
                                                                           op=ALU.mult),
                     reads=[bsg, bpy], writes=[bsg])
                prods.append((sg, bsg))
            c.op("pool", lambda e, a=prods[0][0], b2=prods[1][0], mt_=mt_, dk=dk, n=n: e.tensor_tensor(
                out=mt_[:, dk, 0:n], in0=a[:, :n], in1=b2[:, :n], op=ALU.add),
                reads=[prods[0][1], prods[1][1]], writes=[bmb])
        wt, bw = W["out"]
        for dk in range(8):
            ps, bps = env.ps_d.next()
            for k in range(8):
                c.op("pe", lambda e, ps=ps, wt=wt, mt_=mt_, k=k, dk=dk, n=n: e.matmul(
                    ps[:, :n], lhsT=wt[:, k, dk * 128:(dk + 1) * 128], rhs=mt_[:, k, 0:n],
                    start=(k == 0), stop=(k == 7)), reads=[bw, bmb], writes=[bps])
            c.op("dve", lambda e, ps=ps, xt=xt, dk=dk, n=n, sid=sid: e.scalar_tensor_tensor(
                out=xt[:, dk, 0:n], in0=ps[:, :n], scalar=mt[:, dk, sid, 0:1], in1=xt[:, dk, 0:n],
                op0=ALU.mult, op1=ALU.add), reads=[bps, b_m, bx], writes=[bx])
        c.dma("sp", xdst_of(t0, t1), xt[:, :, 0:n], reads=[bx], writes=[bo])
    return bo


def build_mix_prog(T, segs, z_token_major=False):
    nc = bass.Bass("TRN2", target_bir_lowering=False)
    aps = {}
    for nm, dt in (("xT", F32), ("hT", BF16), ("ogT", BF16)):
        aps[nm] = nc.dram_tensor(nm, [D, T], dt, kind="ExternalInput").ap().rearrange("(k p) t -> p k t", p=128)
    ztm = None
    if z_token_major:
        ztm = nc.dram_tensor("zTM", [T, D], BF16, kind="ExternalInput").ap()
        identd = nc.dram_tensor("ident", [128, 128], BF16, kind="ExternalInput").ap()
    else:
        aps["zT"] = nc.dram_tensor("zT", [D, T], BF16, kind="ExternalInput").ap().rearrange("(k p) t -> p k t", p=128)
    mods = nc.dram_tensor("mods", [128, 8, 2, 1], F32, kind="ExternalInput").ap()
    wd = mix_dram_weights(nc)
    xo = nc.dram_tensor("xo", [D, T], F32, kind="ExternalOutput").ap().rearrange("(k p) t -> p k t", p=128)
    c = Ctx(nc)
    env = setup_common(nc, c)
    if z_token_major:
        load_ident(nc, c, env, identd)
    mt = sb_alloc(nc, "mt", [128, 8, 2, 1], F32)
    b_m = Buf()
    c.dma("sp", mt[:], mods, writes=[b_m])
    src_of = {"x": lambda t0, t1: aps["xT"][:, :, t0:t1], "h": lambda t0, t1: aps["hT"][:, :, t0:t1],
              "og": lambda t0, t1: aps["ogT"][:, :, t0:t1]}
    if not z_token_major:
        src_of["z"] = lambda t0, t1: aps["zT"][:, :, t0:t1]
    bo = emit_mix(nc, c, env, T, segs, src_of, lambda t0, t1: xo[:, :, t0:t1], mt, b_m, wd,
                  z_tm_of=(lambda t0, t1: ztm[t0:t1, :]) if z_token_major else None)
    c.wait_all("sp", [bo])
    c.emit()
    return nc


def load_ident(nc, c, env, ident_dram):
    env.ident = sb_alloc(nc, "ident_bf", [128, 128], BF16)
    env.b_ident = Buf()
    c.dma("sp", env.ident[:], ident_dram, writes=[env.b_ident])


TF = CTX + SEQ
NT = TF // 128
GLA_DK = 128
GLA_DV = 256


def gla_consts():
    j = np.arange(128)[:, None]
    i = np.arange(128)[None, :]
    same = (j // 64) == (i // 64)
    s = -1.0 / 16.0
    cs = {
        "tri_inc_f": (same & (j <= i)) * s, "tri_exc_f": (same & (j > i)) * s,
        "tri_inc_b": (same & (j >= i)) * s, "tri_exc_b": (same & (j < i)) * s,
        "mask_f": (same & (j <= i)) * 1.0, "mask_b": (same & (j >= i)) * 1.0,
        "colm0": np.broadcast_to(i < 64, (128, 128)) * 1.0, "colm1": np.broadcast_to(i >= 64, (128, 128)) * 1.0,
        "rowm0": np.broadcast_to(j < 64, (128, 128)) * 1.0, "rowm1": np.broadcast_to(j >= 64, (128, 128)) * 1.0,
    }
    return {k: np.ascontiguousarray(v.astype(np.float32)) for k, v in cs.items()}


def gla_dram_inputs(nc, hT=None, tag="", shared=None):
    d = {}
    d["hT"] = hT if hT is not None else nc.dram_tensor("hT", [D, TF], BF16, kind="ExternalInput").ap()
    for nm, shp in (("wk", [D, 256]), ("wq", [D, 256]), ("wv", [D, 512]), ("wr", [D, 512]), ("wa", [D, 32]),
                    ("waf", [32, 256]), ("wab", [32, 256]), ("gng", [128, 4])):
        d[nm] = nc.dram_tensor(nm + tag, shp, F32, kind="ExternalInput").ap()
    if shared is not None and "atmpl" in shared:
        for n in ("atmpl",) + GLA_CNAMES:
            d[n] = shared[n]
    else:
        d["atmpl"] = nc.dram_tensor("atmpl", [32, TF], F32, kind="ExternalInput").ap()
        for n in GLA_CNAMES:
            d[n] = nc.dram_tensor(n, [128, 128], F32, kind="ExternalInput").ap()
        if shared is not None:
            for n in ("atmpl",) + GLA_CNAMES:
                shared[n] = d[n]
    return d


GLA_CNAMES = ("tri_inc_f", "tri_exc_f", "tri_inc_b", "tri_exc_b", "mask_f", "mask_b",
              "colm0", "colm1", "rowm0", "rowm1")


def emit_gla(nc, c, psum, din, ogT, b_out):
    hT, wk, wq, wv, wr, wa, waf, wab, atmpl, gng = (din[k] for k in (
        "hT", "wk", "wq", "wv", "wr", "wa", "waf", "wab", "atmpl", "gng"))
    cnames = GLA_CNAMES
    cdram = {n: din[n] for n in cnames}
    oS = nc.dram_tensor(c.uid("oS"), [2, 512, TF], F32).ap()
    ones = sb_alloc(nc, "ones_bf", [128, 128], BF16)
    b_ones = Buf()
    c.op("pool", lambda e: e.memset(ones[:], 1.0), writes=[b_ones])
    onec = sb_alloc(nc, "onec", [128, 1], F32)
    epsc = sb_alloc(nc, "epsc", [128, 1], F32)
    b_cc = Buf()
    c.op("pool", lambda e: e.memset(onec[:], 1.0), writes=[b_cc])
    c.op("pool", lambda e: e.memset(epsc[:], EPS), writes=[b_cc])
    H = sb_alloc(nc, "H", [128, 8, TF], BF16)
    bH = [Buf() for _ in range(NT)]
    hsrc = hT.rearrange("(k p) t -> p k t", p=128)
    for g in range(0, NT, 4):
        n = min(4, NT - g)
        c.dma("sp", H[:, :, g * 128:(g + n) * 128], hsrc[:, :, g * 128:(g + n) * 128], writes=bH[g:g + n])
    Wt = {}
    for name, ap, ncol in (("k", wk, 256), ("q", wq, 256), ("v", wv, 512), ("r", wr, 512), ("a", wa, 32)):
        t = sb_alloc(nc, f"w_{name}", [128, 8, ncol], BF16)
        b = Buf()
        c.dma("pool", t[:], ap.rearrange("(k p) n -> p k n", p=128), writes=[b])
        Wt[name] = (t, b)
    waug = {}
    for name, ap in (("f", waf), ("b", wab)):
        t = sb_alloc(nc, f"waug_{name}", [32, 256], F32)
        b = Buf()
        c.dma("sp", t[:], ap, writes=[b])
        waug[name] = (t, b)
    gt = sb_alloc(nc, "gngt", [128, 4], F32)
    b_g = Buf()
    c.dma("sp", gt[:], gng, writes=[b_g])
    C = {}
    for n in cnames:
        t = sb_alloc(nc, f"c_{n}", [128, 128], F32)
        b = Buf()
        c.dma("sp", t[:], cdram[n], writes=[b])
        C[n] = (t, b)
    aX = {}
    for name in ("f", "b"):
        t = sb_alloc(nc, f"a_{name}", [32, TF], F32)
        bl = [Buf() for _ in range(NT)]
        c.dma("sp", t[:], atmpl, writes=bl)
        aX[name] = (t, bl)
    psA = RotSlice(psum, [7])
    ps = PsumHalves(psum, 7)
    wa_t, wa_b = Wt["a"]
    for t0 in range(0, TF, 512):
        n = min(512, TF - t0)
        tiles = list(range(t0 // 128, (t0 + n) // 128))
        for di, name in enumerate(("f", "b")):
            p, bp = psA.next()
            for k in range(8):
                c.op("pe", lambda e, p=p, k=k, di=di, t0=t0, n=n: e.matmul(
                    p[0:16, :n], lhsT=wa_t[:, k, di * 16:(di + 1) * 16], rhs=H[:, k, t0:t0 + n],
                    start=(k == 0), stop=(k == 7)), reads=[wa_b] + [bH[i] for i in tiles], writes=[bp])
            at, abl = aX[name]
            c.op("act", lambda e, p=p, at=at, t0=t0, n=n: e.copy(out=at[0:16, t0:t0 + n], in_=p[0:16, :n]),
                 reads=[bp], writes=[abl[i] for i in tiles])
    def rot(name, n, shape, dt):
        return Rot(nc, name, n, shape, dt)
    r_f32 = rot("rf", 24, [128, 128], F32)
    r_bf = rot("rb", 48, [128, 128], BF16)
    r_v = rot("rv", 8, [128, 256], BF16)
    r_o = rot("ro", 8, [128, 2, 128], F32)
    chains = []
    for head in range(2):
        for di, dname in enumerate(("f", "b")):
            S = sb_alloc(nc, f"S_{head}{dname}", [128, 256], F32)
            Sb = [sb_alloc(nc, f"Sb{i}_{head}{dname}", [128, 256], BF16) for i in range(3)]
            bS = Buf()
            bSb = [Buf(), Buf(), Buf()]
            c.op("pool", lambda e, S=S: e.memset(S[:], 0.0), writes=[bS])
            c.op("pool", lambda e, t=Sb[0]: e.memset(t[:], 0.0), writes=[bSb[0]])
            order = list(range(NT)) if di == 0 else [1, 0] + list(range(NT - 1, 1, -1))
            chains.append(dict(head=head, di=di, dname=dname, S=S, Sb=Sb, bS=bS, bSb=bSb, cur=0, order=order))
    osd = [oS[di].rearrange("(c p) t -> p c t", p=128) for di in range(2)]
    b_oS = [[[Buf() for _ in range(NT)] for _ in range(2)] for _ in range(2)]
    scale = float(GLA_DK) ** -0.5
    for step in range(NT):
        for ch in chains:
            head, di, dname = ch["head"], ch["di"], ch["dname"]
            ti = ch["order"][step]
            t0 = ti * 128
            tri_inc, b_ti = C[f"tri_inc_{dname}"]
            tri_exc, b_te = C[f"tri_exc_{dname}"]
            mask, b_mk = C[f"mask_{dname}"]
            wkt, bwk = Wt["k"]
            wqt, bwq = Wt["q"]
            wvt, bwv = Wt["v"]
            hb = bH[ti]
            p_kT, b_kT = ps.next()
            p_qT, b_qT = ps.next()
            p_ktm, b_ktm = ps.next()
            p_vtm, b_vtm = ps.next()
            for k in range(8):
                c.op("pe", lambda e, p=p_kT, k=k, head=head, t0=t0: e.matmul(
                    p[:, 0:128], lhsT=wkt[:, k, head * 128:(head + 1) * 128], rhs=H[:, k, t0:t0 + 128],
                    start=(k == 0), stop=(k == 7)), reads=[bwk, hb], writes=[b_kT])
            for k in range(8):
                c.op("pe", lambda e, p=p_qT, k=k, head=head, t0=t0: e.matmul(
                    p[:, 0:128], lhsT=wqt[:, k, head * 128:(head + 1) * 128], rhs=H[:, k, t0:t0 + 128],
                    start=(k == 0), stop=(k == 7)), reads=[bwq, hb], writes=[b_qT])
            for k in range(8):
                c.op("pe", lambda e, p=p_ktm, k=k, head=head, t0=t0: e.matmul(
                    p[:, 0:128], lhsT=H[:, k, t0:t0 + 128], rhs=wkt[:, k, head * 128:(head + 1) * 128],
                    start=(k == 0), stop=(k == 7)), reads=[bwk, hb], writes=[b_ktm])
            for k in range(8):
                c.op("pe", lambda e, p=p_vtm, k=k, head=head, t0=t0: e.matmul(
                    p[:, 0:256], lhsT=H[:, k, t0:t0 + 128], rhs=wvt[:, k, head * 256:(head + 1) * 256],
                    start=(k == 0), stop=(k == 7)), reads=[bwv, hb], writes=[b_vtm])
            at, abl = aX[dname]
            wg_t, wg_b = waug[dname]
            p_x, b_x = ps.next()
            c.op("pe", lambda e, p=p_x, at=at, wg_t=wg_t, head=head, t0=t0: e.matmul(
                p[:, 0:128], lhsT=at[0:32, t0:t0 + 128], rhs=wg_t[0:32, head * 128:(head + 1) * 128],
                start=True, stop=True), reads=[abl[ti], wg_b], writes=[b_x])
            ex, b_ex = r_f32.next()
            c.op("act", lambda e, ex=ex, p=p_x: e.activation(out=ex[:], in_=p[:, 0:128], func=AF.Exp, scale=-1.0),
                 reads=[b_x], writes=[b_ex])
            sp, b_sp = r_f32.next()
            c.op("act", lambda e, ex=ex, sp=sp: e.activation(out=sp[:], in_=ex[:], func=AF.Ln, bias=onec[:, 0:1],
                                                              scale=1.0), reads=[b_ex, b_cc], writes=[b_sp])
            p_bT, b_bT = ps.next()
            c.op("pe", lambda e, p=p_bT, sp=sp, tri_inc=tri_inc: e.matmul(p[:, 0:128], lhsT=sp[:], rhs=tri_inc[:],
                                                                           start=True, stop=True),
                 reads=[b_sp, b_ti], writes=[b_bT])
            p_de, b_de = ps.next()
            c.op("pe", lambda e, p=p_de, sp=sp, tri_exc=tri_exc: e.matmul(p[:, 0:128], lhsT=tri_exc[:], rhs=sp[:],
                                                                           start=True, stop=True),
                 reads=[b_sp, b_te], writes=[b_de])
            E1, b_E1 = r_f32.next()
            E2, b_E2 = r_f32.next()
            E3, b_E3 = r_f32.next()
            c.op("act", lambda e, E1=E1, p=p_bT: e.activation(out=E1[:], in_=p[:, 0:128], func=AF.Exp),
                 reads=[b_bT], writes=[b_E1])
            c.op("act", lambda e, E2=E2, p=p_bT: e.activation(out=E2[:], in_=p[:, 0:128], func=AF.Exp, scale=-1.0),
                 reads=[b_bT], writes=[b_E2])
            c.op("act", lambda e, E3=E3, p=p_de: e.activation(out=E3[:], in_=p[:, 0:128], func=AF.Exp),
                 reads=[b_de], writes=[b_E3])
            qd, b_qd = r_bf.next()
            ki, b_ki = r_bf.next()
            ke, b_ke = r_bf.next()
            c.op("dve", lambda e, qd=qd, p=p_qT, E1=E1: e.scalar_tensor_tensor(
                out=qd[:], in0=p[:, 0:128], scalar=scale, in1=E1[:], op0=ALU.mult, op1=ALU.mult),
                reads=[b_qT, b_E1], writes=[b_qd])
            c.op("dve", lambda e, ki=ki, p=p_kT, E2=E2: e.tensor_tensor(out=ki[:], in0=p[:, 0:128], in1=E2[:],
                                                                        op=ALU.mult),
                 reads=[b_kT, b_E2], writes=[b_ki])
            c.op("dve", lambda e, ke=ke, p=p_ktm, E3=E3: e.tensor_tensor(out=ke[:], in0=p[:, 0:128], in1=E3[:],
                                                                         op=ALU.mult),
                 reads=[b_ktm, b_E3], writes=[b_ke])
            vs, b_vs = r_v.next()
            c.op("act", lambda e, vs=vs, p=p_vtm: e.copy(out=vs[:], in_=p[:, 0:256]), reads=[b_vtm], writes=[b_vs])
            p_at, b_at = ps.next()
            c.op("pe", lambda e, p=p_at, ki=ki, qd=qd: e.matmul(p[:, 0:128], lhsT=ki[:], rhs=qd[:], start=True, stop=True),
                 reads=[b_ki, b_qd], writes=[b_at])
            am, b_am = r_bf.next()
            c.op("dve", lambda e, am=am, p=p_at, mask=mask: e.tensor_tensor(out=am[:], in0=p[:, 0:128], in1=mask[:],
                                                                            op=ALU.mult),
                 reads=[b_at, b_mk], writes=[b_am])
            corder = (0, 1) if di == 0 else (1, 0)
            S, Sb, bS, bSb = ch["S"], ch["Sb"], ch["bS"], ch["bSb"]
            cur = ch["cur"]
            kec, qdc = {}, {}
            for c2 in (0, 1):
                rm, b_rm = C[f"rowm{c2}"]
                cm, b_cm = C[f"colm{c2}"]
                t_ke, b_tke = r_bf.next()
                c.op("dve", lambda e, t=t_ke, ke=ke, rm=rm: e.tensor_tensor(out=t[:], in0=ke[:], in1=rm[:], op=ALU.mult),
                     reads=[b_ke, b_rm], writes=[b_tke])
                t_qd, b_tqd = r_bf.next()
                c.op("dve", lambda e, t=t_qd, qd=qd, cm=cm: e.tensor_tensor(out=t[:], in0=qd[:], in1=cm[:], op=ALU.mult),
                     reads=[b_qd, b_cm], writes=[b_tqd])
                kec[c2] = (t_ke, b_tke)
                qdc[c2] = (t_qd, b_tqd)
            st_bf = []
            for ci, c2 in enumerate(corder):
                st_bf.append((Sb[cur], bSb[cur]))
                p_dS, b_dS = ps.next()
                t_ke, b_tke = kec[c2]
                c.op("pe", lambda e, p=p_dS, t_ke=t_ke, vs=vs: e.matmul(
                    p[:, 0:256], lhsT=t_ke[:], rhs=vs[:], start=True, stop=True),
                    reads=[b_tke, b_vs], writes=[b_dS])
                lo = c2 * 64
                col = lo + 63 if di == 0 else lo
                c.op("dve", lambda e, S=S, p=p_dS, E1=E1, col=col: e.scalar_tensor_tensor(
                    out=S[:], in0=S[:], scalar=E1[:, col:col + 1], in1=p[:, 0:256], op0=ALU.mult, op1=ALU.add),
                    reads=[bS, b_E1, b_dS], writes=[bS])
                nxt = (cur + 1) % 3
                c.op("act", lambda e, t=Sb[nxt], S=S: e.copy(out=t[:], in_=S[:]), reads=[bS], writes=[bSb[nxt]])
                cur = nxt
            ch["cur"] = cur
            ot, b_ot = r_o.next()
            for dvc in range(2):
                p_o, b_o = ps.next()
                c.op("pe", lambda e, p=p_o, vs=vs, am=am, dvc=dvc: e.matmul(
                    p[:, 0:128], lhsT=vs[:, dvc * 128:(dvc + 1) * 128], rhs=am[:], start=True, stop=False),
                    reads=[b_vs, b_am], writes=[b_o])
                for ci, c2 in enumerate(corder):
                    sbt, sbb = st_bf[ci]
                    t_qd, b_tqd = qdc[c2]
                    c.op("pe", lambda e, p=p_o, sbt=sbt, t_qd=t_qd, dvc=dvc, ci=ci: e.matmul(
                        p[:, 0:128], lhsT=sbt[:, dvc * 128:(dvc + 1) * 128], rhs=t_qd[:],
                        start=False, stop=(ci == 1)), reads=[sbb, b_tqd], writes=[b_o])
                c.op("act", lambda e, ot=ot, p=p_o, dvc=dvc: e.copy(out=ot[:, dvc, :], in_=p[:, 0:128]),
                     reads=[b_o], writes=[b_ot])
            c.dma("st", osd[di][:, head * 2:head * 2 + 2, t0:t0 + 128], ot[:], reads=[b_ot],
                  writes=[b_oS[di][head][ti]])
    r_in = rot("rin", 6, [128, 2, 128], F32)
    r_og = rot("rog", 3, [128, 4, 128], BF16)
    r_sq = rot("rsq", 4, [128, 128], BF16)
    wrt, bwr = Wt["r"]
    ogd = ogT.rearrange("(c p) t -> p c t", p=128)
    for ti in range(NT):
        t0 = ti * 128
        og, b_og = r_og.next()
        for head in range(2):
            of, b_of = r_in.next()
            ob, b_ob = r_in.next()
            c.dma("sp", of[:], osd[0][:, head * 2:head * 2 + 2, t0:t0 + 128], reads=[b_oS[0][head][ti]], writes=[b_of])
            c.dma("sp", ob[:], osd[1][:, head * 2:head * 2 + 2, t0:t0 + 128], reads=[b_oS[1][head][ti]], writes=[b_ob])
            c.op("pool", lambda e, of=of, ob=ob: e.tensor_tensor(out=of[:], in0=of[:], in1=ob[:], op=ALU.add),
                 reads=[b_of, b_ob], writes=[b_of])
            p_ss, b_ss = ps.next()
            for dvc in range(2):
                sq, b_sq = r_sq.next()
                c.op("act", lambda e, sq=sq, of=of, dvc=dvc: e.activation(out=sq[:], in_=of[:, dvc, :], func=AF.Square),
                     reads=[b_of], writes=[b_sq])
                c.op("pe", lambda e, p=p_ss, sq=sq, dvc=dvc: e.matmul(p[:, 0:128], lhsT=ones[:], rhs=sq[:],
                                                                       start=(dvc == 0), stop=(dvc == 1)),
                     reads=[b_sq, b_ones], writes=[b_ss])
            ln, b_ln = r_f32.next()
            c.op("act", lambda e, ln=ln, p=p_ss: e.activation(out=ln[:], in_=p[:, 0:128], func=AF.Ln,
                                                               bias=epsc[:, 0:1], scale=1.0 / GLA_DV),
                 reads=[b_ss, b_cc], writes=[b_ln])
            rs, b_rs = r_f32.next()
            c.op("act", lambda e, ln=ln, rs=rs: e.activation(out=rs[:], in_=ln[:], func=AF.Exp, scale=-0.5),
                 reads=[b_ln], writes=[b_rs])
            for dvc in range(2):
                cidx = head * 2 + dvc
                p_r, b_r = ps.next()
                for k in range(8):
                    c.op("pe", lambda e, p=p_r, k=k, cidx=cidx, t0=t0: e.matmul(
                        p[:, 0:128], lhsT=wrt[:, k, cidx * 128:(cidx + 1) * 128], rhs=H[:, k, t0:t0 + 128],
                        start=(k == 0), stop=(k == 7)), reads=[bwr, bH[ti]], writes=[b_r])
                sr, b_sr = r_f32.next()
                c.op("act", lambda e, sr=sr, p=p_r: e.activation(out=sr[:], in_=p[:, 0:128], func=AF.Silu),
                     reads=[b_r], writes=[b_sr])
                tt, b_tt = r_f32.next()
                c.op("dve", lambda e, tt=tt, of=of, rs=rs, dvc=dvc, cidx=cidx: e.scalar_tensor_tensor(
                    out=tt[:], in0=of[:, dvc, :], scalar=gt[:, cidx:cidx + 1], in1=rs[:], op0=ALU.mult, op1=ALU.mult),
                    reads=[b_of, b_rs, b_g], writes=[b_tt])
                c.op("dve", lambda e, tt=tt, sr=sr, og=og, cidx=cidx: e.tensor_tensor(
                    out=og[:, cidx, :], in0=tt[:], in1=sr[:], op=ALU.mult), reads=[b_tt, b_sr], writes=[b_og])
        c.dma("sp", ogd[:, :, t0:t0 + 128], og[:], reads=[b_og], writes=[b_out])


def build_gla_prog():
    nc = bass.Bass("TRN2", target_bir_lowering=False)
    c = Ctx(nc)
    psum = make_psum(nc)
    din = gla_dram_inputs(nc)
    ogT = nc.dram_tensor("ogT", [512, TF], BF16, kind="ExternalOutput").ap()
    b_out = Buf()
    emit_gla(nc, c, psum, din, ogT, b_out)
    c.wait_all("sp", [b_out])
    c.emit()
    return nc


OFF_K, OFF_V, OFF_A, OFF_Q, OFF_R, OFF_H, OFF_G = 0, 512, 1536, 1568, 2080, 3104, 6176


def gla_inputs_for_core(inp, l, core, hT_full):
    b, hf = divmod(core, 2)
    w = inp["w_in"][l]
    heads = (2 * hf, 2 * hf + 1)
    def cols(off, width):
        return np.ascontiguousarray(np.concatenate([w[:, off + h * width: off + (h + 1) * width] for h in heads], axis=1))
    m = {"wk": cols(OFF_K, 128), "wq": cols(OFF_Q, 128), "wv": cols(OFF_V, 256),
         "wr": cols(OFF_R, 256), "wa": np.ascontiguousarray(w[:, OFF_A:OFF_A + 32])}
    for nm, wa_, ba_ in (("waf", inp["gla_wa_f"][l], inp["gla_ba_f"][l]), ("wab", inp["gla_wa_b"][l], inp["gla_ba_b"][l])):
        aug = np.concatenate([wa_, ba_[None, :], np.zeros((15, wa_.shape[1]), np.float32)], axis=0)
        m[nm] = np.ascontiguousarray(np.concatenate([aug[:, h * 128:(h + 1) * 128] for h in heads], axis=1))
    if hT_full is not None:
        m["hT"] = hT_full
    g = inp["gla_norm_g"][l][hf * 512:(hf + 1) * 512]
    m["gng"] = np.ascontiguousarray(g.reshape(4, 128).T)
    tm = np.zeros((32, TF), np.float32)
    tm[16] = 1.0
    m["atmpl"] = tm
    m.update(gla_consts())
    return m


NK1 = 65
HC = 512


def fft_consts(NC):
    N = 128 * NC
    n1 = np.arange(128)[:, None, None]
    n2 = np.arange(NC)[None, :, None]
    k1 = np.arange(NK1)[None, None, :]
    ang = 2 * np.pi * ((NC * n1 + n2) * k1 % N) / N
    DAr, DAi = np.cos(ang), -np.sin(ang)
    KB = max(32, NC)
    a = np.arange(NC)
    angb = 2 * np.pi * (np.outer(a, a) % NC) / NC
    WBr = np.zeros((KB, NC)); WBi = np.zeros((KB, NC))
    WBr[:NC], WBi[:NC] = np.cos(angb), -np.sin(angb)
    wgt = np.full(NK1, 2.0); wgt[0] = 1.0; wgt[64] = 1.0
    kk = np.arange(NK1)[:, None, None]
    m2 = np.arange(NC)[None, :, None]
    m1 = np.arange(64)[None, None, :]
    th = 2 * np.pi * (kk * (NC * m1 + m2) % N) / N
    Er = np.zeros((96, NC, 64)); Ei = np.zeros((96, NC, 64))
    Er[:NK1] = wgt[:, None, None] * np.cos(th) / N
    Ei[:NK1] = -wgt[:, None, None] * np.sin(th) / N
    bf = ml_dtypes.bfloat16
    return {"DAr": DAr.astype(np.float32).astype(bf), "DAi": DAi.astype(np.float32).astype(bf),
            "WBr": WBr.astype(np.float32).astype(bf), "WBi": WBi.astype(np.float32).astype(bf),
            "WBin": (-WBi).astype(np.float32).astype(bf),
            "Er": Er.astype(np.float32).astype(bf), "Ei": Ei.astype(np.float32).astype(bf), "KB": KB}


_CONST_DRAM = {}


class FFTConv:
    def __init__(self, nc, c, name, NC, rows_A, psum, C=HC, share=None, uniq=""):
        self.nc, self.c, self.name, self.NC, self.C = nc, c, name, NC, C
        self.uniq = uniq
        self.rows_A = rows_A
        self.ps = psum
        cs = fft_consts(NC)
        self.KB = cs["KB"]
        self.host = {f"{name}_{k}": v for k, v in cs.items() if k != "KB"}
        self.host[f"{name}_DAr"] = np.ascontiguousarray(cs["DAr"][:rows_A])
        self.host[f"{name}_DAi"] = np.ascontiguousarray(cs["DAi"][:rows_A])
        shapes = {"DAr": [rows_A, NC, NK1], "DAi": [rows_A, NC, NK1], "WBr": [self.KB, NC], "WBi": [self.KB, NC],
                  "WBin": [self.KB, NC], "Er": [96, NC, 64], "Ei": [96, NC, 64]}
        self.T = {}
        for k, shp in shapes.items():
            key = (id(nc), f"{name}_{k}")
            if key not in _CONST_DRAM:
                _CONST_DRAM[key] = nc.dram_tensor(f"{name}_{k}", shp, BF16, kind="ExternalInput").ap()
            d = _CONST_DRAM[key]
            t = sb_alloc(nc, f"t_{uniq}{name}_{k}", shp, BF16)
            b = Buf()
            c.dma("sp", t[:], d, writes=[b])
            self.T[k] = (t, b)
        if share is not None:
            self.ev, self.ld, self.lda, self.xf, self.yb, self.tf = (
                share.ev, share.ld, share.lda, share.xf, share.yb, share.tf)
        else:
            self.ev = Rot(nc, f"{name}_ev", 4, [128, C], BF16)
            self.ld = Rot(nc, f"{name}_ld", 4, [96, C], BF16)
            self.lda = Rot(nc, f"{name}_lda", 4, [64, C], BF16)
            self.xf = Rot(nc, f"{name}_xf", 2, [64, C], F32)
            self.yb = Rot(nc, f"{name}_yb", 4, [64, C], BF16)
            self.tf = Rot(nc, f"{name}_tf", 4, [64, C], F32)
        for rot_ in (self.ld, self.lda, self.yb):
            for t, b in rot_.tiles:
                c.op("pool", lambda e, t=t: e.memset(t[:], 0.0), writes=[b])
        self.flip = 0

    def scratch(self, tag):
        NC, C = self.NC, self.C
        As = self.nc.dram_tensor(f"{self.uniq}{self.name}_As_{tag}", [2, NK1, NC, C], BF16).ap()
        Cs = self.nc.dram_tensor(f"{self.uniq}{self.name}_Cs_{tag}", [2, NC, NK1, C], BF16).ap()
        return dict(As=As, Cs=Cs, bA=Buf(), bC=Buf())

    def _evac(self, dst, src, reads, writes):
        eng = "act"
        self.flip += 1
        if eng == "act":
            self.c.op("act", lambda e: e.copy(out=dst, in_=src), reads=reads, writes=writes)
        else:
            self.c.op("dve", lambda e: e.tensor_copy(out=dst, in_=src), reads=reads, writes=writes)

    def stage_a(self, sc, n2, z_ap, z_buf):
        c = self.c
        for ri, key in enumerate(("DAr", "DAi")):
            tb, bt = self.T[key]
            p, bp = self.ps.next()
            c.op("pe", lambda e, p=p, tb=tb, n2=n2: e.matmul(p[0:NK1, 0:self.C], lhsT=tb[:, n2, :], rhs=z_ap,
                                                            start=True, stop=True), reads=[bt, z_buf], writes=[bp])
            ev, bev = self.ev.next()
            self._evac(ev[0:NK1, :], p[0:NK1, 0:self.C], [bp], [bev])
            c.dma("st", sc["As"][ri, :, n2, :], ev[0:NK1, :], reads=[bev], writes=[sc["bA"]])

    def stage_b(self, sc, k1, h_re, h_im, h_buf, spectrum_out=None, scale_tab=None):
        c, NC, C, KB = self.c, self.NC, self.C, self.KB
        ar, bar = self.lda.next()
        ai, bai = self.lda.next()
        c.dma("sp", ar[0:NC, :], sc["As"][0, k1, :, :], reads=[sc["bA"]], writes=[bar])
        c.dma("sp", ai[0:NC, :], sc["As"][1, k1, :, :], reads=[sc["bA"]], writes=[bai])
        wr, bwr = self.T["WBr"]
        wi, bwi = self.T["WBi"]
        wn, bwn = self.T["WBin"]
        pxr, bxr = self.ps.next()
        pxi, bxi = self.ps.next()
        c.op("pe", lambda e: e.matmul(pxr[0:NC, 0:C], lhsT=wr[:], rhs=ar[0:KB, :], start=True, stop=False),
             reads=[bwr, bar], writes=[bxr])
        c.op("pe", lambda e: e.matmul(pxr[0:NC, 0:C], lhsT=wn[:], rhs=ai[0:KB, :], start=False, stop=True),
             reads=[bwn, bai], writes=[bxr])
        c.op("pe", lambda e: e.matmul(pxi[0:NC, 0:C], lhsT=wi[:], rhs=ar[0:KB, :], start=True, stop=False),
             reads=[bwi, bar], writes=[bxi])
        c.op("pe", lambda e: e.matmul(pxi[0:NC, 0:C], lhsT=wr[:], rhs=ai[0:KB, :], start=False, stop=True),
             reads=[bwr, bai], writes=[bxi])
        if spectrum_out is not None:
            tab, btab = scale_tab
            for ri, (p, bp) in enumerate(((pxr, bxr), (pxi, bxi))):
                t, bt = self.tf.next()
                c.op("dve", lambda e, t=t, p=p: e.tensor_tensor(out=t[0:NC, :], in0=p[0:NC, 0:C], in1=tab[0:NC, :],
                                                                 op=ALU.mult), reads=[bp, btab], writes=[bt])
                c.dma("sp", spectrum_out[ri, k1, :, :], t[0:NC, :], reads=[bt], writes=[sc["bH"]])
            return
        hbr, hbi = (h_buf, h_buf) if h_buf is not None else self._hb
        xr, bxr2 = self.xf.next()
        xi, bxi2 = self.xf.next()
        c.op("act", lambda e: e.copy(out=xr[0:NC, :], in_=pxr[0:NC, 0:C]), reads=[bxr], writes=[bxr2])
        c.op("act", lambda e: e.copy(out=xi[0:NC, :], in_=pxi[0:NC, 0:C]), reads=[bxi], writes=[bxi2])
        t1, b1 = self.tf.next()
        t2, b2 = self.tf.next()
        yr, byr = self.yb.next()
        yi, byi = self.yb.next()
        c.op("dve", lambda e: e.tensor_tensor(out=t1[0:NC, :], in0=xr[0:NC, :], in1=h_re, op=ALU.mult),
             reads=[bxr2, hbr], writes=[b1])
        c.op("dve", lambda e: e.tensor_tensor(out=t2[0:NC, :], in0=xi[0:NC, :], in1=h_im, op=ALU.mult),
             reads=[bxi2, hbi], writes=[b2])
        c.op("dve", lambda e: e.tensor_tensor(out=yr[0:NC, :], in0=t1[0:NC, :], in1=t2[0:NC, :], op=ALU.subtract),
             reads=[b1, b2], writes=[byr])
        t3, b3 = self.tf.next()
        t4, b4 = self.tf.next()
        c.op("pool", lambda e: e.tensor_tensor(out=t3[0:NC, :], in0=xr[0:NC, :], in1=h_im, op=ALU.mult),
             reads=[bxr2, hbi], writes=[b3])
        c.op("dve", lambda e: e.tensor_tensor(out=t4[0:NC, :], in0=xi[0:NC, :], in1=h_re, op=ALU.mult),
             reads=[bxi2, hbr], writes=[b4])
        c.op("dve", lambda e: e.tensor_tensor(out=yi[0:NC, :], in0=t3[0:NC, :], in1=t4[0:NC, :], op=ALU.add),
             reads=[b3, b4], writes=[byi])
        if KB > NC:
            pass
        pcr, bcr = self.ps.next()
        pci, bci = self.ps.next()
        c.op("pe", lambda e: e.matmul(pcr[0:NC, 0:C], lhsT=wr[:], rhs=yr[0:KB, :], start=True, stop=False),
             reads=[bwr, byr], writes=[bcr])
        c.op("pe", lambda e: e.matmul(pcr[0:NC, 0:C], lhsT=wi[:], rhs=yi[0:KB, :], start=False, stop=True),
             reads=[bwi, byi], writes=[bcr])
        c.op("pe", lambda e: e.matmul(pci[0:NC, 0:C], lhsT=wr[:], rhs=yi[0:KB, :], start=True, stop=False),
             reads=[bwr, byi], writes=[bci])
        c.op("pe", lambda e: e.matmul(pci[0:NC, 0:C], lhsT=wn[:], rhs=yr[0:KB, :], start=False, stop=True),
             reads=[bwn, byr], writes=[bci])
        for ri, (p, bp) in enumerate(((pcr, bcr), (pci, bci))):
            ev, bev = self.ev.next()
            self._evac(ev[0:NC, :], p[0:NC, 0:C], [bp], [bev])
            c.dma("st", sc["Cs"][ri, :, k1, :], ev[0:NC, :], reads=[bev], writes=[sc["bC"]])

    def stage_b_h(self, sc, k1, hr, bhr, hi, bhi):
        NC = self.NC
        self._hb = (bhr, bhi)
        self.stage_b(sc, k1, hr[0:NC, :], hi[0:NC, :], None)

    def stage_ainv(self, sc, m2):
        c = self.c
        cr, bcr = self.ld.next()
        ci, bci = self.ld.next()
        c.dma("sp", cr[0:NK1, :], sc["Cs"][0, m2, :, :], reads=[sc["bC"]], writes=[bcr])
        c.dma("sp", ci[0:NK1, :], sc["Cs"][1, m2, :, :], reads=[sc["bC"]], writes=[bci])
        er, ber = self.T["Er"]
        ei, bei = self.T["Ei"]
        p, bp = self.ps.next()
        c.op("pe", lambda e: e.matmul(p[0:64, 0:self.C], lhsT=er[:, m2, :], rhs=cr[:], start=True, stop=False),
             reads=[ber, bcr], writes=[bp])
        c.op("pe", lambda e: e.matmul(p[0:64, 0:self.C], lhsT=ei[:, m2, :], rhs=ci[:], start=False, stop=True),
             reads=[bei, bci], writes=[bp])
        return p, bp


def filt_consts(L, NC):
    N = 2 * L
    bands = 16
    p = np.arange(N)
    d = np.where(p < L, p, N - p).astype(np.float64)
    valid = (p != L)
    t = d / (L - 1)
    w = (2.0 * np.pi / L) * d
    f = np.linspace(1e-4, bands - 1, bands)
    feat = np.concatenate([t[:, None], np.cos(f[None, :] * w[:, None]), -np.sin(f[None, :] * w[:, None])], axis=1)
    feat = feat * valid[:, None]
    featT = np.zeros((64, N), np.float32)
    featT[:33] = feat.T
    tpos = t.reshape(128, NC).astype(np.float32)
    pmask = valid.reshape(128, NC).astype(np.float32)
    return featT, tpos, pmask


def filt_dram_inputs(nc, tg, L, NC, shared=None, wtag="", pos=None):
    N = 2 * L
    C = HC
    d = {}
    for nm, shp in (("featT", [64, N]), ("tpos", [128, NC]), ("pmask", [128, NC])):
        key = nm + tg
        if pos is not None and key in pos:
            d[nm] = pos[key]
        else:
            d[nm] = nc.dram_tensor(key, shp, F32, kind="ExternalInput").ap()
            if pos is not None:
                pos[key] = d[nm]
    if shared is None:
        d["w1"] = nc.dram_tensor("w1" + wtag, [64, 64], F32, kind="ExternalInput").ap()
        d["w2p"] = nc.dram_tensor("w2p" + wtag, [2, 64, 128], F32, kind="ExternalInput").ap()
        d["pr1"] = nc.dram_tensor("pr1" + wtag, [64, 2], F32, kind="ExternalInput").ap()
        d["pr2"] = nc.dram_tensor("pr2" + wtag, [2, 128, 3], F32, kind="ExternalInput").ap()
        d["w3s"] = nc.dram_tensor("w3s" + wtag, [2, 128, C], F32, kind="ExternalInput").ap()
        d["dlt"] = nc.dram_tensor("dlt" + wtag, [128, C], F32, kind="ExternalInput").ap()
    else:
        for k in ("w1", "w2p", "pr1", "pr2", "w3s", "dlt"):
            d[k] = shared[k]
    return d


def emit_filter(nc, c, psum, fft, tg, L, NC, din, Hout, bH):
    N = 2 * L
    C = HC
    BS = min(512, L)
    NBLK = N // BS
    featT, tpos, pmask, w1, w2p, pr1, pr2, w3s, dlt = (din[k] for k in (
        "featT", "tpos", "pmask", "w1", "w2p", "pr1", "pr2", "w3s", "dlt"))
    def load(name, ap, shape, dt=F32):
        t = sb_alloc(nc, name + tg, shape, dt)
        b = Buf()
        c.dma("sp", t[:], ap, writes=[b])
        return t, b
    ft, bft = load("featT_t", featT, [64, N])
    tp, btp = load("tpos_t", tpos, [128, NC])
    pm, bpm = load("pmask_t", pmask, [128, NC])
    w1t, bw1 = load("w1_t", w1, [64, 64])
    w2t = [load(f"w2_{d}", w2p[d], [64, 128]) for d in range(2)]
    p1t, bp1 = load("pr1_t", pr1, [64, 2])
    p2t = [load(f"pr2_{d}", pr2[d], [128, 3]) for d in range(2)]
    w3t = [load(f"w3_{o}", w3s[o], [128, C]) for o in range(2)]
    dl, bdl = load("dlt_t", dlt, [128, C])
    cst = sb_alloc(nc, "cst" + tg, [128, 2], F32)
    bcst = Buf()
    c.op("pool", lambda e: e.memset(cst[:, 0:1], -np.pi), writes=[bcst])
    c.op("pool", lambda e: e.memset(cst[:, 1:2], EPS), writes=[bcst])
    onesf = sb_alloc(nc, "onesf" + tg, [128, 128], F32)
    bof = Buf()
    c.op("pool", lambda e: e.memset(onesf[:], 1.0), writes=[bof])
    ntp = sb_alloc(nc, "ntp" + tg, [128, NC], F32)
    bntp = Buf()
    c.op("dve", lambda e: e.tensor_scalar(out=ntp[:], in0=tp[:], scalar1=-1.0, scalar2=None, op0=ALU.mult),
         reads=[btp], writes=[bntp])
    SH = 33.0 * np.pi
    TWO_PI = 2.0 * np.pi
    tmp = Rot(nc, "ftmp" + tg, 6, [128, 512], F32)
    h1 = sb_alloc(nc, "h1" + tg, [64, N], F32)
    bh1 = [Buf() for _ in range(NBLK)]
    S2 = sb_alloc(nc, "S2" + tg, [128, N], F32)
    bS2 = [Buf() for _ in range(NBLK)]

    itmp = Rot(nc, "fitmp" + tg, 2, [128, 512], mybir.dt.int32)

    def sin_layer(p, bp, rows, bcol, fcol, bpar, out_ap, out_buf, mcol=None):
        t, bt = tmp.next()
        it, bit = itmp.next()
        kf, bkf = tmp.next()
        c.op("dve", lambda e: e.tensor_scalar(out=t[0:rows, 0:BS], in0=p[0:rows, 0:BS], scalar1=bcol, scalar2=fcol,
                                              op0=ALU.add, op1=ALU.mult), reads=[bp, bpar], writes=[bt])
        c.op("dve", lambda e: e.tensor_scalar(out=t[0:rows, 0:BS], in0=t[0:rows, 0:BS], scalar1=1.0 / TWO_PI, scalar2=None,
                                              op0=ALU.mult), reads=[bt], writes=[bt])
        c.op("dve", lambda e: e.tensor_copy(out=it[0:rows, 0:BS], in_=t[0:rows, 0:BS]), reads=[bt], writes=[bit])
        c.op("dve", lambda e: e.tensor_copy(out=kf[0:rows, 0:BS], in_=it[0:rows, 0:BS]), reads=[bit], writes=[bkf])
        c.op("dve", lambda e: e.tensor_tensor(out=t[0:rows, 0:BS], in0=t[0:rows, 0:BS], in1=kf[0:rows, 0:BS], op=ALU.subtract),
             reads=[bt, bkf], writes=[bt])
        if mcol is None:
            c.op("act", lambda e: e.activation(out=out_ap, in_=t[0:rows, 0:BS], func=AF.Sin, scale=TWO_PI),
                 reads=[bt], writes=[out_buf])
        else:
            c.op("act", lambda e: e.activation(out=t[0:rows, 0:BS], in_=t[0:rows, 0:BS], func=AF.Sin, scale=TWO_PI),
                 reads=[bt], writes=[bt])
            c.op("dve", lambda e: e.tensor_scalar(out=out_ap, in0=t[0:rows, 0:BS], scalar1=mcol, scalar2=None, op0=ALU.mult),
                 reads=[bt, bpar], writes=[out_buf])

    for blk in range(NBLK):
        sl = slice(blk * BS, (blk + 1) * BS)
        p, bp = psum.next()
        c.op("pe", lambda e, p=p, sl=sl: e.matmul(p[0:64, 0:BS], lhsT=w1t[:], rhs=ft[:, sl], start=True, stop=True),
             reads=[bw1, bft], writes=[bp])
        sin_layer(p, bp, 64, p1t[:, 0:1], p1t[:, 1:2], bp1, h1[:, sl], bh1[blk])
        d = 0 if blk * BS < L else 1
        w2d, bw2 = w2t[d]
        p2d, bp2 = p2t[d]
        p, bp = psum.next()
        c.op("pe", lambda e, p=p, sl=sl, w2d=w2d: e.matmul(p[:, 0:BS], lhsT=w2d[:], rhs=h1[:, sl], start=True, stop=True),
             reads=[bw2, bh1[blk]], writes=[bp])
        sin_layer(p, bp, 128, p2d[:, 0:1], p2d[:, 1:2], bp2, S2[:, sl], bS2[blk], mcol=p2d[:, 2:3])
    S2v = S2[:].rearrange("p (a b) -> p b a", b=NC)
    for o in range(2):
        sc = fft.scratch(f"f{o}")
        sc["bH"] = bH
        w3o, bw3 = w3t[o]
        acc = sb_alloc(nc, f"acc{o}{tg}", [128, C], F32)
        bacc = Buf()
        c.op("pool", lambda e, acc=acc: e.memset(acc[:], 0.0), writes=[bacc])
        for n2 in range(NC):
            p, bp = psum.next()
            c.op("pe", lambda e, p=p, n2=n2, w3o=w3o: e.matmul(p[:, 0:C], lhsT=S2v[:, n2, :], rhs=w3o[:], start=True, stop=True),
                 reads=bS2 + [bw3], writes=[bp])
            win, bwin = tmp.next()
            c.op("act", lambda e, win=win, n2=n2: e.activation(out=win[:, 0:C], in_=dl[:], func=AF.Exp,
                                                               scale=ntp[:, n2:n2 + 1]),
                 reads=[bdl, bntp], writes=[bwin])
            tap, btap = tmp.next()
            c.op("dve", lambda e, tap=tap, p=p, win=win: e.tensor_tensor(out=tap[:, 0:C], in0=p[:, 0:C], in1=win[:, 0:C],
                                                                        op=ALU.mult), reads=[bp, bwin], writes=[btap])
            c.op("dve", lambda e, tap=tap, n2=n2: e.tensor_scalar(out=tap[:, 0:C], in0=tap[:, 0:C],
                                                                  scalar1=pm[:, n2:n2 + 1], scalar2=None, op0=ALU.mult),
                 reads=[btap, bpm], writes=[btap])
            ab, bab = tmp.next()
            c.op("act", lambda e, ab=ab, tap=tap: e.activation(out=ab[:, 0:C], in_=tap[:, 0:C], func=AF.Abs),
                 reads=[btap], writes=[bab])
            c.op("pool", lambda e, ab=ab, acc=acc: e.tensor_tensor(out=acc[:], in0=acc[:], in1=ab[:, 0:C], op=ALU.add),
                 reads=[bab, bacc], writes=[bacc])
            zb, bzb = fft.ev.next()
            c.op("act", lambda e, zb=zb, tap=tap: e.copy(out=zb[:], in_=tap[:, 0:C]), reads=[btap], writes=[bzb])
            fft.stage_a(sc, n2, zb[:], bzb)
        p, bp = psum.next()
        c.op("pe", lambda e, p=p, acc=acc: e.matmul(p[:, 0:C], lhsT=onesf[:], rhs=acc[:], start=True, stop=True),
             reads=[bof, bacc], writes=[bp])
        rn = sb_alloc(nc, f"rn{o}{tg}", [128, C], F32)
        brn = Buf()
        c.op("dve", lambda e, rn=rn, p=p: e.tensor_scalar(out=rn[:], in0=p[:, 0:C], scalar1=EPS, scalar2=None, op0=ALU.add),
             reads=[bp], writes=[brn])
        c.op("dve", lambda e, rn=rn: e.reciprocal(out=rn[:], in_=rn[:]), reads=[brn], writes=[brn])
        for k1 in range(NK1):
            fft.stage_b(sc, k1, None, None, None, spectrum_out=Hout[o], scale_tab=(rn, brn))


def build_filt_prog(L, NC):
    nc = bass.Bass("TRN2", target_bir_lowering=False)
    Hout = nc.dram_tensor("Hout", [2, 2, NK1, NC, HC], F32, kind="ExternalOutput").ap()
    c = Ctx(nc)
    psum = Rot(nc, "psf", 8, [128, 512], F32, psum=True)
    for t, b in psum.tiles:
        b.excl = True
    fft = FFTConv(nc, c, "ff", NC, 128, psum)
    din = filt_dram_inputs(nc, "", L, NC)
    bH = Buf()
    emit_filter(nc, c, psum, fft, "", L, NC, din, Hout, bH)
    c.wait_all("sp", [bH])
    c.emit()
    return nc, fft.host


def filt_inputs(inp, l, hf, L, NC, fft_host):
    ch = slice(hf * HC, (hf + 1) * HC)
    featT, tpos, pmask = filt_consts(L, NC)
    w1 = np.zeros((64, 64), np.float32)
    w1[:33] = inp["hy_f_w1"][l]
    w2 = inp["hy_f_w2"][l]
    z = np.zeros_like(w2)
    w2p = np.stack([np.concatenate([w2, z], 1), np.concatenate([z, w2], 1)])
    pr1 = np.stack([inp["hy_f_b1"][l], inp["hy_f_freq1"][l]], axis=1)
    b2 = np.tile(inp["hy_f_b2"][l], 2)
    f2 = np.tile(inp["hy_f_freq2"][l], 2)
    hm = [np.concatenate([np.ones(64), np.zeros(64)]), np.concatenate([np.zeros(64), np.ones(64)])]
    pr2 = np.stack([np.stack([b2, f2, hm[d]], axis=1) for d in range(2)]).astype(np.float32)
    w3 = inp["hy_f_w3"][l].reshape(64, 2, 2, 1024)
    w3s = np.stack([np.concatenate([w3[:, 0, o, ch], w3[:, 1, o, ch]], axis=0) for o in range(2)])
    mx, mn = np.log(1e-2) / 0.3, np.log(1e-2) / 1.5
    deltas = np.abs(np.linspace(mn, mx, 1024)).astype(np.float32)[ch]
    m = {"featT": featT, "tpos": tpos, "pmask": pmask, "w1": w1, "w2p": np.ascontiguousarray(w2p.astype(np.float32)),
         "pr1": np.ascontiguousarray(pr1.astype(np.float32)), "pr2": np.ascontiguousarray(pr2),
         "w3s": np.ascontiguousarray(w3s.astype(np.float32)),
         "dlt": np.ascontiguousarray(np.broadcast_to(deltas, (128, HC)))}
    m.update(fft_host)
    return m


TFP = TF + 2
CTX0 = 1
LAT0 = CTX + 2


def hy_dram_inputs(nc, with_ctx, hTp=None, tag="", pos=None):
    C = HC
    d = {}
    d["hTp"] = hTp if hTp is not None else nc.dram_tensor("hTp", [D, TFP], BF16, kind="ExternalInput").ap()
    d["wh"] = nc.dram_tensor("wh" + tag, [D, 3 * C], F32, kind="ExternalInput").ap()
    d["cw"] = nc.dram_tensor("cw" + tag, [64, 3, 3 * C], F32, kind="ExternalInput").ap()
    d["cb"] = nc.dram_tensor("cb" + tag, [64, 3 * C], F32, kind="ExternalInput").ap()
    d["hb"] = nc.dram_tensor("hb" + tag, [64, 2, C], F32, kind="ExternalInput").ap()
    d["fL"] = filt_dram_inputs(nc, "L", SEQ, 64, wtag=tag, pos=pos)
    if with_ctx:
        d["fC"] = filt_dram_inputs(nc, "C", CTX, 4, shared=d["fL"], wtag=tag, pos=pos)
    return d


def emit_hy(nc, c, psum, din, zlat, zctx, b_out, with_ctx):
    C = HC
    hT, wh, cw, cb, hb = din["hTp"], din["wh"], din["cw"], din["cb"], din["hb"]
    u = c.uid("hy")
    Hlat = nc.dram_tensor(u + "Hlat", [2, 2, NK1, 64, C], F32).ap()
    Hctx = nc.dram_tensor(u + "Hctx", [2, 2, NK1, 4, C], F32).ap() if with_ctx else None
    host = {}
    bHl, bHc = Buf(), Buf()
    dl = din["fL"]
    with phase_stack():
        fft = FFTConv(nc, c, "fL", 64, 128, psum, uniq=u)
        host.update(fft.host)
        emit_filter(nc, c, psum, fft, "L", SEQ, 64, dl, Hlat, bHl)
    c.barrier()
    if with_ctx:
        dc = din["fC"]
        with phase_stack():
            fft = FFTConv(nc, c, "fC", 4, 128, psum, uniq=u)
            host.update(fft.host)
            emit_filter(nc, c, psum, fft, "C", CTX, 4, dc, Hctx, bHc)
        c.barrier()
    H = sb_alloc(nc, "H", [128, 8, TFP], BF16)
    bHt = Buf()
    hsrc = hT.rearrange("(k p) t -> p k t", p=128)
    for k in range(8):
        c.dma("sp", H[:, k, :], hsrc[:, k, :], writes=[bHt])
    cwt = sb_alloc(nc, "cwt", [64, 3, C], F32)
    cbt = sb_alloc(nc, "cbt", [64, C], F32)
    hbt = sb_alloc(nc, "hbt", [64, 2, C], F32)
    bct = Buf()
    c.dma("sp", hbt[:], hb, writes=[bct])
    Wf = sb_alloc(nc, "Wf", [128, 8, C], BF16)
    bWf = Buf()
    ur = Rot(nc, "ur", 5, [64, C], F32)
    cr = Rot(nc, "cr", 5, [64, C], F32)
    zb = Rot(nc, "zbf", 3, [64, C], BF16)
    ob = Rot(nc, "obf", 2, [64, C], BF16)
    hl = Rot(nc, "hl", 4, [64, C], F32)
    zs = Rot(nc, "zs", 2, [64, C], F32)

    def load_family(fam):
        c.dma("pool", Wf[:], wh[:, fam * C:(fam + 1) * C].rearrange("(k p) n -> p k n", p=128), writes=[bWf])
        c.dma("sp", cwt[:], cw[:, :, fam * C:(fam + 1) * C], writes=[bct])
        c.dma("sp", cbt[:], cb[:, fam * C:(fam + 1) * C], writes=[bct])

    def project(tok0, stride):
        p, bp = psum.next()
        for k in range(8):
            c.op("pe", lambda e, p=p, k=k: e.matmul(p[0:64, 0:C], lhsT=H[:, k, tok0:tok0 + 63 * stride + 1:stride],
                                                    rhs=Wf[:, k, :], start=(k == 0), stop=(k == 7)),
                 reads=[bHt, bWf], writes=[bp])
        u, bu = ur.next()
        c.op("act", lambda e, u=u, p=p: e.copy(out=u[:], in_=p[0:64, 0:C]), reads=[bp], writes=[bu])
        return u, bu

    def conv3(fam, um, u0, up):
        a, ba = cr.next()
        c.op("dve", lambda e: e.tensor_tensor(out=a[:], in0=u0[0][:], in1=cwt[:, 1, :], op=ALU.mult),
             reads=[u0[1], bct], writes=[ba])
        c.op("dve", lambda e: e.tensor_tensor(out=a[:], in0=a[:], in1=cbt[:], op=ALU.add),
             reads=[ba, bct], writes=[ba])
        t2 = None
        if up is not None:
            t2, bt2 = cr.next()
            c.op("pool", lambda e: e.tensor_tensor(out=t2[:], in0=up[0][:], in1=cwt[:, 2, :], op=ALU.mult),
                 reads=[up[1], bct], writes=[bt2])
        if um is not None:
            t, bt = cr.next()
            c.op("dve", lambda e: e.tensor_tensor(out=t[:], in0=um[0][:], in1=cwt[:, 0, :], op=ALU.mult),
                 reads=[um[1], bct], writes=[bt])
            c.op("dve", lambda e: e.tensor_tensor(out=a[:], in0=a[:], in1=t[:], op=ALU.add),
                 reads=[ba, bt], writes=[ba])
        if t2 is not None:
            c.op("dve", lambda e: e.tensor_tensor(out=a[:], in0=a[:], in1=t2[:], op=ALU.add),
                 reads=[ba, bt2], writes=[ba])
        return a, ba

    class ColStream:
        def __init__(s2, fam, NC, base, plain_seq):
            s2.fam, s2.NC, s2.base, s2.plain = fam, NC, base, plain_seq
            s2.cache = {}

        def u(s2, n2):
            if n2 not in s2.cache:
                s2.cache[n2] = project(s2.base + n2, s2.NC)
                for k in [k for k in s2.cache if k < n2 - 2]:
                    del s2.cache[k]
            return s2.cache[n2]

        def col(s2, n2):
            if s2.plain:
                um = project(s2.base + n2 - 1, s2.NC)
                u0 = project(s2.base + n2, s2.NC)
                up = project(s2.base + n2 + 1, s2.NC)
                return conv3(s2.fam, um, u0, up)
            um = s2.u(n2 - 1) if n2 > 0 else None
            u0 = s2.u(n2)
            up = s2.u(n2 + 1) if n2 < s2.NC - 1 else None
            return conv3(s2.fam, um, u0, up)

    shared_fft = [None]

    def run_conv(tag, NC, base, plain, Hs, bHs, zout):
        fftc = FFTConv(nc, c, "d" + tag, NC, 64, psum, share=shared_fft[0], uniq=u)
        if shared_fft[0] is None:
            shared_fft[0] = fftc
        host.update(fftc.host)
        zst = [nc.dram_tensor(f"{u}zst{tag}{o}", [NC, 64, C], F32).ap() for o in range(2)]
        bzst = [Buf(), Buf()]
        sc = [fftc.scratch(f"c{o}") for o in range(2)]
        load_family(0)
        st = ColStream(0, NC, base, plain)
        for n2 in range(NC):
            a, ba = st.col(n2)
            c.dma("sp", zst[0][n2], a[:], reads=[ba], writes=[bzst[0]])
            z, bz = zb.next()
            c.op("act", lambda e, z=z, a=a: e.copy(out=z[:], in_=a[:]), reads=[ba], writes=[bz])
            fftc.stage_a(sc[0], n2, z[:], bz)
        for o in range(2):
            for k1 in range(NK1):
                hr, bhr = hl.next()
                hi, bhi = hl.next()
                c.dma("sp", hr[0:NC, :], Hs[o, 0, k1], reads=[bHs], writes=[bhr])
                c.dma("sp", hi[0:NC, :], Hs[o, 1, k1], reads=[bHs], writes=[bhi])
                fftc.stage_b_h(sc[o], k1, hr, bhr, hi, bhi)
            load_family(1 + o)
            st = ColStream(1 + o, NC, base, plain)
            for m2 in range(NC):
                p, bp = fftc.stage_ainv(sc[o], m2)
                g, bg = st.col(m2)
                zp, bzp = zs.next()
                c.dma("sp", zp[:], zst[o][m2], reads=[bzst[o]], writes=[bzp])
                t, bt = cr.next()
                c.op("dve", lambda e, t=t, zp=zp, o=o: e.tensor_tensor(out=t[:], in0=zp[:], in1=hbt[:, o, :], op=ALU.mult),
                     reads=[bzp, bct], writes=[bt])
                c.op("dve", lambda e, t=t, p=p: e.tensor_tensor(out=t[:], in0=t[:], in1=p[0:64, 0:C], op=ALU.add),
                     reads=[bt, bp], writes=[bt])
                c.op("dve", lambda e, t=t, g=g: e.tensor_tensor(out=t[:], in0=t[:], in1=g[:], op=ALU.mult),
                     reads=[bt, bg], writes=[bt])
                if o == 0:
                    c.dma("sp", zst[1][m2], t[:], reads=[bt], writes=[bzst[1]])
                    z, bz = zb.next()
                    c.op("act", lambda e, z=z, t=t: e.copy(out=z[:], in_=t[:]), reads=[bt], writes=[bz])
                    fftc.stage_a(sc[1], m2, z[:], bz)
                else:
                    oo, boo = ob.next()
                    c.op("act", lambda e, oo=oo, t=t: e.copy(out=oo[:], in_=t[:]), reads=[bt], writes=[boo])
                    c.dma("st", zout[:, m2, :], oo[:], reads=[boo], writes=[b_out])

    run_conv("L", 64, LAT0, False, Hlat, bHl, zlat)
    if with_ctx:
        run_conv("C", 4, CTX0, True, Hctx, bHc, zctx)
    return host


def build_hy_prog(with_ctx):
    nc = bass.Bass("TRN2", target_bir_lowering=False)
    c = Ctx(nc)
    psum = make_psum(nc)
    din = hy_dram_inputs(nc, with_ctx)
    zlat = nc.dram_tensor("zlat", [64, 64, HC], BF16, kind="ExternalOutput").ap()
    zctx = nc.dram_tensor("zctx", [64, 4, HC], BF16, kind="ExternalOutput").ap() if with_ctx else None
    b_out = Buf()
    with phase_stack():
        host = emit_hy(nc, c, psum, din, zlat, zctx, b_out, with_ctx)
    c.wait_all("sp", [b_out])
    c.emit()
    return nc, host


def build_mixers_prog(with_ctx):
    nc = bass.Bass("TRN2", target_bir_lowering=False)
    c = Ctx(nc)
    psum = make_psum(nc)
    dg = gla_dram_inputs(nc)
    dh = hy_dram_inputs(nc, with_ctx)
    ogT = nc.dram_tensor("ogT", [512, TF], BF16, kind="ExternalOutput").ap()
    zlat = nc.dram_tensor("zlat", [64, 64, HC], BF16, kind="ExternalOutput").ap()
    zctx = nc.dram_tensor("zctx", [64, 4, HC], BF16, kind="ExternalOutput").ap() if with_ctx else None
    b_out = Buf()
    with phase_stack():
        emit_gla(nc, c, psum, dg, ogT, b_out)
    c.barrier()
    with phase_stack():
        host = emit_hy(nc, c, psum, dh, zlat, zctx, b_out, with_ctx)
    c.wait_all("sp", [b_out])
    c.emit()
    return nc, host


def hy_inputs_for_core(inp, l, core, hT_full, with_ctx, host_consts):
    b, hf = divmod(core, 2)
    ch = slice(hf * HC, (hf + 1) * HC)
    cols = np.concatenate([np.arange(OFF_H + f * 1024 + hf * HC, OFF_H + f * 1024 + (hf + 1) * HC) for f in range(3)])
    ccols = np.concatenate([np.arange(f * 1024 + hf * HC, f * 1024 + (hf + 1) * HC) for f in range(3)])
    m = {"wh": np.ascontiguousarray(inp["w_in"][l][:, cols]),
         "cw": np.ascontiguousarray(np.broadcast_to(inp["hy_conv_w"][l][:, ccols], (64, 3, 3 * HC))),
         "cb": np.ascontiguousarray(np.broadcast_to(inp["hy_conv_b"][l][ccols], (64, 3 * HC))),
         "hb": np.ascontiguousarray(np.broadcast_to(inp["hy_bias"][l][:, ch], (64, 2, HC)))}
    if hT_full is not None:
        hp = np.zeros((D, TFP), hT_full.dtype)
        hp[:, CTX0:CTX0 + CTX] = hT_full[:, :CTX]
        hp[:, LAT0:LAT0 + SEQ] = hT_full[:, CTX:]
        m["hTp"] = hp
    fl = filt_inputs(inp, l, hf, SEQ, 64, {})
    for k in ("featT", "tpos", "pmask"):
        m[k + "L"] = fl.pop(k)
    m.update(fl)
    if with_ctx:
        fc = filt_inputs(inp, l, hf, CTX, 4, {})
        for k in ("featT", "tpos", "pmask"):
            m[k + "C"] = fc[k]
    m.update(host_consts)
    return m


def _mods_cols(mods, core, cols):
    b = core // 2
    out = np.zeros((128, 8, 2, len(cols)), np.float32)
    for sid, row in enumerate((4, b)):
        for j, (l, m) in enumerate(cols):
            out[:, :, sid, j] = mods[l, row, m].reshape(8, 128).T
    return out


def _gv(vecs):
    return np.ascontiguousarray(np.stack([fmv(v) for v in vecs], axis=-1))


def _split_tokens(resT, with_ctx):
    lat = np.zeros((NB, SEQ, resT[0].shape[0]), resT[0].dtype)
    ctx = np.zeros((NB, CTX, resT[0].shape[0]), resT[0].dtype) if with_ctx else None
    for core in range(NCORES):
        b, hf = divmod(core, 2)
        a = resT[core].T
        o = 0
        if with_ctx:
            ctx[b, hf * TC:(hf + 1) * TC] = a[:TC]
            o = TC
        lat[b, hf * TL:(hf + 1) * TL] = a[o:]
    return ctx, lat


def kernel_multilaunch(**inp):
    inp = {k: np.asarray(v) for k, v in inp.items()}
    mods = run_ada(inp)
    segs_all = [(0, TC, 0), (TC, TT, 1)]
    segs_lat = [(0, TL, 1)]
    xl, xc = inp["x"], inp["ctx"]
    nc = get_prog(("ffn", 1, "mixnorm"), lambda: build_ffn_prog(1, "mixnorm", TT, segs_all))
    maps = []
    for core in range(NCORES):
        maps.append({"xT": core_tokens_T(xl, xc, core),
                     "mods": _mods_cols(mods, core, [(0, 0), (0, 1), (0, 2), (0, 3), (0, 4)]),
                     "gv": _gv([inp["ffn1_norm_g"][0], inp["mix_norm_g"][0]]),
                     "wg0": inp["ffn1_w_gate"][0], "wu0": inp["ffn1_w_up"][0], "wd0": inp["ffn1_w_down"][0]})
    res = run(nc, maps)
    xc, xl = _split_tokens([r["xo"] for r in res], True)
    hc, hl = _split_tokens([r["ho"] for r in res], True)
    out = None
    for l in range(2):
        last = (l == 1)
        hT_full = [np.ascontiguousarray(np.concatenate([hc[b], hl[b]], axis=0).T) for b in range(NB)]
        wc = not last
        nc, hcst = get_prog(("mixers", wc), lambda: build_mixers_prog(wc))
        maps = []
        for core in range(NCORES):
            m = gla_inputs_for_core(inp, l, core, hT_full[core // 2])
            m.update(hy_inputs_for_core(inp, l, core, hT_full[core // 2], wc, hcst))
            maps.append(m)
        res = run(nc, maps)
        og = [np.concatenate([res[2 * b]["ogT"], res[2 * b + 1]["ogT"]], axis=0) for b in range(NB)]
        zz = []
        for b in range(NB):
            parts = []
            if wc:
                parts.append(np.concatenate([res[2 * b]["zctx"].reshape(CTX, HC), res[2 * b + 1]["zctx"].reshape(CTX, HC)], axis=1))
            else:
                parts.append(np.zeros((CTX, 2 * HC), res[0]["zlat"].dtype))
            parts.append(np.concatenate([res[2 * b]["zlat"].reshape(SEQ, HC), res[2 * b + 1]["zlat"].reshape(SEQ, HC)], axis=1))
            zz.append(np.ascontiguousarray(np.concatenate(parts, axis=0).T))
        T = TL if last else TT
        segs = segs_lat if last else segs_all
        nc = get_prog(("mix", T), lambda: build_mix_prog(T, segs))
        maps = []
        for core in range(NCORES):
            b, hf = divmod(core, 2)
            tok = ([] if last else [np.arange(hf * TC, (hf + 1) * TC)]) + [CTX + np.arange(hf * TL, (hf + 1) * TL)]
            tok = np.concatenate(tok)
            maps.append({"xT": core_tokens_T(xl, xc, core, with_ctx=not last),
                         "hT": np.ascontiguousarray(hT_full[b][:, tok]),
                         "ogT": np.ascontiguousarray(og[b][:, tok]), "zT": np.ascontiguousarray(zz[b][:, tok]),
                         "mods": _mods_cols(mods, core, [(l, 5)]),
                         "wgla": inp["w_gla_out"][l], "why": inp["w_hy_out"][l],
                         "wgt": np.ascontiguousarray(inp["w_in"][l][:, OFF_G:OFF_G + 2 * D]), "wout": inp["w_out"][l]})
        res = run(nc, maps)
        xc2, xl = _split_tokens([r["xo"] for r in res], not last)
        if not last:
            xc = xc2
            nc = get_prog(("ffn", 2, "mixnorm"), lambda: build_ffn_prog(2, "mixnorm", TT, segs_all))
            maps = []
            for core in range(NCORES):
                maps.append({"xT": core_tokens_T(xl, xc, core),
                             "mods": _mods_cols(mods, core, [(0, 6), (0, 7), (0, 8), (1, 0), (1, 1), (1, 2), (1, 3), (1, 4)]),
                             "gv": _gv([inp["ffn2_norm_g"][0], inp["ffn1_norm_g"][1], inp["mix_norm_g"][1]]),
                             "wg0": inp["ffn2_w_gate"][0], "wu0": inp["ffn2_w_up"][0], "wd0": inp["ffn2_w_down"][0],
                             "wg1": inp["ffn1_w_gate"][1], "wu1": inp["ffn1_w_up"][1], "wd1": inp["ffn1_w_down"][1]})
            res = run(nc, maps)
            xc, xl = _split_tokens([r["xo"] for r in res], True)
            hc, hl = _split_tokens([r["ho"] for r in res], True)
        else:
            nc = get_prog(("ffn", 1, "final"), lambda: build_ffn_prog(1, "final", TL, segs_lat))
            maps = []
            for core in range(NCORES):
                maps.append({"xT": core_tokens_T(xl, xc, core, with_ctx=False),
                             "mods": _mods_cols(mods, core, [(1, 6), (1, 7), (1, 8), (1, 0), (1, 0)]),
                             "gv": _gv([inp["ffn2_norm_g"][1], inp["final_norm_g"]]),
                             "wg0": inp["ffn2_w_gate"][1], "wu0": inp["ffn2_w_up"][1], "wd0": inp["ffn2_w_down"][1]})
            res = run(nc, maps)
            _, out = _split_tokens([r["xo"] for r in res], False)
    return np.ascontiguousarray(out.astype(np.float32))


NMODCH = 2 * 9 * 8


def emit_ada_local(nc, c, psum, cs32, adaW, adaB, MODS, b_mods):
    cst = sb_alloc(nc, "ada_cs", [128, 8, 32], F32)
    sct = sb_alloc(nc, "ada_sc", [128, 8, 32], F32)
    bt = sb_alloc(nc, "ada_b", [128, NMODCH], F32)
    b_cs, b_sc, b_bt = Buf(), Buf(), Buf()
    c.dma("sp", cst[:], cs32, writes=[b_cs])
    c.dma("sp", bt[:], adaB, writes=[b_bt])
    c.op("act", lambda e: e.activation(out=sct[:], in_=cst[:], func=AF.Silu), reads=[b_cs], writes=[b_sc])
    wrot = Rot(nc, "ada_wp", 3, [128, 8, 512], F32)
    for pc in range(NMODCH // 4):
        wt, bw = wrot.next()
        c.dma("sp", wt[:], adaW[:, pc * 512:(pc + 1) * 512].rearrange("(k p) n -> p k n", p=128), writes=[bw])
        for j in range(4):
            ch = pc * 4 + j
            p, bp = psum.next()
            for k in range(8):
                c.op("pe", lambda e, p=p, wt=wt, k=k, j=j: e.matmul(
                    p[:, 0:32], lhsT=wt[:, k, j * 128:(j + 1) * 128], rhs=sct[:, k, :], start=(k == 0), stop=(k == 7)),
                    reads=[bw, b_sc], writes=[bp])
            c.op("dve", lambda e, p=p, ch=ch: e.tensor_scalar(out=MODS[:, ch, :], in0=p[:, 0:2], scalar1=bt[:, ch:ch + 1],
                                                               scalar2=None, op0=ALU.add),
                 reads=[bp, b_bt], writes=[b_mods])


def build_fused_prog():
    nc = bass.Bass("TRN2", target_bir_lowering=False)
    c = Ctx(nc)
    psum = make_psum(nc)
    ein = lambda name, shape, dt=F32: nc.dram_tensor(name, shape, dt, kind="ExternalInput").ap()
    xin = ein("xin", [D, TF])
    cs32 = ein("cs32", [128, 8, 32])
    adaW = ein("adaW", [D, NMODCH * 128])
    adaB = ein("adaB", [128, NMODCH])
    gvd = ein("gvec", [128, 8, 7])
    identd = ein("ident", [128, 128], BF16)
    zcol = ein("zcol", [128, 8], BF16)
    W_ffn = {(l, w): ffn_dram_weights(nc, 1, tag=f"_{l}{w}")[0] for l in range(2) for w in (1, 2)}
    W_mix = [mix_dram_weights(nc, tag=f"_{l}") for l in range(2)]
    out = nc.dram_tensor("out", [D, SEQ], F32, kind="ExternalOutput").ap()
    Xs = nc.dram_tensor("Xs", [D, TF], F32).ap()
    Hs = nc.dram_tensor("Hs", [D, TF], BF16).ap()
    Hp = nc.dram_tensor("Hp", [D, TFP], BF16).ap()
    OGs = nc.dram_tensor("OGs", [2 * HC, TF], BF16).ap()
    ZL = nc.dram_tensor("ZL", [64, 64, 2 * HC], BF16).ap()
    ZC = nc.dram_tensor("ZC", [64, 4, 2 * HC], BF16).ap()
    bXs, bHs, bOG, bZ = Buf(), Buf(), Buf(), Buf()
    shared_consts, pos = {}, {}
    dG = {(l, q): gla_dram_inputs(nc, hT=Hs, tag=f"_{l}{q}", shared=shared_consts) for l in range(2) for q in range(2)}
    dH = {(l, q): hy_dram_inputs(nc, l == 0, hTp=Hp, tag=f"_{l}{q}", pos=pos) for l in range(2) for q in range(2)}
    host = {}
    MODS = sb_alloc(nc, "MODS", [128, NMODCH, 2], F32)
    b_mods = Buf()
    gt_all = sb_alloc(nc, "gvec_t", [128, 8, 7], F32)
    b_gall = Buf()
    c.dma("sp", gt_all[:], gvd, writes=[b_gall])
    zc_t = sb_alloc(nc, "zcol_t", [128, 8], BF16)
    b_zc = Buf()
    c.dma("sp", zc_t[:], zcol, writes=[b_zc])
    hp_v = Hp.rearrange("(k p) t -> p k t", p=128)
    for col in (0, CTX0 + CTX):
        c.dma("sp", hp_v[:, :, col], zc_t[:], reads=[b_zc], writes=[bHs], allow_slow_non_contiguous=True)
    with phase_stack():
        emit_ada_local(nc, c, psum, cs32, adaW, adaB, MODS, b_mods)
    c.barrier()

    def mods_table(cols):
        mt = sb_alloc(nc, "mt", [128, 8, 2, len(cols)], F32)
        b_m = Buf()
        for j, (l, m) in enumerate(cols):
            ch0 = (l * 9 + m) * 8
            for sid in range(2):
                c.op("dve", lambda e, mt=mt, j=j, sid=sid, ch0=ch0: e.tensor_copy(out=mt[:, :, sid, j],
                                                                                  in_=MODS[:, ch0:ch0 + 8, sid]),
                     reads=[b_mods], writes=[b_m])
        return mt, b_m

    def gv_table(idx):
        gt = sb_alloc(nc, "gt", [128, 8, len(idx)], F32)
        b_g = Buf()
        for j, i in enumerate(idx):
            c.op("dve", lambda e, gt=gt, j=j, i=i: e.tensor_copy(out=gt[:, :, j], in_=gt_all[:, :, i]),
                 reads=[b_gall], writes=[b_g])
        return gt, b_g

    def gcol(hh, with_ctx):
        def f(t0):
            if with_ctx:
                return hh * TC + t0 if t0 < TC else CTX + hh * TL + (t0 - TC)
            return CTX + hh * TL + t0
        return f

    xs_v = Xs.rearrange("(k p) t -> p k t", p=128)
    xin_v = xin.rearrange("(k p) t -> p k t", p=128)
    hs_v = Hs.rearrange("(k p) t -> p k t", p=128)
    og_v = OGs.rearrange("(k p) t -> p k t", p=128)
    segs_all = [(0, TC, 0), (TC, TT, 1)]
    segs_lat = [(0, TL, 1)]

    def f_phase(hh, stages, first):
        g = gcol(hh, True)
        src = xin_v if first else xs_v
        with phase_stack():
            env = setup_common(nc, c, psum)
            cols, gidx, ws = [], [], []
            for (l, w) in stages:
                base = 0 if w == 1 else 6
                cols += [(l, base), (l, base + 1), (l, base + 2)]
                gidx.append(3 * l + (0 if w == 1 else 2))
                ws.append(W_ffn[(l, w)])
            lt = stages[-1][0] + (0 if stages[-1][1] == 1 else 1)
            cols += [(lt, 3), (lt, 4)]
            gidx.append(3 * lt + 1)
            mt, b_m = mods_table(cols)
            gt, b_g = gv_table(gidx)

            def hp_of(t0, t1):
                g0 = g(t0)
                off = CTX0 if g0 < CTX else LAT0 - CTX
                return hp_v[:, :, g0 + off:g0 + off + (t1 - t0)]
            bo = emit_ffn_chain(nc, c, env, len(stages), "mixnorm", TT, segs_all,
                                lambda t0, t1: src[:, :, g(t0):g(t0) + (t1 - t0)], mt, b_m, gt, b_g, ws,
                                lambda t0, t1: xs_v[:, :, g(t0):g(t0) + (t1 - t0)],
                                [lambda t0, t1: hs_v[:, :, g(t0):g(t0) + (t1 - t0)], hp_of])
            c.wait_all("sp", [bo])
        c.barrier()

    def mixers_phase(l):
        for q in range(2):
            with phase_stack():
                emit_gla(nc, c, psum, dG[(l, q)], OGs[q * HC:(q + 1) * HC, :], bOG)
            c.barrier()
            with phase_stack():
                host.update(emit_hy(nc, c, psum, dH[(l, q)], ZL[:, :, q * HC:(q + 1) * HC],
                                    ZC[:, :, q * HC:(q + 1) * HC] if l == 0 else None, bZ, l == 0))
            c.barrier()

    zl_flat = ZL.rearrange("r c f -> (r c) f")
    zc_flat = ZC.rearrange("r c f -> (r c) f")

    def mix_phase(l, hh, with_ctx, env, xdst_of=None):
        g = gcol(hh, with_ctx)
        T = TT if with_ctx else TL
        segs = segs_all if with_ctx else segs_lat
        mt, b_m = mods_table([(l, 5)])

        def z_tm(t0, t1):
            g0 = g(t0)
            if g0 < CTX:
                return zc_flat[g0:g0 + (t1 - t0), :]
            return zl_flat[g0 - CTX:g0 - CTX + (t1 - t0), :]
        rng = lambda v: (lambda t0, t1: v[:, :, g(t0):g(t0) + (t1 - t0)])
        src_of = {"x": rng(xs_v), "h": rng(hs_v), "og": rng(og_v)}
        return emit_mix(nc, c, env, T, segs, src_of, xdst_of or rng(xs_v), mt, b_m, W_mix[l], z_tm_of=z_tm)

    for hh in range(2):
        f_phase(hh, [(0, 1)], first=True)
    mixers_phase(0)
    for hh in range(2):
        with phase_stack():
            env = setup_common(nc, c, psum)
            load_ident(nc, c, env, identd)
            bo = mix_phase(0, hh, True, env)
            c.wait_all("sp", [bo])
        c.barrier()
    for hh in range(2):
        f_phase(hh, [(0, 2), (1, 1)], first=False)
    mixers_phase(1)
    out_v = out.rearrange("(k p) t -> p k t", p=128)
    for hh in range(2):
        with phase_stack():
            env = setup_common(nc, c, psum)
            load_ident(nc, c, env, identd)
            bo = mix_phase(1, hh, False, env)
            c.wait_all("sp", [bo])
        c.barrier()
    b_fin = Buf()
    for hh in range(2):
        g = gcol(hh, False)
        with phase_stack():
            env = setup_common(nc, c, psum)
            mt, b_m = mods_table([(1, 6), (1, 7), (1, 8), (1, 0), (1, 0)])
            gt, b_g = gv_table([5, 6])
            bo = emit_ffn_chain(nc, c, env, 1, "final", TL, segs_lat,
                                lambda t0, t1, g=g: xs_v[:, :, g(t0):g(t0) + (t1 - t0)], mt, b_m, gt, b_g,
                                [W_ffn[(1, 2)]],
                                lambda t0, t1, hh=hh: out_v[:, :, hh * TL + t0:hh * TL + t1])
            c.wait_all("sp", [bo])
        c.barrier()
    c.emit()
    return nc, host


def fused_inputs_for_batch(inp, b, host_consts):
    bf = ml_dtypes.bfloat16
    m = {}
    m["xin"] = np.ascontiguousarray(np.concatenate([inp["ctx"][b], inp["x"][b]], axis=0).T)
    cs = np.zeros((32, D), np.float32)
    cs[0] = inp["c_ctx"]
    cs[1] = inp["c"][b]
    m["cs32"] = np.ascontiguousarray(cs.reshape(32, 8, 128).transpose(2, 1, 0))
    m["adaW"] = np.ascontiguousarray(np.concatenate([inp["ada_w"][0], inp["ada_w"][1]], axis=1))
    m["adaB"] = np.ascontiguousarray(np.concatenate([inp["ada_b"][0], inp["ada_b"][1]]).reshape(NMODCH, 128).T)
    m["gvec"] = _gv([inp["ffn1_norm_g"][0], inp["mix_norm_g"][0], inp["ffn2_norm_g"][0],
                     inp["ffn1_norm_g"][1], inp["mix_norm_g"][1], inp["ffn2_norm_g"][1], inp["final_norm_g"]])
    m["ident"] = np.eye(128, dtype=np.float32).astype(bf)
    m["zcol"] = np.zeros((128, 8), bf)
    for l in range(2):
        for w, pre in ((1, "ffn1"), (2, "ffn2")):
            m[f"wg0_{l}{w}"] = inp[f"{pre}_w_gate"][l]
            m[f"wu0_{l}{w}"] = inp[f"{pre}_w_up"][l]
            m[f"wd0_{l}{w}"] = inp[f"{pre}_w_down"][l]
        m[f"wgla_{l}"] = inp["w_gla_out"][l]
        m[f"why_{l}"] = inp["w_hy_out"][l]
        m[f"wgt_{l}"] = np.ascontiguousarray(inp["w_in"][l][:, OFF_G:OFF_G + 2 * D])
        m[f"wout_{l}"] = inp["w_out"][l]
        for q in range(2):
            tag = f"_{l}{q}"
            g = gla_inputs_for_core(inp, l, q, None)
            for k in ("wk", "wq", "wv", "wr", "wa", "waf", "wab", "gng"):
                m[k + tag] = g[k]
            for k in ("atmpl",) + GLA_CNAMES:
                m[k] = g[k]
            h = hy_inputs_for_core(inp, l, q, None, l == 0, {})
            for k in ("wh", "cw", "cb", "hb", "w1", "w2p", "pr1", "pr2", "w3s", "dlt"):
                m[k + tag] = h[k]
            for k in h:
                if k.startswith(("featT", "tpos", "pmask")):
                    m[k] = h[k]
    m.update(host_consts)
    return m


def kernel_fused(**inp):
    inp = {k: np.asarray(v) for k, v in inp.items()}
    nc, hcst = get_prog("fused", build_fused_prog)
    per_batch = [fused_inputs_for_batch(inp, b, hcst) for b in range(NB)]
    res = run(nc, [per_batch[core // 2] for core in range(NCORES)])
    out = np.zeros((NB, SEQ, D), np.float32)
    for core in range(NCORES):
        b, hf = divmod(core, 2)
        out[b, hf * TL:(hf + 1) * TL] = res[core]["out"][:, hf * TL:(hf + 1) * TL].T
    return out


def kernel(**inp):
    return kernel_fused(**inp)
```

```python
import numpy as np
import ml_dtypes
import concourse.bass as bass
import concourse.mybir as mybir
from concourse.bass_utils import run_bass_kernel_spmd

F32 = mybir.dt.float32
BF16 = mybir.dt.bfloat16
AF = mybir.ActivationFunctionType
ALU = mybir.AluOpType
AX = mybir.AxisListType

SEM_EPOCH = 30000
NCORES = 8
D = 1024
DFF = 2816
NFF = DFF // 128
EPS = 1e-6
SEQ = 4096
CTX = 256
NB = 4
TL = SEQ // 2
TC = CTX // 2
TT = TL + TC


class Buf:
    __slots__ = ("w", "r", "name", "excl")

    def __init__(self, name="", excl=False):
        self.w = None
        self.r = []
        self.name = name
        self.excl = excl


class Ctx:
    ENG = ("pe", "dve", "act", "pool", "sp")

    def __init__(self, nc, n_dma_sems=24):
        self.nc = nc
        self.prog = {e: [] for e in self.ENG}
        self.sems = {}
        self.cnt = {e: 0 for e in self.ENG}
        self.seen = {e: {} for e in self.ENG}
        self.n_dma_sems = n_dma_sems
        self.n_hw = (2 * n_dma_sems) // 3
        self.n_dma = {"hw": 0, "sw": 0}
        self.dma_cnt = [0] * n_dma_sems
        self._sid = 0
        self._uid = 0

    def uid(self, p="t"):
        self._uid += 1
        return f"{p}{self._uid}"

    def _sem(self, key):
        if key not in self.sems:
            self._sid += 1
            self.sems[key] = self.nc.alloc_semaphore(f"s{self._sid}")
        return self.sems[key]

    def _next_event(self, eng):
        c = self.cnt[eng]
        self.cnt[eng] = c + 1
        return ((eng, c // SEM_EPOCH), c % SEM_EPOCH + 1, eng)

    def _collect(self, eng, reads, writes, extra=()):
        deps = {}

        def add(ev):
            if ev is None:
                return
            key, val, src = ev
            if src == "pe" and eng == "pe":
                return
            if deps.get(key, 0) < val:
                deps[key] = val
        for b in reads:
            add(b.w)
        for b in writes:
            add(b.w)
            for ev in b.r:
                add(ev)
        for ev in extra:
            add(ev)
        waits = []
        seen = self.seen[eng]
        for key, val in deps.items():
            if seen.get(key, 0) < val:
                seen[key] = val
                waits.append((self._sem(key), val))
        return waits

    def _mark(self, ev, reads, writes):
        for b in writes:
            b.w = ev
            b.r = []
        for b in reads:
            if b in writes:
                continue
            b.r.append(ev)
            if len(b.r) > 48:
                best = {}
                for e in b.r:
                    if best.get(e[0], (0,))[0] < e[1]:
                        best[e[0]] = (e[1], e)
                b.r = [v[1] for v in best.values()]

    def op(self, eng, fn, reads=(), writes=()):
        reads = list(reads)
        writes = list(writes)
        ex = [b for b in reads if b.excl and b not in writes]
        if ex:
            writes = writes + ex
            reads = [b for b in reads if not b.excl]
        waits = self._collect(eng, reads, writes)
        ev = self._next_event(eng)
        sem = self._sem(ev[0])
        self.prog[eng].append((waits, fn, sem, 1))
        self._mark(ev, reads, writes)
        return ev

    def dma(self, q, out, in_, reads=(), writes=(), **kw):
        reads = list(reads)
        writes = list(writes)
        if q == "st":
            prod = {b.w[2] for b in reads if b.w is not None}
            q = "act" if prod == {"act"} else "sp"
        kind = "sw" if q == "pool" else "hw"
        if kind == "hw":
            i = self.n_dma["hw"] % self.n_hw
        else:
            i = self.n_hw + self.n_dma["sw"] % (self.n_dma_sems - self.n_hw)
        self.n_dma[kind] += 1
        c = self.dma_cnt[i]
        per = SEM_EPOCH // 16
        ep, k = divmod(c, per)
        key = ("dma", i, ep)
        prev = []
        if k > 0:
            prev.append((key, 16 * k, "dma"))
        elif ep > 0:
            prev.append((("dma", i, ep - 1), 16 * per, "dma"))
        self.dma_cnt[i] = c + 1
        waits = self._collect(q, reads, writes, extra=prev)
        ev = (key, 16 * (k + 1), "dma")
        sem = self._sem(key)
        fn = lambda e, out=out, in_=in_, kw=kw: e.dma_start(out=out, in_=in_, **kw)
        self.prog[q].append((waits, fn, sem, 16))
        self._mark(ev, reads, writes)
        return ev

    def barrier(self):
        evs = []
        for e in self.ENG:
            cnt = self.cnt[e]
            if cnt > 0:
                cc = cnt - 1
                evs.append(((e, cc // SEM_EPOCH), cc % SEM_EPOCH + 1))
        per = SEM_EPOCH // 16
        for i in range(self.n_dma_sems):
            cnt = self.dma_cnt[i]
            if cnt > 0:
                ep, k = divmod(cnt - 1, per)
                evs.append((("dma", i, ep), 16 * (k + 1)))
        for e in self.ENG:
            waits = []
            for key, val in evs:
                if self.seen[e].get(key, 0) < val:
                    self.seen[e][key] = val
                    waits.append((self._sem(key), val))
            self.prog[e].append((waits, None, None, 0))

    def wait_all(self, eng, bufs):
        waits = self._collect(eng, list(bufs), [])
        self.prog[eng].append((waits, None, None, 0))

    def emit(self):
        nc = self.nc
        eobj = {"pe": "tensor", "dve": "vector", "act": "scalar", "pool": "gpsimd", "sp": "sync"}
        with nc.Block() as block:
            def mk(name):
                def body(e):
                    for waits, fn, sem, inc in self.prog[name]:
                        for s, v in waits:
                            e.wait_ge(s, v)
                        if fn is not None:
                            fn(e).then_inc(sem, inc)
                return body
            for name in self.ENG:
                getattr(block, eobj[name])(mk(name))


class FM:
    def __init__(self, nc, name, nch, T, dtype, blocks):
        self.t = sb_alloc(nc, name, [128, nch, T], dtype)
        self.nch = nch
        self.T = T
        self.blocks = blocks
        self.b = [[Buf(f"{name}{k}_{i}") for i in range(len(blocks))] for k in range(nch)]

    def ap(self, k, bi):
        t0, t1 = self.blocks[bi][0], self.blocks[bi][1]
        return self.t[:, k, t0:t1]


class Rot:
    def __init__(self, nc, name, n, shape, dtype, psum=False):
        self.tiles = []
        for i in range(n):
            if psum:
                t = nc.alloc_psum_tensor(f"{name}{i}", shape, dtype)
            else:
                t = sb_alloc(nc, f"{name}{i}", shape, dtype)
            self.tiles.append((t, Buf(f"{name}{i}")))
        self.i = 0

    def next(self):
        r = self.tiles[self.i % len(self.tiles)]
        self.i += 1
        return r


import contextlib

_STACK = [None]


_NAME_CTR = [0]


def sb_alloc(nc, name, shape, dtype):
    _NAME_CTR[0] += 1
    name = f"{name}_u{_NAME_CTR[0]}"
    if _STACK[0] is not None:
        return _STACK[0].enter_context(nc.sbuf_tensor(name, shape, dtype))
    return nc.alloc_sbuf_tensor(name, shape, dtype)


@contextlib.contextmanager
def phase_stack():
    es = contextlib.ExitStack()
    prev = _STACK[0]
    _STACK[0] = es
    try:
        yield es
    finally:
        _STACK[0] = prev
        es.close()


class View:
    def __init__(self, t, off, width):
        self.t, self.off, self.width = t, off, width

    def __getitem__(self, idx):
        if not isinstance(idx, tuple):
            idx = (idx, slice(None))
        ps, fs = idx
        a = 0 if fs.start is None else fs.start
        b = self.width if fs.stop is None else fs.stop
        return self.t[ps, self.off + a:self.off + b]


class RotPsumHalf:
    def __init__(self, nc, name, nbanks, width=256):
        self.tiles = []
        per = 512 // width
        banks = []
        for i in range(nbanks):
            t = nc.alloc_psum_tensor(f"{name}{i}", [128, 512], F32)
            banks.append((t, Buf(f"{name}{i}", excl=True)))
        for h in range(per):
            for t, b in banks:
                self.tiles.append((View(t, h * width, width), b))
        self.i = 0

    def next(self):
        r = self.tiles[self.i % len(self.tiles)]
        self.i += 1
        return r


def make_psum(nc, name="psg"):
    r = Rot(nc, name, 8, [128, 512], F32, psum=True)
    for t, b in r.tiles:
        b.excl = True
    return r


class PsumHalves:
    def __init__(self, rot, nbanks=7, width=256):
        self.tiles = []
        per = 512 // width
        banks = rot.tiles[:nbanks]
        for h in range(per):
            for t, b in banks:
                self.tiles.append((View(t, h * width, width), b))
        self.i = 0

    def next(self):
        r = self.tiles[self.i % len(self.tiles)]
        self.i += 1
        return r


class RotSlice:
    def __init__(self, rot, idx):
        self.tiles = [rot.tiles[i] for i in idx]
        self.i = 0

    def next(self):
        r = self.tiles[self.i % len(self.tiles)]
        self.i += 1
        return r


def make_blocks(segs):
    out = []
    for s, e, sid in segs:
        t = s
        while t < e:
            n = min(512, e - t)
            out.append((t, t + n, sid))
            t += n
    return out


class Env:
    pass


def setup_common(nc, c, psum=None):
    env = Env()
    env.nc = nc
    env.c = c
    env.ones = sb_alloc(nc, "ones_bf", [128, 128], BF16)
    env.b_ones = Buf("ones")
    c.op("pool", lambda e: e.memset(env.ones[:], 1.0), writes=[env.b_ones])
    if psum is None:
        env.ps_mm = Rot(nc, "psmm", 4, [128, 512], F32, psum=True)
        env.ps_d = Rot(nc, "psd", 2, [128, 512], F32, psum=True)
        env.ps_ss = Rot(nc, "psss", 1, [128, 512], F32, psum=True)
    else:
        env.ps_mm = RotSlice(psum, [0, 1, 2, 3])
        env.ps_d = RotSlice(psum, [4, 5])
        env.ps_ss = RotSlice(psum, [6])
    env.sq = Rot(nc, "sq", 2, [128, 512], BF16)
    env.tmpf = Rot(nc, "tmpf", 3, [128, 512], F32)
    env.tmpr = Rot(nc, "tmpr", 2, [128, 512], F32)
    env.tmpb = Rot(nc, "tmpb", 3, [128, 512], BF16)
    env.epsb = sb_alloc(nc, "epsb", [128, 1], F32)
    env.b_eps = Buf("eps")
    c.op("pool", lambda e: e.memset(env.epsb[:], EPS), writes=[env.b_eps])
    return env


class FMRot:
    def __init__(self, nc, name, nch, dtype, blocks, n=2):
        self.tiles = [sb_alloc(nc, f"{name}{i}", [128, nch, 512], dtype) for i in range(n)]
        self.bufs = [[Buf() for _ in range(nch)] for i in range(n)]
        self.n = n
        self.nch = nch
        self.blocks = blocks
        self.bo = Buf()

        class _B:
            def __init__(s2, outer):
                s2.o = outer

            def __getitem__(s2, k):
                class _R:
                    def __getitem__(s3, bi):
                        return s2.o.bufs[bi % s2.o.n][k]
                return _R()
        self.b = _B(self)

    def ap(self, k, bi):
        t0, t1 = self.blocks[bi][0], self.blocks[bi][1]
        return self.tiles[bi % self.n][:, k, 0:t1 - t0]

    def flush_to(self, c, dst_of, bi, q="sp"):
        t0, t1 = self.blocks[bi][0], self.blocks[bi][1]
        c.dma(q, dst_of(t0, t1), self.tiles[bi % self.n][:, :, 0:t1 - t0],
              reads=self.bufs[bi % self.n], writes=[self.bo])

    def flush(self, c, dst, bi, q="sp"):
        t0, t1 = self.blocks[bi][0], self.blocks[bi][1]
        c.dma(q, dst[:, :, t0:t1], self.tiles[bi % self.n][:, :, 0:t1 - t0],
              reads=self.bufs[bi % self.n], writes=[self.bo])


def emit_norm_mod(env, X, OUT, A_ap, S_ap, b_AS, nfeat=D, out_cb=None):
    c = env.c
    nch = X.nch
    for bi, (t0, t1, sid) in enumerate(X.blocks):
        n = t1 - t0
        ps, bps = env.ps_ss.next()
        for k in range(nch):
            sq, bsq = env.sq.next()
            c.op("act", lambda e, sq=sq, k=k, bi=bi, n=n: e.activation(out=sq[:, :n], in_=X.ap(k, bi), func=AF.Square),
                 reads=[X.b[k][bi]], writes=[bsq])
            c.op("pe", lambda e, ps=ps, sq=sq, k=k, n=n: e.matmul(ps[:, :n], lhsT=env.ones[:], rhs=sq[:, :n],
                                                                   start=(k == 0), stop=(k == nch - 1)),
                 reads=[bsq, env.b_ones], writes=[bps])
        ln, bln = env.tmpr.next()
        c.op("act", lambda e, ln=ln, ps=ps, n=n: e.activation(out=ln[:, :n], in_=ps[:, :n], func=AF.Ln,
                                                               bias=env.epsb[:, 0:1], scale=1.0 / nfeat),
             reads=[bps, env.b_eps], writes=[bln])
        rs, brs = env.tmpr.next()
        c.op("act", lambda e, ln=ln, rs=rs, n=n: e.activation(out=rs[:, :n], in_=ln[:, :n], func=AF.Exp, scale=-0.5),
             reads=[bln], writes=[brs])
        for k in range(nch):
            t2, bt2 = env.tmpf.next()
            c.op("dve", lambda e, t2=t2, rs=rs, k=k, bi=bi, n=n, sid=sid: e.scalar_tensor_tensor(
                out=t2[:, :n], in0=X.ap(k, bi), scalar=A_ap(k, sid), in1=rs[:, :n], op0=ALU.mult, op1=ALU.mult),
                reads=[X.b[k][bi], brs, b_AS], writes=[bt2])
            if S_ap is None:
                c.op("act", lambda e, t2=t2, k=k, bi=bi, n=n: e.copy(out=OUT.ap(k, bi), in_=t2[:, :n]),
                     reads=[bt2], writes=[OUT.b[k][bi]])
            else:
                c.op("act", lambda e, t2=t2, k=k, bi=bi, n=n, sid=sid: e.activation(
                    out=OUT.ap(k, bi), in_=t2[:, :n], func=AF.Identity, bias=S_ap(k, sid), scale=1.0),
                    reads=[bt2, b_AS], writes=[OUT.b[k][bi]])
        if out_cb is not None:
            out_cb(bi)


def load_w_panel(env, rot, w_ap, r0, nrow, c0, ncol, q="pool"):
    t, b = rot.next()
    kc = nrow // 128
    src = w_ap[r0:r0 + nrow, c0:c0 + ncol].rearrange("(k p) n -> p k n", p=128)
    env.c.dma(q, t[:, 0:kc, 0:ncol], src, writes=[b])
    return t, b


def emit_ffn(env, H, X, U, wg, wu, wd, CG_ap, b_CG, wpan, wdpan):
    c = env.c
    nhalf = U.nch
    j0 = 0
    while j0 < NFF:
        jn = min(nhalf, NFF - j0)
        wds = []
        for jl in range(jn):
            j = j0 + jl
            tg, bg = load_w_panel(env, wpan, wg, 0, D, j * 128, 128)
            tu, bu = load_w_panel(env, wpan, wu, 0, D, j * 128, 128)
            td, bd = load_w_panel(env, wdpan, wd, j * 128, 128, 0, D)
            wds.append((td, bd))
            for bi, (t0, t1, sid) in enumerate(H.blocks):
                n = t1 - t0
                pg, bpg = env.ps_mm.next()
                pu, bpu = env.ps_mm.next()
                for k in range(8):
                    c.op("pe", lambda e, pg=pg, tg=tg, k=k, bi=bi, n=n: e.matmul(
                        pg[:, :n], lhsT=tg[:, k, :], rhs=H.ap(k, bi), start=(k == 0), stop=(k == 7)),
                        reads=[bg, H.b[k][bi]], writes=[bpg])
                for k in range(8):
                    c.op("pe", lambda e, pu=pu, tu=tu, k=k, bi=bi, n=n: e.matmul(
                        pu[:, :n], lhsT=tu[:, k, :], rhs=H.ap(k, bi), start=(k == 0), stop=(k == 7)),
                        reads=[bu, H.b[k][bi]], writes=[bpu])
                sg, bsg = env.tmpb.next()
                c.op("act", lambda e, sg=sg, pg=pg, n=n: e.activation(out=sg[:, :n], in_=pg[:, :n], func=AF.Silu),
                     reads=[bpg], writes=[bsg])
                c.op("dve", lambda e, sg=sg, pu=pu, jl=jl, bi=bi, n=n: e.tensor_tensor(
                    out=U.ap(jl, bi), in0=sg[:, :n], in1=pu[:, :n], op=ALU.mult),
                    reads=[bsg, bpu], writes=[U.b[jl][bi]])
        for dk in range(8):
            for bi, (t0, t1, sid) in enumerate(X.blocks):
                n = t1 - t0
                pd, bpd = env.ps_d.next()
                for jl in range(jn):
                    td, bd = wds[jl]
                    c.op("pe", lambda e, pd=pd, td=td, jl=jl, dk=dk, bi=bi, n=n: e.matmul(
                        pd[:, :n], lhsT=td[:, 0, dk * 128:(dk + 1) * 128], rhs=U.ap(jl, bi),
                        start=(jl == 0), stop=(jl == jn - 1)),
                        reads=[bd, U.b[jl][bi]], writes=[bpd])
                c.op("dve", lambda e, pd=pd, dk=dk, bi=bi, n=n, sid=sid: e.scalar_tensor_tensor(
                    out=X.ap(dk, bi), in0=pd[:, :n], scalar=CG_ap(dk, sid), in1=X.ap(dk, bi),
                    op0=ALU.mult, op1=ALU.add),
                    reads=[bpd, b_CG, X.b[dk][bi]], writes=[X.b[dk][bi]])
        j0 += jn


ADA_COLS = 2 * 9 * D // NCORES


def build_ada():
    nc = bass.Bass("TRN2", target_bir_lowering=False)
    cs = nc.dram_tensor("cs", [128, 8, 8], F32, kind="ExternalInput").ap()
    w = nc.dram_tensor("w", [D, ADA_COLS], F32, kind="ExternalInput").ap()
    bb = nc.dram_tensor("bb", [8, ADA_COLS], F32, kind="ExternalInput").ap()
    out = nc.dram_tensor("out", [8, ADA_COLS], F32, kind="ExternalOutput").ap()
    c = Ctx(nc)
    cst = sb_alloc(nc, "cst", [128, 8, 8], F32)
    sct = sb_alloc(nc, "sct", [128, 8, 8], F32)
    bt = sb_alloc(nc, "bt", [8, ADA_COLS], F32)
    ot = sb_alloc(nc, "ot", [8, ADA_COLS], F32)
    b_cs, b_sc, b_bt, b_ot, b_out = Buf(), Buf(), Buf(), Buf(), Buf()
    c.dma("sp", cst[:], cs, writes=[b_cs])
    c.dma("sp", bt[:], bb, writes=[b_bt])
    c.op("act", lambda e: e.activation(out=sct[:], in_=cst[:], func=AF.Silu), reads=[b_cs], writes=[b_sc])
    wrot = Rot(nc, "wp", 3, [128, 8, 512], F32)
    prot = Rot(nc, "pp", 2, [8, 512], F32, psum=True)
    col = 0
    while col < ADA_COLS:
        n = min(512, ADA_COLS - col)
        wt, bw = wrot.next()
        c.dma("sp", wt[:, :, 0:n], w[:, col:col + n].rearrange("(k p) n -> p k n", p=128), writes=[bw])
        ps, bps = prot.next()
        for k in range(8):
            c.op("pe", lambda e, ps=ps, wt=wt, k=k, n=n: e.matmul(ps[:, :n], lhsT=sct[:, k, :], rhs=wt[:, k, 0:n],
                                                                   start=(k == 0), stop=(k == 7)),
                 reads=[b_sc, bw], writes=[bps])
        c.op("dve", lambda e, ps=ps, col=col, n=n: e.tensor_tensor(out=ot[:, col:col + n], in0=ps[:, :n],
                                                                    in1=bt[:, col:col + n], op=ALU.add),
             reads=[bps, b_bt], writes=[b_ot])
        col += n
    c.dma("sp", out, ot[:], reads=[b_ot], writes=[b_out])
    c.wait_all("sp", [b_out])
    c.emit()
    return nc


def ffn_dram_weights(nc, n_ffn, tag=""):
    ws = []
    for i in range(n_ffn):
        ws.append((nc.dram_tensor(f"wg{i}{tag}", [D, DFF], F32, kind="ExternalInput").ap(),
                   nc.dram_tensor(f"wu{i}{tag}", [D, DFF], F32, kind="ExternalInput").ap(),
                   nc.dram_tensor(f"wd{i}{tag}", [DFF, D], F32, kind="ExternalInput").ap()))
    return ws


def emit_ffn_chain(nc, c, env, n_ffn, tail, T, segs, xsrc_of, mt, b_m, gt, b_g, ws, xdst_of, hdsts=()):
    blocks = make_blocks(segs)
    X = FM(nc, "X", 8, T, F32, blocks)
    H = FM(nc, "H", 8, T, BF16, blocks)
    U = FM(nc, "U", 11, T, BF16, blocks)
    wpan = Rot(nc, "wpan", 4, [128, 8, 128], BF16)
    wdpan = Rot(nc, "wdpan", 12, [128, 1, D], BF16)
    for bi, (t0, t1, sid) in enumerate(blocks):
        c.dma("sp", X.t[:, :, t0:t1], xsrc_of(t0, t1), writes=[X.b[k][bi] for k in range(8)])
    der = sb_alloc(nc, "der", [128, 8, 2, 2 * (n_ffn + 1)], F32)
    b_der = Buf()
    nst = n_ffn + (1 if tail == "mixnorm" else 0)
    for i in range(n_ffn + 1):
        for sid in range(2):
            if i < nst:
                sc_col = 3 * i + 1
                c.op("dve", lambda e, i=i, sid=sid, sc_col=sc_col: e.scalar_tensor_tensor(
                    out=der[:, :, sid, 2 * i], in0=mt[:, :, sid, sc_col], scalar=1.0, in1=gt[:, :, i],
                    op0=ALU.add, op1=ALU.mult), reads=[b_m, b_g], writes=[b_der])
            else:
                c.op("dve", lambda e, i=i, sid=sid: e.tensor_copy(out=der[:, :, sid, 2 * i], in_=gt[:, :, i]),
                     reads=[b_g], writes=[b_der])
            if i < n_ffn:
                c.op("dve", lambda e, i=i, sid=sid: e.tensor_scalar(
                    out=der[:, :, sid, 2 * i + 1], in0=mt[:, :, sid, 3 * i + 2], scalar1=0.5, scalar2=None,
                    op0=ALU.mult), reads=[b_m], writes=[b_der])
    for i in range(n_ffn):
        emit_norm_mod(env, X, H, lambda k, sid, i=i: der[:, k, sid, 2 * i:2 * i + 1],
                      lambda k, sid, i=i: mt[:, k, sid, 3 * i:3 * i + 1], b_der)
        emit_ffn(env, H, X, U, ws[i][0], ws[i][1], ws[i][2],
                 lambda k, sid, i=i: der[:, k, sid, 2 * i + 1:2 * i + 2], b_der, wpan, wdpan)
    i = n_ffn
    bo = Buf()
    if tail == "mixnorm":
        emit_norm_mod(env, X, H, lambda k, sid: der[:, k, sid, 2 * i:2 * i + 1],
                      lambda k, sid: mt[:, k, sid, 3 * i:3 * i + 1], b_der)
        for bi, (t0, t1, sid) in enumerate(blocks):
            for hdst_of in hdsts:
                c.dma("sp", hdst_of(t0, t1), H.t[:, :, t0:t1], reads=[H.b[k][bi] for k in range(8)], writes=[bo])
            c.dma("sp", xdst_of(t0, t1), X.t[:, :, t0:t1], reads=[X.b[k][bi] for k in range(8)], writes=[bo])
    else:
        O = FMRot(nc, "O", 8, F32, blocks, n=1)
        O.bo = bo
        emit_norm_mod(env, X, O, lambda k, sid: der[:, k, sid, 2 * i:2 * i + 1], None, b_der,
                      out_cb=lambda bi: O.flush_to(c, xdst_of, bi))
    return bo


def build_ffn_prog(n_ffn, tail, T, segs):
    nc = bass.Bass("TRN2", target_bir_lowering=False)
    nm = 3 * n_ffn + 2
    xT = nc.dram_tensor("xT", [D, T], F32, kind="ExternalInput").ap()
    mods = nc.dram_tensor("mods", [128, 8, 2, nm], F32, kind="ExternalInput").ap()
    gv = nc.dram_tensor("gv", [128, 8, n_ffn + 1], F32, kind="ExternalInput").ap()
    ws = ffn_dram_weights(nc, n_ffn)
    xo = nc.dram_tensor("xo", [D, T], F32, kind="ExternalOutput").ap()
    hdsts = []
    if tail == "mixnorm":
        ho = nc.dram_tensor("ho", [D, T], BF16, kind="ExternalOutput").ap()
        hd = ho.rearrange("(k p) t -> p k t", p=128)
        hdsts.append(lambda t0, t1: hd[:, :, t0:t1])
    c = Ctx(nc)
    env = setup_common(nc, c)
    mt = sb_alloc(nc, "mt", [128, 8, 2, nm], F32)
    gt = sb_alloc(nc, "gt", [128, 8, n_ffn + 1], F32)
    b_m, b_g = Buf(), Buf()
    c.dma("sp", mt[:], mods, writes=[b_m])
    c.dma("sp", gt[:], gv, writes=[b_g])
    xs = xT.rearrange("(k p) t -> p k t", p=128)
    xd = xo.rearrange("(k p) t -> p k t", p=128)
    bo = emit_ffn_chain(nc, c, env, n_ffn, tail, T, segs, lambda t0, t1: xs[:, :, t0:t1], mt, b_m, gt, b_g, ws,
                        lambda t0, t1: xd[:, :, t0:t1], hdsts)
    c.wait_all("sp", [bo])
    c.emit()
    return nc


def fmv(v):
    v = np.asarray(v, np.float32)
    lead = v.shape[:-1]
    a = v.reshape(lead + (v.shape[-1] // 128, 128))
    a = np.moveaxis(a, [-1, -2], [0, 1])
    return np.ascontiguousarray(a)


_PROGS = {}


def get_prog(key, builder):
    if key not in _PROGS:
        _PROGS[key] = builder()
    return _PROGS[key]


def run(nc, in_maps):
    res = run_bass_kernel_spmd(nc, in_maps, core_ids=list(range(NCORES)))
    return res.results


def run_ada(inp):
    cs = np.zeros((8, D), np.float32)
    cs[0:4] = inp["c"]
    cs[4] = inp["c_ctx"]
    csf = fmv(cs.T.reshape(D, 8).T) if False else np.ascontiguousarray(
        cs.reshape(8, 8, 128).transpose(2, 1, 0))
    W = np.concatenate([inp["ada_w"][0], inp["ada_w"][1]], axis=1)
    bvec = np.concatenate([inp["ada_b"][0], inp["ada_b"][1]])
    nc = get_prog("ada", build_ada)
    maps = []
    for i in range(NCORES):
        sl = slice(i * ADA_COLS, (i + 1) * ADA_COLS)
        maps.append({"cs": csf, "w": np.ascontiguousarray(W[:, sl]),
                     "bb": np.ascontiguousarray(np.broadcast_to(bvec[sl], (8, ADA_COLS)))})
    res = run(nc, maps)
    full = np.concatenate([r["out"] for r in res], axis=1)[:5]
    return np.ascontiguousarray(full.reshape(5, 2, 9, D).transpose(1, 0, 2, 3))


def core_tokens_T(xl, xc, core, with_ctx=True):
    b, hf = divmod(core, 2)
    parts = []
    if with_ctx:
        parts.append(xc[b, hf * TC:(hf + 1) * TC])
    parts.append(xl[b, hf * TL:(hf + 1) * TL])
    return np.ascontiguousarray(np.concatenate(parts, axis=0).T)


def mods_table(mods_l, core, cols):
    b = core // 2
    out = np.zeros((128, 8, 2, len(cols)), np.float32)
    for sid, row in enumerate((4, b)):
        for j, m in enumerate(cols):
            if m is not None:
                out[:, :, sid, j] = mods_l[row, m].reshape(8, 128).T
    return out


def mix_dram_weights(nc, tag=""):
    return {"wgla": nc.dram_tensor(f"wgla{tag}", [D, D], F32, kind="ExternalInput").ap(),
            "why": nc.dram_tensor(f"why{tag}", [D, D], F32, kind="ExternalInput").ap(),
            "wgt": nc.dram_tensor(f"wgt{tag}", [D, 2 * D], F32, kind="ExternalInput").ap(),
            "wout": nc.dram_tensor(f"wout{tag}", [D, D], F32, kind="ExternalInput").ap()}


def emit_mix(nc, c, env, T, segs, src_of, xdst_of, mt, b_m, wd, z_tm_of=None, in_bufs=()):
    blocks = make_blocks(segs)
    in_bufs = list(in_bufs)
    W = {}
    for name, key, ncol in (("gla", "wgla", D), ("hy", "why", D), ("g", "wgt", 2 * D), ("out", "wout", D)):
        t = sb_alloc(nc, f"w_{name}", [128, 8, ncol], BF16)
        bw = Buf()
        for c0 in range(0, ncol, 512):
            c.dma("pool", t[:, :, c0:c0 + 512], wd[key][:, c0:c0 + 512].rearrange("(k p) n -> p k n", p=128), writes=[bw])
        W[name] = (t, bw)
    inb = {n: Rot(nc, f"in_{n}", 2, [128, 8, 512], BF16) for n in ("h", "og", "z")}
    xin = Rot(nc, "in_x", 2, [128, 8, 512], F32)
    mrot = Rot(nc, "mblk", 2, [128, 8, 512], BF16)
    if z_tm_of is not None:
        ztm = Rot(nc, "ztm", 2, [128, D], BF16)
    bo = Buf()
    for bi, (t0, t1, sid) in enumerate(blocks):
        n = t1 - t0
        cur = {}
        for nm in ("h", "og", "z"):
            t, bt = inb[nm].next()
            if nm == "z" and z_tm_of is not None:
                for j in range(0, n, 128):
                    zt, bzt = ztm.next()
                    c.dma("sp", zt[:], z_tm_of(t0 + j, t0 + j + 128), reads=in_bufs, writes=[bzt])
                    for kc in range(8):
                        p, bp = env.ps_d.next()
                        c.op("pe", lambda e, p=p, zt=zt, kc=kc: e.matmul(
                            p[:, 0:128], lhsT=zt[:, kc * 128:(kc + 1) * 128], rhs=env.ident[:], start=True, stop=True),
                            reads=[bzt, env.b_ident], writes=[bp])
                        c.op("act", lambda e, p=p, t=t, kc=kc, j=j: e.copy(out=t[:, kc, j:j + 128], in_=p[:, 0:128]),
                             reads=[bp], writes=[bt])
            else:
                c.dma("sp", t[:, :, 0:n], src_of[nm](t0, t1), reads=in_bufs, writes=[bt])
            cur[nm] = (t, bt)
        xt, bx = xin.next()
        c.dma("sp", xt[:, :, 0:n], src_of["x"](t0, t1), reads=in_bufs, writes=[bx])
        mt_, bmb = mrot.next()
        for dk in range(8):
            acc = []
            for wname, inname, coff in (("gla", "og", 0), ("hy", "z", 0), ("g", "h", 0), ("g", "h", D)):
                ps, bps = env.ps_mm.next()
                wt, bw = W[wname]
                it, ib = cur[inname]
                for k in range(8):
                    c.op("pe", lambda e, ps=ps, wt=wt, it=it, k=k, dk=dk, coff=coff, n=n: e.matmul(
                        ps[:, :n], lhsT=wt[:, k, coff + dk * 128:coff + (dk + 1) * 128], rhs=it[:, k, 0:n],
                        start=(k == 0), stop=(k == 7)), reads=[bw, ib], writes=[bps])
                acc.append((ps, bps))
            prods = []
            for (py, bpy), (pg, bpg) in ((acc[0], acc[2]), (acc[1], acc[3])):
                sg, bsg = env.tmpf.next()
                c.op("act", lambda e, sg=sg, pg=pg, n=n: e.activation(out=sg[:, :n], in_=pg[:, :n], func=AF.Sigmoid),
                     reads=[bpg], writes=[bsg])
                c.op("dve", lambda e, sg=sg, py=py, n=n: e.tensor_tensor(out=sg[:, :n], in0=sg[:, :n], in1=py[:, :n],
                                                                           op=ALU.mult),
                     reads=[bsg, bpy], writes=[bsg])
                prods.append((sg, bsg))
            c.op("pool", lambda e, a=prods[0][0], b2=prods[1][0], mt_=mt_, dk=dk, n=n: e.tensor_tensor(
                out=mt_[:, dk, 0:n], in0=a[:, :n], in1=b2[:, :n], op=ALU.add),
                reads=[prods[0][1], prods[1][1]], writes=[bmb])
        wt, bw = W["out"]
        for dk in range(8):
            ps, bps = env.ps_d.next()
            for k in range(8):
                c.op("pe", lambda e, ps=ps, wt=wt, mt_=mt_, k=k, dk=dk, n=n: e.matmul(
                    ps[:, :n], lhsT=wt[:, k, dk * 128:(dk + 1) * 128], rhs=mt_[:, k, 0:n],
                    start=(k == 0), stop=(k == 7)), reads=[bw, bmb], writes=[bps])
            c.op("dve", lambda e, ps=ps, xt=xt, dk=dk, n=n, sid=sid: e.scalar_tensor_tensor(
                out=xt[:, dk, 0:n], in0=ps[:, :n], scalar=mt[:, dk, sid, 0:1], in1=xt[:, dk, 0:n],
                op0=ALU.mult, op1=ALU.add), reads=[bps, b_m, bx], writes=[bx])
        c.dma("sp", xdst_of(t0, t1), xt[:, :, 0:n], reads=[bx], writes=[bo])
    return bo


def build_mix_prog(T, segs, z_token_major=False):
    nc = bass.Bass("TRN2", target_bir_lowering=False)
    aps = {}
    for nm, dt in (("xT", F32), ("hT", BF16), ("ogT", BF16)):
        aps[nm] = nc.dram_tensor(nm, [D, T], dt, kind="ExternalInput").ap().rearrange("(k p) t -> p k t", p=128)
    ztm = None
    if z_token_major:
        ztm = nc.dram_tensor("zTM", [T, D], BF16, kind="ExternalInput").ap()
        identd = nc.dram_tensor("ident", [128, 128], BF16, kind="ExternalInput").ap()
    else:
        aps["zT"] = nc.dram_tensor("zT", [D, T], BF16, kind="ExternalInput").ap().rearrange("(k p) t -> p k t", p=128)
    mods = nc.dram_tensor("mods", [128, 8, 2, 1], F32, kind="ExternalInput").ap()
    wd = mix_dram_weights(nc)
    xo = nc.dram_tensor("xo", [D, T], F32, kind="ExternalOutput").ap().rearrange("(k p) t -> p k t", p=128)
    c = Ctx(nc)
    env = setup_common(nc, c)
    if z_token_major:
        load_ident(nc, c, env, identd)
    mt = sb_alloc(nc, "mt", [128, 8, 2, 1], F32)
    b_m = Buf()
    c.dma("sp", mt[:], mods, writes=[b_m])
    src_of = {"x": lambda t0, t1: aps["xT"][:, :, t0:t1], "h": lambda t0, t1: aps["hT"][:, :, t0:t1],
              "og": lambda t0, t1: aps["ogT"][:, :, t0:t1]}
    if not z_token_major:
        src_of["z"] = lambda t0, t1: aps["zT"][:, :, t0:t1]
    bo = emit_mix(nc, c, env, T, segs, src_of, lambda t0, t1: xo[:, :, t0:t1], mt, b_m, wd,
                  z_tm_of=(lambda t0, t1: ztm[t0:t1, :]) if z_token_major else None)
    c.wait_all("sp", [bo])
    c.emit()
    return nc


def load_ident(nc, c, env, ident_dram):
    env.ident = sb_alloc(nc, "ident_bf", [128, 128], BF16)
    env.b_ident = Buf()
    c.dma("sp", env.ident[:], ident_dram, writes=[env.b_ident])


TF = CTX + SEQ
NT = TF // 128
GLA_DK = 128
GLA_DV = 256


def gla_consts():
    j = np.arange(128)[:, None]
    i = np.arange(128)[None, :]
    same = (j // 64) == (i // 64)
    s = -1.0 / 16.0
    cs = {
        "tri_inc_f": (same & (j <= i)) * s, "tri_exc_f": (same & (j > i)) * s,
        "tri_inc_b": (same & (j >= i)) * s, "tri_exc_b": (same & (j < i)) * s,
        "mask_f": (same & (j <= i)) * 1.0, "mask_b": (same & (j >= i)) * 1.0,
        "colm0": np.broadcast_to(i < 64, (128, 128)) * 1.0, "colm1": np.broadcast_to(i >= 64, (128, 128)) * 1.0,
        "rowm0": np.broadcast_to(j < 64, (128, 128)) * 1.0, "rowm1": np.broadcast_to(j >= 64, (128, 128)) * 1.0,
    }
    return {k: np.ascontiguousarray(v.astype(np.float32)) for k, v in cs.items()}


def gla_dram_inputs(nc, hT=None, tag="", shared=None):
    d = {}
    d["hT"] = hT if hT is not None else nc.dram_tensor("hT", [D, TF], BF16, kind="ExternalInput").ap()
    for nm, shp in (("wk", [D, 256]), ("wq", [D, 256]), ("wv", [D, 512]), ("wr", [D, 512]), ("wa", [D, 32]),
                    ("waf", [32, 256]), ("wab", [32, 256]), ("gng", [128, 4])):
        d[nm] = nc.dram_tensor(nm + tag, shp, F32, kind="ExternalInput").ap()
    if shared is not None and "atmpl" in shared:
        for n in ("atmpl",) + GLA_CNAMES:
            d[n] = shared[n]
    else:
        d["atmpl"] = nc.dram_tensor("atmpl", [32, TF], F32, kind="ExternalInput").ap()
        for n in GLA_CNAMES:
            d[n] = nc.dram_tensor(n, [128, 128], F32, kind="ExternalInput").ap()
        if shared is not None:
            for n in ("atmpl",) + GLA_CNAMES:
                shared[n] = d[n]
    return d


GLA_CNAMES = ("tri_inc_f", "tri_exc_f", "tri_inc_b", "tri_exc_b", "mask_f", "mask_b",
              "colm0", "colm1", "rowm0", "rowm1")


def emit_gla(nc, c, psum, din, ogT, b_out):
    hT, wk, wq, wv, wr, wa, waf, wab, atmpl, gng = (din[k] for k in (
        "hT", "wk", "wq", "wv", "wr", "wa", "waf", "wab", "atmpl", "gng"))
    cnames = GLA_CNAMES
    cdram = {n: din[n] for n in cnames}
    oS = nc.dram_tensor(c.uid("oS"), [2, 512, TF], F32).ap()
    ones = sb_alloc(nc, "ones_bf", [128, 128], BF16)
    b_ones = Buf()
    c.op("pool", lambda e: e.memset(ones[:], 1.0), writes=[b_ones])
    onec = sb_alloc(nc, "onec", [128, 1], F32)
    epsc = sb_alloc(nc, "epsc", [128, 1], F32)
    b_cc = Buf()
    c.op("pool", lambda e: e.memset(onec[:], 1.0), writes=[b_cc])
    c.op("pool", lambda e: e.memset(epsc[:], EPS), writes=[b_cc])
    H = sb_alloc(nc, "H", [128, 8, TF], BF16)
    bH = [Buf() for _ in range(NT)]
    hsrc = hT.rearrange("(k p) t -> p k t", p=128)
    for g in range(0, NT, 4):
        n = min(4, NT - g)
        c.dma("sp", H[:, :, g * 128:(g + n) * 128], hsrc[:, :, g * 128:(g + n) * 128], writes=bH[g:g + n])
    Wt = {}
    for name, ap, ncol in (("k", wk, 256), ("q", wq, 256), ("v", wv, 512), ("r", wr, 512), ("a", wa, 32)):
        t = sb_alloc(nc, f"w_{name}", [128, 8, ncol], BF16)
        b = Buf()
        c.dma("pool", t[:], ap.rearrange("(k p) n -> p k n", p=128), writes=[b])
        Wt[name] = (t, b)
    waug = {}
    for name, ap in (("f", waf), ("b", wab)):
        t = sb_alloc(nc, f"waug_{name}", [32, 256], F32)
        b = Buf()
        c.dma("sp", t[:], ap, writes=[b])
        waug[name] = (t, b)
    gt = sb_alloc(nc, "gngt", [128, 4], F32)
    b_g = Buf()
    c.dma("sp", gt[:], gng, writes=[b_g])
    C = {}
    for n in cnames:
        t = sb_alloc(nc, f"c_{n}", [128, 128], F32)
        b = Buf()
        c.dma("sp", t[:], cdram[n], writes=[b])
        C[n] = (t, b)
    aX = {}
    for name in ("f", "b"):
        t = sb_alloc(nc, f"a_{name}", [32, TF], F32)
        bl = [Buf() for _ in range(NT)]
        c.dma("sp", t[:], atmpl, writes=bl)
        aX[name] = (t, bl)
    psA = RotSlice(psum, [7])
    ps = PsumHalves(psum, 7)
    wa_t, wa_b = Wt["a"]
    for t0 in range(0, TF, 512):
        n = min(512, TF - t0)
        tiles = list(range(t0 // 128, (t0 + n) // 128))
        for di, name in enumerate(("f", "b")):
            p, bp = psA.next()
            for k in range(8):
                c.op("pe", lambda e, p=p, k=k, di=di, t0=t0, n=n: e.matmul(
                    p[0:16, :n], lhsT=wa_t[:, k, di * 16:(di + 1) * 16], rhs=H[:, k, t0:t0 + n],
                    start=(k == 0), stop=(k == 7)), reads=[wa_b] + [bH[i] for i in tiles], writes=[bp])
            at, abl = aX[name]
            c.op("act", lambda e, p=p, at=at, t0=t0, n=n: e.copy(out=at[0:16, t0:t0 + n], in_=p[0:16, :n]),
                 reads=[bp], writes=[abl[i] for i in tiles])
    def rot(name, n, shape, dt):
        return Rot(nc, name, n, shape, dt)
    r_f32 = rot("rf", 24, [128, 128], F32)
    r_bf = rot("rb", 48, [128, 128], BF16)
    r_v = rot("rv", 8, [128, 256], BF16)
    r_o = rot("ro", 8, [128, 2, 128], F32)
    chains = []
    for head in range(2):
        for di, dname in enumerate(("f", "b")):
            S = sb_alloc(nc, f"S_{head}{dname}", [128, 256], F32)
            Sb = [sb_alloc(nc, f"Sb{i}_{head}{dname}", [128, 256], BF16) for i in range(3)]
            bS = Buf()
            bSb = [Buf(), Buf(), Buf()]
            c.op("pool", lambda e, S=S: e.memset(S[:], 0.0), writes=[bS])
            c.op("pool", lambda e, t=Sb[0]: e.memset(t[:], 0.0), writes=[bSb[0]])
            order = list(range(NT)) if di == 0 else [1, 0] + list(range(NT - 1, 1, -1))
            chains.append(dict(head=head, di=di, dname=dname, S=S, Sb=Sb, bS=bS, bSb=bSb, cur=0, order=order))
    osd = [oS[di].rearrange("(c p) t -> p c t", p=128) for di in range(2)]
    b_oS = [[[Buf() for _ in range(NT)] for _ in range(2)] for _ in range(2)]
    scale = float(GLA_DK) ** -0.5
    for step in range(NT):
        for ch in chains:
            head, di, dname = ch["head"], ch["di"], ch["dname"]
            ti = ch["order"][step]
            t0 = ti * 128
            tri_inc, b_ti = C[f"tri_inc_{dname}"]
            tri_exc, b_te = C[f"tri_exc_{dname}"]
            mask, b_mk = C[f"mask_{dname}"]
            wkt, bwk = Wt["k"]
            wqt, bwq = Wt["q"]
            wvt, bwv = Wt["v"]
            hb = bH[ti]
            p_kT, b_kT = ps.next()
            p_qT, b_qT = ps.next()
            p_ktm, b_ktm = ps.next()
            p_vtm, b_vtm = ps.next()
            for k in range(8):
                c.op("pe", lambda e, p=p_kT, k=k, head=head, t0=t0: e.matmul(
                    p[:, 0:128], lhsT=wkt[:, k, head * 128:(head + 1) * 128], rhs=H[:, k, t0:t0 + 128],
                    start=(k == 0), stop=(k == 7)), reads=[bwk, hb], writes=[b_kT])
            for k in range(8):
                c.op("pe", lambda e, p=p_qT, k=k, head=head, t0=t0: e.matmul(
                    p[:, 0:128], lhsT=wqt[:, k, head * 128:(head + 1) * 128], rhs=H[:, k, t0:t0 + 128],
                    start=(k == 0), stop=(k == 7)), reads=[bwq, hb], writes=[b_qT])
            for k in range(8):
                c.op("pe", lambda e, p=p_ktm, k=k, head=head, t0=t0: e.matmul(
                    p[:, 0:128], lhsT=H[:, k, t0:t0 + 128], rhs=wkt[:, k, head * 128:(head + 1) * 128],
                    start=(k == 0), stop=(k == 7)), reads=[bwk, hb], writes=[b_ktm])
            for k in range(8):
                c.op("pe", lambda e, p=p_vtm, k=k, head=head, t0=t0: e.matmul(
                    p[:, 0:256], lhsT=H[:, k, t0:t0 + 128], rhs=wvt[:, k, head * 256:(head + 1) * 256],
                    start=(k == 0), stop=(k == 7)), reads=[bwv, hb], writes=[b_vtm])
            at, abl = aX[dname]
            wg_t, wg_b = waug[dname]
            p_x, b_x = ps.next()
            c.op("pe", lambda e, p=p_x, at=at, wg_t=wg_t, head=head, t0=t0: e.matmul(
                p[:, 0:128], lhsT=at[0:32, t0:t0 + 128], rhs=wg_t[0:32, head * 128:(head + 1) * 128],
                start=True, stop=True), reads=[abl[ti], wg_b], writes=[b_x])
            ex, b_ex = r_f32.next()
            c.op("act", lambda e, ex=ex, p=p_x: e.activation(out=ex[:], in_=p[:, 0:128], func=AF.Exp, scale=-1.0),
                 reads=[b_x], writes=[b_ex])
            sp, b_sp = r_f32.next()
            c.op("act", lambda e, ex=ex, sp=sp: e.activation(out=sp[:], in_=ex[:], func=AF.Ln, bias=onec[:, 0:1],
                                                              scale=1.0), reads=[b_ex, b_cc], writes=[b_sp])
            p_bT, b_bT = ps.next()
            c.op("pe", lambda e, p=p_bT, sp=sp, tri_inc=tri_inc: e.matmul(p[:, 0:128], lhsT=sp[:], rhs=tri_inc[:],
                                                                           start=True, stop=True),
                 reads=[b_sp, b_ti], writes=[b_bT])
            p_de, b_de = ps.next()
            c.op("pe", lambda e, p=p_de, sp=sp, tri_exc=tri_exc: e.matmul(p[:, 0:128], lhsT=tri_exc[:], rhs=sp[:],
                                                                           start=True, stop=True),
                 reads=[b_sp, b_te], writes=[b_de])
            E1, b_E1 = r_f32.next()
            E2, b_E2 = r_f32.next()
            E3, b_E3 = r_f32.next()
            c.op("act", lambda e, E1=E1, p=p_bT: e.activation(out=E1[:], in_=p[:, 0:128], func=AF.Exp),
                 reads=[b_bT], writes=[b_E1])
            c.op("act", lambda e, E2=E2, p=p_bT: e.activation(out=E2[:], in_=p[:, 0:128], func=AF.Exp, scale=-1.0),
                 reads=[b_bT], writes=[b_E2])
            c.op("act", lambda e, E3=E3, p=p_de: e.activation(out=E3[:], in_=p[:, 0:128], func=AF.Exp),
                 reads=[b_de], writes=[b_E3])
            qd, b_qd = r_bf.next()
            ki, b_ki = r_bf.next()
            ke, b_ke = r_bf.next()
            c.op("dve", lambda e, qd=qd, p=p_qT, E1=E1: e.scalar_tensor_tensor(
                out=qd[:], in0=p[:, 0:128], scalar=scale, in1=E1[:], op0=ALU.mult, op1=ALU.mult),
                reads=[b_qT, b_E1], writes=[b_qd])
            c.op("dve", lambda e, ki=ki, p=p_kT, E2=E2: e.tensor_tensor(out=ki[:], in0=p[:, 0:128], in1=E2[:],
                                                                        op=ALU.mult),
                 reads=[b_kT, b_E2], writes=[b_ki])
            c.op("dve", lambda e, ke=ke, p=p_ktm, E3=E3: e.tensor_tensor(out=ke[:], in0=p[:, 0:128], in1=E3[:],
                                                                         op=ALU.mult),
                 reads=[b_ktm, b_E3], writes=[b_ke])
            vs, b_vs = r_v.next()
            c.op("act", lambda e, vs=vs, p=p_vtm: e.copy(out=vs[:], in_=p[:, 0:256]), reads=[b_vtm], writes=[b_vs])
            p_at, b_at = ps.next()
            c.op("pe", lambda e, p=p_at, ki=ki, qd=qd: e.matmul(p[:, 0:128], lhsT=ki[:], rhs=qd[:], start=True, stop=True),
                 reads=[b_ki, b_qd], writes=[b_at])
            am, b_am = r_bf.next()
            c.op("dve", lambda e, am=am, p=p_at, mask=mask: e.tensor_tensor(out=am[:], in0=p[:, 0:128], in1=mask[:],
                                                                            op=ALU.mult),
                 reads=[b_at, b_mk], writes=[b_am])
            corder = (0, 1) if di == 0 else (1, 0)
            S, Sb, bS, bSb = ch["S"], ch["Sb"], ch["bS"], ch["bSb"]
            cur = ch["cur"]
            kec, qdc = {}, {}
            for c2 in (0, 1):
                rm, b_rm = C[f"rowm{c2}"]
                cm, b_cm = C[f"colm{c2}"]
                t_ke, b_tke = r_bf.next()
                c.op("dve", lambda e, t=t_ke, ke=ke, rm=rm: e.tensor_tensor(out=t[:], in0=ke[:], in1=rm[:], op=ALU.mult),
                     reads=[b_ke, b_rm], writes=[b_tke])
                t_qd, b_tqd = r_bf.next()
                c.op("dve", lambda e, t=t_qd, qd=qd, cm=cm: e.tensor_tensor(out=t[:], in0=qd[:], in1=cm[:], op=ALU.mult),
                     reads=[b_qd, b_cm], writes=[b_tqd])
                kec[c2] = (t_ke, b_tke)
                qdc[c2] = (t_qd, b_tqd)
            st_bf = []
            for ci, c2 in enumerate(corder):
                st_bf.append((Sb[cur], bSb[cur]))
                p_dS, b_dS = ps.next()
                t_ke, b_tke = kec[c2]
                c.op("pe", lambda e, p=p_dS, t_ke=t_ke, vs=vs: e.matmul(
                    p[:, 0:256], lhsT=t_ke[:], rhs=vs[:], start=True, stop=True),
                    reads=[b_tke, b_vs], writes=[b_dS])
                lo = c2 * 64
                col = lo + 63 if di == 0 else lo
                c.op("dve", lambda e, S=S, p=p_dS, E1=E1, col=col: e.scalar_tensor_tensor(
                    out=S[:], in0=S[:], scalar=E1[:, col:col + 1], in1=p[:, 0:256], op0=ALU.mult, op1=ALU.add),
                    reads=[bS, b_E1, b_dS], writes=[bS])
                nxt = (cur + 1) % 3
                c.op("act", lambda e, t=Sb[nxt], S=S: e.copy(out=t[:], in_=S[:]), reads=[bS], writes=[bSb[nxt]])
                cur = nxt
            ch["cur"] = cur
            ot, b_ot = r_o.next()
            for dvc in range(2):
                p_o, b_o = ps.next()
                c.op("pe", lambda e, p=p_o, vs=vs, am=am, dvc=dvc: e.matmul(
                    p[:, 0:128], lhsT=vs[:, dvc * 128:(dvc + 1) * 128], rhs=am[:], start=True, stop=False),
                    reads=[b_vs, b_am], writes=[b_o])
                for ci, c2 in enumerate(corder):
                    sbt, sbb = st_bf[ci]
                    t_qd, b_tqd = qdc[c2]
                    c.op("pe", lambda e, p=p_o, sbt=sbt, t_qd=t_qd, dvc=dvc, ci=ci: e.matmul(
                        p[:, 0:128], lhsT=sbt[:, dvc * 128:(dvc + 1) * 128], rhs=t_qd[:],
                        start=False, stop=(ci == 1)), reads=[sbb, b_tqd], writes=[b_o])
                c.op("act", lambda e, ot=ot, p=p_o, dvc=dvc: e.copy(out=ot[:, dvc, :], in_=p[:, 0:128]),
                     reads=[b_o], writes=[b_ot])
            c.dma("st", osd[di][:, head * 2:head * 2 + 2, t0:t0 + 128], ot[:], reads=[b_ot],
                  writes=[b_oS[di][head][ti]])
    r_in = rot("rin", 6, [128, 2, 128], F32)
    r_og = rot("rog", 3, [128, 4, 128], BF16)
    r_sq = rot("rsq", 4, [128, 128], BF16)
    wrt, bwr = Wt["r"]
    ogd = ogT.rearrange("(c p) t -> p c t", p=128)
    for ti in range(NT):
        t0 = ti * 128
        og, b_og = r_og.next()
        for head in range(2):
            of, b_of = r_in.next()
            ob, b_ob = r_in.next()
            c.dma("sp", of[:], osd[0][:, head * 2:head * 2 + 2, t0:t0 + 128], reads=[b_oS[0][head][ti]], writes=[b_of])
            c.dma("sp", ob[:], osd[1][:, head * 2:head * 2 + 2, t0:t0 + 128], reads=[b_oS[1][head][ti]], writes=[b_ob])
            c.op("pool", lambda e, of=of, ob=ob: e.tensor_tensor(out=of[:], in0=of[:], in1=ob[:], op=ALU.add),
                 reads=[b_of, b_ob], writes=[b_of])
            p_ss, b_ss = ps.next()
            for dvc in range(2):
                sq, b_sq = r_sq.next()
                c.op("act", lambda e, sq=sq, of=of, dvc=dvc: e.activation(out=sq[:], in_=of[:, dvc, :], func=AF.Square),
                     reads=[b_of], writes=[b_sq])
                c.op("pe", lambda e, p=p_ss, sq=sq, dvc=dvc: e.matmul(p[:, 0:128], lhsT=ones[:], rhs=sq[:],
                                                                       start=(dvc == 0), stop=(dvc == 1)),
                     reads=[b_sq, b_ones], writes=[b_ss])
            ln, b_ln = r_f32.next()
            c.op("act", lambda e, ln=ln, p=p_ss: e.activation(out=ln[:], in_=p[:, 0:128], func=AF.Ln,
                                                               bias=epsc[:, 0:1], scale=1.0 / GLA_DV),
                 reads=[b_ss, b_cc], writes=[b_ln])
            rs, b_rs = r_f32.next()
            c.op("act", lambda e, ln=ln, rs=rs: e.activation(out=rs[:], in_=ln[:], func=AF.Exp, scale=-0.5),
                 reads=[b_ln], writes=[b_rs])
            for dvc in range(2):
                cidx = head * 2 + dvc
                p_r, b_r = ps.next()
                for k in range(8):
                    c.op("pe", lambda e, p=p_r, k=k, cidx=cidx, t0=t0: e.matmul(
                        p[:, 0:128], lhsT=wrt[:, k, cidx * 128:(cidx + 1) * 128], rhs=H[:, k, t0:t0 + 128],
                        start=(k == 0), stop=(k == 7)), reads=[bwr, bH[ti]], writes=[b_r])
                sr, b_sr = r_f32.next()
                c.op("act", lambda e, sr=sr, p=p_r: e.activation(out=sr[:], in_=p[:, 0:128], func=AF.Silu),
                     reads=[b_r], writes=[b_sr])
                tt, b_tt = r_f32.next()
                c.op("dve", lambda e, tt=tt, of=of, rs=rs, dvc=dvc, cidx=cidx: e.scalar_tensor_tensor(
                    out=tt[:], in0=of[:, dvc, :], scalar=gt[:, cidx:cidx + 1], in1=rs[:], op0=ALU.mult, op1=ALU.mult),
                    reads=[b_of, b_rs, b_g], writes=[b_tt])
                c.op("dve", lambda e, tt=tt, sr=sr, og=og, cidx=cidx: e.tensor_tensor(
                    out=og[:, cidx, :], in0=tt[:], in1=sr[:], op=ALU.mult), reads=[b_tt, b_sr], writes=[b_og])
        c.dma("sp", ogd[:, :, t0:t0 + 128], og[:], reads=[b_og], writes=[b_out])


def build_gla_prog():
    nc = bass.Bass("TRN2", target_bir_lowering=False)
    c = Ctx(nc)
    psum = make_psum(nc)
    din = gla_dram_inputs(nc)
    ogT = nc.dram_tensor("ogT", [512, TF], BF16, kind="ExternalOutput").ap()
    b_out = Buf()
    emit_gla(nc, c, psum, din, ogT, b_out)
    c.wait_all("sp", [b_out])
    c.emit()
    return nc


OFF_K, OFF_V, OFF_A, OFF_Q, OFF_R, OFF_H, OFF_G = 0, 512, 1536, 1568, 2080, 3104, 6176


def gla_inputs_for_core(inp, l, core, hT_full):
    b, hf = divmod(core, 2)
    w = inp["w_in"][l]
    heads = (2 * hf, 2 * hf + 1)
    def cols(off, width):
        return np.ascontiguousarray(np.concatenate([w[:, off + h * width: off + (h + 1) * width] for h in heads], axis=1))
    m = {"wk": cols(OFF_K, 128), "wq": cols(OFF_Q, 128), "wv": cols(OFF_V, 256),
         "wr": cols(OFF_R, 256), "wa": np.ascontiguousarray(w[:, OFF_A:OFF_A + 32])}
    for nm, wa_, ba_ in (("waf", inp["gla_wa_f"][l], inp["gla_ba_f"][l]), ("wab", inp["gla_wa_b"][l], inp["gla_ba_b"][l])):
        aug = np.concatenate([wa_, ba_[None, :], np.zeros((15, wa_.shape[1]), np.float32)], axis=0)
        m[nm] = np.ascontiguousarray(np.concatenate([aug[:, h * 128:(h + 1) * 128] for h in heads], axis=1))
    if hT_full is not None:
        m["hT"] = hT_full
    g = inp["gla_norm_g"][l][hf * 512:(hf + 1) * 512]
    m["gng"] = np.ascontiguousarray(g.reshape(4, 128).T)
    tm = np.zeros((32, TF), np.float32)
    tm[16] = 1.0
    m["atmpl"] = tm
    m.update(gla_consts())
    return m


NK1 = 65
HC = 512


def fft_consts(NC):
    N = 128 * NC
    n1 = np.arange(128)[:, None, None]
    n2 = np.arange(NC)[None, :, None]
    k1 = np.arange(NK1)[None, None, :]
    ang = 2 * np.pi * ((NC * n1 + n2) * k1 % N) / N
    DAr, DAi = np.cos(ang), -np.sin(ang)
    KB = max(32, NC)
    a = np.arange(NC)
    angb = 2 * np.pi * (np.outer(a, a) % NC) / NC
    WBr = np.zeros((KB, NC)); WBi = np.zeros((KB, NC))
    WBr[:NC], WBi[:NC] = np.cos(angb), -np.sin(angb)
    wgt = np.full(NK1, 2.0); wgt[0] = 1.0; wgt[64] = 1.0
    kk = np.arange(NK1)[:, None, None]
    m2 = np.arange(NC)[None, :, None]
    m1 = np.arange(64)[None, None, :]
    th = 2 * np.pi * (kk * (NC * m1 + m2) % N) / N
    Er = np.zeros((96, NC, 64)); Ei = np.zeros((96, NC, 64))
    Er[:NK1] = wgt[:, None, None] * np.cos(th) / N
    Ei[:NK1] = -wgt[:, None, None] * np.sin(th) / N
    bf = ml_dtypes.bfloat16
    return {"DAr": DAr.astype(np.float32).astype(bf), "DAi": DAi.astype(np.float32).astype(bf),
            "WBr": WBr.astype(np.float32).astype(bf), "WBi": WBi.astype(np.float32).astype(bf),
            "WBin": (-WBi).astype(np.float32).astype(bf),
            "Er": Er.astype(np.float32).astype(bf), "Ei": Ei.astype(np.float32).astype(bf), "KB": KB}


_CONST_DRAM = {}


class FFTConv:
    def __init__(self, nc, c, name, NC, rows_A, psum, C=HC, share=None, uniq=""):
        self.nc, self.c, self.name, self.NC, self.C = nc, c, name, NC, C
        self.uniq = uniq
        self.rows_A = rows_A
        self.ps = psum
        cs = fft_consts(NC)
        self.KB = cs["KB"]
        self.host = {f"{name}_{k}": v for k, v in cs.items() if k != "KB"}
        self.host[f"{name}_DAr"] = np.ascontiguousarray(cs["DAr"][:rows_A])
        self.host[f"{name}_DAi"] = np.ascontiguousarray(cs["DAi"][:rows_A])
        shapes = {"DAr": [rows_A, NC, NK1], "DAi": [rows_A, NC, NK1], "WBr": [self.KB, NC], "WBi": [self.KB, NC],
                  "WBin": [self.KB, NC], "Er": [96, NC, 64], "Ei": [96, NC, 64]}
        self.T = {}
        for k, shp in shapes.items():
            key = (id(nc), f"{name}_{k}")
            if key not in _CONST_DRAM:
                _CONST_DRAM[key] = nc.dram_tensor(f"{name}_{k}", shp, BF16, kind="ExternalInput").ap()
            d = _CONST_DRAM[key]
            t = sb_alloc(nc, f"t_{uniq}{name}_{k}", shp, BF16)
            b = Buf()
            c.dma("sp", t[:], d, writes=[b])
            self.T[k] = (t, b)
        if share is not None:
            self.ev, self.ld, self.lda, self.xf, self.yb, self.tf = (
                share.ev, share.ld, share.lda, share.xf, share.yb, share.tf)
        else:
            self.ev = Rot(nc, f"{name}_ev", 4, [128, C], BF16)
            self.ld = Rot(nc, f"{name}_ld", 4, [96, C], BF16)
            self.lda = Rot(nc, f"{name}_lda", 4, [64, C], BF16)
            self.xf = Rot(nc, f"{name}_xf", 2, [64, C], F32)
            self.yb = Rot(nc, f"{name}_yb", 4, [64, C], BF16)
            self.tf = Rot(nc, f"{name}_tf", 4, [64, C], F32)
        for rot_ in (self.ld, self.lda, self.yb):
            for t, b in rot_.tiles:
                c.op("pool", lambda e, t=t: e.memset(t[:], 0.0), writes=[b])
        self.flip = 0

    def scratch(self, tag):
        NC, C = self.NC, self.C
        As = self.nc.dram_tensor(f"{self.uniq}{self.name}_As_{tag}", [2, NK1, NC, C], BF16).ap()
        Cs = self.nc.dram_tensor(f"{self.uniq}{self.name}_Cs_{tag}", [2, NC, NK1, C], BF16).ap()
        return dict(As=As, Cs=Cs, bA=Buf(), bC=Buf())

    def _evac(self, dst, src, reads, writes):
        eng = "act"
        self.flip += 1
        if eng == "act":
            self.c.op("act", lambda e: e.copy(out=dst, in_=src), reads=reads, writes=writes)
        else:
            self.c.op("dve", lambda e: e.tensor_copy(out=dst, in_=src), reads=reads, writes=writes)

    def stage_a(self, sc, n2, z_ap, z_buf):
        c = self.c
        for ri, key in enumerate(("DAr", "DAi")):
            tb, bt = self.T[key]
            p, bp = self.ps.next()
            c.op("pe", lambda e, p=p, tb=tb, n2=n2: e.matmul(p[0:NK1, 0:self.C], lhsT=tb[:, n2, :], rhs=z_ap,
                                                            start=True, stop=True), reads=[bt, z_buf], writes=[bp])
            ev, bev = self.ev.next()
            self._evac(ev[0:NK1, :], p[0:NK1, 0:self.C], [bp], [bev])
            c.dma("st", sc["As"][ri, :, n2, :], ev[0:NK1, :], reads=[bev], writes=[sc["bA"]])

    def stage_b(self, sc, k1, h_re, h_im, h_buf, spectrum_out=None, scale_tab=None):
        c, NC, C, KB = self.c, self.NC, self.C, self.KB
        ar, bar = self.lda.next()
        ai, bai = self.lda.next()
        c.dma("sp", ar[0:NC, :], sc["As"][0, k1, :, :], reads=[sc["bA"]], writes=[bar])
        c.dma("sp", ai[0:NC, :], sc["As"][1, k1, :, :], reads=[sc["bA"]], writes=[bai])
        wr, bwr = self.T["WBr"]
        wi, bwi = self.T["WBi"]
        wn, bwn = self.T["WBin"]
        pxr, bxr = self.ps.next()
        pxi, bxi = self.ps.next()
        c.op("pe", lambda e: e.matmul(pxr[0:NC, 0:C], lhsT=wr[:], rhs=ar[0:KB, :], start=True, stop=False),
             reads=[bwr, bar], writes=[bxr])
        c.op("pe", lambda e: e.matmul(pxr[0:NC, 0:C], lhsT=wn[:], rhs=ai[0:KB, :], start=False, stop=True),
             reads=[bwn, bai], writes=[bxr])
        c.op("pe", lambda e: e.matmul(pxi[0:NC, 0:C], lhsT=wi[:], rhs=ar[0:KB, :], start=True, stop=False),
             reads=[bwi, bar], writes=[bxi])
        c.op("pe", lambda e: e.matmul(pxi[0:NC, 0:C], lhsT=wr[:], rhs=ai[0:KB, :], start=False, stop=True),
             reads=[bwr, bai], writes=[bxi])
        if spectrum_out is not None:
            tab, btab = scale_tab
            for ri, (p, bp) in enumerate(((pxr, bxr), (pxi, bxi))):
                t, bt = self.tf.next()
                c.op("dve", lambda e, t=t, p=p: e.tensor_tensor(out=t[0:NC, :], in0=p[0:NC, 0:C], in1=tab[0:NC, :],
                                                                 op=ALU.mult), reads=[bp, btab], writes=[bt])
                c.dma("sp", spectrum_out[ri, k1, :, :], t[0:NC, :], reads=[bt], writes=[sc["bH"]])
            return
        hbr, hbi = (h_buf, h_buf) if h_buf is not None else self._hb
        xr, bxr2 = self.xf.next()
        xi, bxi2 = self.xf.next()
        c.op("act", lambda e: e.copy(out=xr[0:NC, :], in_=pxr[0:NC, 0:C]), reads=[bxr], writes=[bxr2])
        c.op("act", lambda e: e.copy(out=xi[0:NC, :], in_=pxi[0:NC, 0:C]), reads=[bxi], writes=[bxi2])
        t1, b1 = self.tf.next()
        t2, b2 = self.tf.next()
        yr, byr = self.yb.next()
        yi, byi = self.yb.next()
        c.op("dve", lambda e: e.tensor_tensor(out=t1[0:NC, :], in0=xr[0:NC, :], in1=h_re, op=ALU.mult),
             reads=[bxr2, hbr], writes=[b1])
        c.op("dve", lambda e: e.tensor_tensor(out=t2[0:NC, :], in0=xi[0:NC, :], in1=h_im, op=ALU.mult),
             reads=[bxi2, hbi], writes=[b2])
        c.op("dve", lambda e: e.tensor_tensor(out=yr[0:NC, :], in0=t1[0:NC, :], in1=t2[0:NC, :], op=ALU.subtract),
             reads=[b1, b2], writes=[byr])
        t3, b3 = self.tf.next()
        t4, b4 = self.tf.next()
        c.op("pool", lambda e: e.tensor_tensor(out=t3[0:NC, :], in0=xr[0:NC, :], in1=h_im, op=ALU.mult),
             reads=[bxr2, hbi], writes=[b3])
        c.op("dve", lambda e: e.tensor_tensor(out=t4[0:NC, :], in0=xi[0:NC, :], in1=h_re, op=ALU.mult),
             reads=[bxi2, hbr], writes=[b4])
        c.op("dve", lambda e: e.tensor_tensor(out=yi[0:NC, :], in0=t3[0:NC, :], in1=t4[0:NC, :], op=ALU.add),
             reads=[b3, b4], writes=[byi])
        if KB > NC:
            pass
        pcr, bcr = self.ps.next()
        pci, bci = self.ps.next()
        c.op("pe", lambda e: e.matmul(pcr[0:NC, 0:C], lhsT=wr[:], rhs=yr[0:KB, :], start=True, stop=False),
             reads=[bwr, byr], writes=[bcr])
        c.op("pe", lambda e: e.matmul(pcr[0:NC, 0:C], lhsT=wi[:], rhs=yi[0:KB, :], start=False, stop=True),
             reads=[bwi, byi], writes=[bcr])
        c.op("pe", lambda e: e.matmul(pci[0:NC, 0:C], lhsT=wr[:], rhs=yi[0:KB, :], start=True, stop=False),
             reads=[bwr, byi], writes=[bci])
        c.op("pe", lambda e: e.matmul(pci[0:NC, 0:C], lhsT=wn[:], rhs=yr[0:KB, :], start=False, stop=True),
             reads=[bwn, byr], writes=[bci])
        for ri, (p, bp) in enumerate(((pcr, bcr), (pci, bci))):
            ev, bev = self.ev.next()
            self._evac(ev[0:NC, :], p[0:NC, 0:C], [bp], [bev])
            c.dma("st", sc["Cs"][ri, :, k1, :], ev[0:NC, :], reads=[bev], writes=[sc["bC"]])

    def stage_b_h(self, sc, k1, hr, bhr, hi, bhi):
        NC = self.NC
        self._hb = (bhr, bhi)
        self.stage_b(sc, k1, hr[0:NC, :], hi[0:NC, :], None)

    def stage_ainv(self, sc, m2):
        c = self.c
        cr, bcr = self.ld.next()
        ci, bci = self.ld.next()
        c.dma("sp", cr[0:NK1, :], sc["Cs"][0, m2, :, :], reads=[sc["bC"]], writes=[bcr])
        c.dma("sp", ci[0:NK1, :], sc["Cs"][1, m2, :, :], reads=[sc["bC"]], writes=[bci])
        er, ber = self.T["Er"]
        ei, bei = self.T["Ei"]
        p, bp = self.ps.next()
        c.op("pe", lambda e: e.matmul(p[0:64, 0:self.C], lhsT=er[:, m2, :], rhs=cr[:], start=True, stop=False),
             reads=[ber, bcr], writes=[bp])
        c.op("pe", lambda e: e.matmul(p[0:64, 0:self.C], lhsT=ei[:, m2, :], rhs=ci[:], start=False, stop=True),
             reads=[bei, bci], writes=[bp])
        return p, bp


def filt_consts(L, NC):
    N = 2 * L
    bands = 16
    p = np.arange(N)
    d = np.where(p < L, p, N - p).astype(np.float64)
    valid = (p != L)
    t = d / (L - 1)
    w = (2.0 * np.pi / L) * d
    f = np.linspace(1e-4, bands - 1, bands)
    feat = np.concatenate([t[:, None], np.cos(f[None, :] * w[:, None]), -np.sin(f[None, :] * w[:, None])], axis=1)
    feat = feat * valid[:, None]
    featT = np.zeros((64, N), np.float32)
    featT[:33] = feat.T
    tpos = t.reshape(128, NC).astype(np.float32)
    pmask = valid.reshape(128, NC).astype(np.float32)
    return featT, tpos, pmask


def filt_dram_inputs(nc, tg, L, NC, shared=None, wtag="", pos=None):
    N = 2 * L
    C = HC
    d = {}
    for nm, shp in (("featT", [64, N]), ("tpos", [128, NC]), ("pmask", [128, NC])):
        key = nm + tg
        if pos is not None and key in pos:
            d[nm] = pos[key]
        else:
            d[nm] = nc.dram_tensor(key, shp, F32, kind="ExternalInput").ap()
            if pos is not None:
                pos[key] = d[nm]
    if shared is None:
        d["w1"] = nc.dram_tensor("w1" + wtag, [64, 64], F32, kind="ExternalInput").ap()
        d["w2p"] = nc.dram_tensor("w2p" + wtag, [2, 64, 128], F32, kind="ExternalInput").ap()
        d["pr1"] = nc.dram_tensor("pr1" + wtag, [64, 2], F32, kind="ExternalInput").ap()
        d["pr2"] = nc.dram_tensor("pr2" + wtag, [2, 128, 3], F32, kind="ExternalInput").ap()
        d["w3s"] = nc.dram_tensor("w3s" + wtag, [2, 128, C], F32, kind="ExternalInput").ap()
        d["dlt"] = nc.dram_tensor("dlt" + wtag, [128, C], F32, kind="ExternalInput").ap()
    else:
        for k in ("w1", "w2p", "pr1", "pr2", "w3s", "dlt"):
            d[k] = shared[k]
    return d


def emit_filter(nc, c, psum, fft, tg, L, NC, din, Hout, bH):
    N = 2 * L
    C = HC
    BS = min(512, L)
    NBLK = N // BS
    featT, tpos, pmask, w1, w2p, pr1, pr2, w3s, dlt = (din[k] for k in (
        "featT", "tpos", "pmask", "w1", "w2p", "pr1", "pr2", "w3s", "dlt"))
    def load(name, ap, shape, dt=F32):
        t = sb_alloc(nc, name + tg, shape, dt)
        b = Buf()
        c.dma("sp", t[:], ap, writes=[b])
        return t, b
    ft, bft = load("featT_t", featT, [64, N])
    tp, btp = load("tpos_t", tpos, [128, NC])
    pm, bpm = load("pmask_t", pmask, [128, NC])
    w1t, bw1 = load("w1_t", w1, [64, 64])
    w2t = [load(f"w2_{d}", w2p[d], [64, 128]) for d in range(2)]
    p1t, bp1 = load("pr1_t", pr1, [64, 2])
    p2t = [load(f"pr2_{d}", pr2[d], [128, 3]) for d in range(2)]
    w3t = [load(f"w3_{o}", w3s[o], [128, C]) for o in range(2)]
    dl, bdl = load("dlt_t", dlt, [128, C])
    cst = sb_alloc(nc, "cst" + tg, [128, 2], F32)
    bcst = Buf()
    c.op("pool", lambda e: e.memset(cst[:, 0:1], -np.pi), writes=[bcst])
    c.op("pool", lambda e: e.memset(cst[:, 1:2], EPS), writes=[bcst])
    onesf = sb_alloc(nc, "onesf" + tg, [128, 128], F32)
    bof = Buf()
    c.op("pool", lambda e: e.memset(onesf[:], 1.0), writes=[bof])
    ntp = sb_alloc(nc, "ntp" + tg, [128, NC], F32)
    bntp = Buf()
    c.op("dve", lambda e: e.tensor_scalar(out=ntp[:], in0=tp[:], scalar1=-1.0, scalar2=None, op0=ALU.mult),
         reads=[btp], writes=[bntp])
    SH = 33.0 * np.pi
    TWO_PI = 2.0 * np.pi
    tmp = Rot(nc, "ftmp" + tg, 6, [128, 512], F32)
    h1 = sb_alloc(nc, "h1" + tg, [64, N], F32)
    bh1 = [Buf() for _ in range(NBLK)]
    S2 = sb_alloc(nc, "S2" + tg, [128, N], F32)
    bS2 = [Buf() for _ in range(NBLK)]

    itmp = Rot(nc, "fitmp" + tg, 2, [128, 512], mybir.dt.int32)

    def sin_layer(p, bp, rows, bcol, fcol, bpar, out_ap, out_buf, mcol=None):
        t, bt = tmp.next()
        it, bit = itmp.next()
        kf, bkf = tmp.next()
        c.op("dve", lambda e: e.tensor_scalar(out=t[0:rows, 0:BS], in0=p[0:rows, 0:BS], scalar1=bcol, scalar2=fcol,
                                              op0=ALU.add, op1=ALU.mult), reads=[bp, bpar], writes=[bt])
        c.op("dve", lambda e: e.tensor_scalar(out=t[0:rows, 0:BS], in0=t[0:rows, 0:BS], scalar1=1.0 / TWO_PI, scalar2=None,
                                              op0=ALU.mult), reads=[bt], writes=[bt])
        c.op("dve", lambda e: e.tensor_copy(out=it[0:rows, 0:BS], in_=t[0:rows, 0:BS]), reads=[bt], writes=[bit])
        c.op("dve", lambda e: e.tensor_copy(out=kf[0:rows, 0:BS], in_=it[0:rows, 0:BS]), reads=[bit], writes=[bkf])
        c.op("dve", lambda e: e.tensor_tensor(out=t[0:rows, 0:BS], in0=t[0:rows, 0:BS], in1=kf[0:rows, 0:BS], op=ALU.subtract),
             reads=[bt, bkf], writes=[bt])
        if mcol is None:
            c.op("act", lambda e: e.activation(out=out_ap, in_=t[0:rows, 0:BS], func=AF.Sin, scale=TWO_PI),
                 reads=[bt], writes=[out_buf])
        else:
            c.op("act", lambda e: e.activation(out=t[0:rows, 0:BS], in_=t[0:rows, 0:BS], func=AF.Sin, scale=TWO_PI),
                 reads=[bt], writes=[bt])
            c.op("dve", lambda e: e.tensor_scalar(out=out_ap, in0=t[0:rows, 0:BS], scalar1=mcol, scalar2=None, op0=ALU.mult),
                 reads=[bt, bpar], writes=[out_buf])

    for blk in range(NBLK):
        sl = slice(blk * BS, (blk + 1) * BS)
        p, bp = psum.next()
        c.op("pe", lambda e, p=p, sl=sl: e.matmul(p[0:64, 0:BS], lhsT=w1t[:], rhs=ft[:, sl], start=True, stop=True),
             reads=[bw1, bft], writes=[bp])
        sin_layer(p, bp, 64, p1t[:, 0:1], p1t[:, 1:2], bp1, h1[:, sl], bh1[blk])
        d = 0 if blk * BS < L else 1
        w2d, bw2 = w2t[d]
        p2d, bp2 = p2t[d]
        p, bp = psum.next()
        c.op("pe", lambda e, p=p, sl=sl, w2d=w2d: e.matmul(p[:, 0:BS], lhsT=w2d[:], rhs=h1[:, sl], start=True, stop=True),
             reads=[bw2, bh1[blk]], writes=[bp])
        sin_layer(p, bp, 128, p2d[:, 0:1], p2d[:, 1:2], bp2, S2[:, sl], bS2[blk], mcol=p2d[:, 2:3])
    S2v = S2[:].rearrange("p (a b) -> p b a", b=NC)
    for o in range(2):
        sc = fft.scratch(f"f{o}")
        sc["bH"] = bH
        w3o, bw3 = w3t[o]
        acc = sb_alloc(nc, f"acc{o}{tg}", [128, C], F32)
        bacc = Buf()
        c.op("pool", lambda e, acc=acc: e.memset(acc[:], 0.0), writes=[bacc])
        for n2 in range(NC):
            p, bp = psum.next()
            c.op("pe", lambda e, p=p, n2=n2, w3o=w3o: e.matmul(p[:, 0:C], lhsT=S2v[:, n2, :], rhs=w3o[:], start=True, stop=True),
                 reads=bS2 + [bw3], writes=[bp])
            win, bwin = tmp.next()
            c.op("act", lambda e, win=win, n2=n2: e.activation(out=win[:, 0:C], in_=dl[:], func=AF.Exp,
                                                               scale=ntp[:, n2:n2 + 1]),
                 reads=[bdl, bntp], writes=[bwin])
            tap, btap = tmp.next()
            c.op("dve", lambda e, tap=tap, p=p, win=win: e.tensor_tensor(out=tap[:, 0:C], in0=p[:, 0:C], in1=win[:, 0:C],
                                                                        op=ALU.mult), reads=[bp, bwin], writes=[btap])
            c.op("dve", lambda e, tap=tap, n2=n2: e.tensor_scalar(out=tap[:, 0:C], in0=tap[:, 0:C],
                                                                  scalar1=pm[:, n2:n2 + 1], scalar2=None, op0=ALU.mult),
                 reads=[btap, bpm], writes=[btap])
            ab, bab = tmp.next()
            c.op("act", lambda e, ab=ab, tap=tap: e.activation(out=ab[:, 0:C], in_=tap[:, 0:C], func=AF.Abs),
                 reads=[btap], writes=[bab])
            c.op("pool", lambda e, ab=ab, acc=acc: e.tensor_tensor(out=acc[:], in0=acc[:], in1=ab[:, 0:C], op=ALU.add),
                 reads=[bab, bacc], writes=[bacc])
            zb, bzb = fft.ev.next()
            c.op("act", lambda e, zb=zb, tap=tap: e.copy(out=zb[:], in_=tap[:, 0:C]), reads=[btap], writes=[bzb])
            fft.stage_a(sc, n2, zb[:], bzb)
        p, bp = psum.next()
        c.op("pe", lambda e, p=p, acc=acc: e.matmul(p[:, 0:C], lhsT=onesf[:], rhs=acc[:], start=True, stop=True),
             reads=[bof, bacc], writes=[bp])
        rn = sb_alloc(nc, f"rn{o}{tg}", [128, C], F32)
        brn = Buf()
        c.op("dve", lambda e, rn=rn, p=p: e.tensor_scalar(out=rn[:], in0=p[:, 0:C], scalar1=EPS, scalar2=None, op0=ALU.add),
             reads=[bp], writes=[brn])
        c.op("dve", lambda e, rn=rn: e.reciprocal(out=rn[:], in_=rn[:]), reads=[brn], writes=[brn])
        for k1 in range(NK1):
            fft.stage_b(sc, k1, None, None, None, spectrum_out=Hout[o], scale_tab=(rn, brn))


def build_filt_prog(L, NC):
    nc = bass.Bass("TRN2", target_bir_lowering=False)
    Hout = nc.dram_tensor("Hout", [2, 2, NK1, NC, HC], F32, kind="ExternalOutput").ap()
    c = Ctx(nc)
    psum = Rot(nc, "psf", 8, [128, 512], F32, psum=True)
    for t, b in psum.tiles:
        b.excl = True
    fft = FFTConv(nc, c, "ff", NC, 128, psum)
    din = filt_dram_inputs(nc, "", L, NC)
    bH = Buf()
    emit_filter(nc, c, psum, fft, "", L, NC, din, Hout, bH)
    c.wait_all("sp", [bH])
    c.emit()
    return nc, fft.host


def filt_inputs(inp, l, hf, L, NC, fft_host):
    ch = slice(hf * HC, (hf + 1) * HC)
    featT, tpos, pmask = filt_consts(L, NC)
    w1 = np.zeros((64, 64), np.float32)
    w1[:33] = inp["hy_f_w1"][l]
    w2 = inp["hy_f_w2"][l]
    z = np.zeros_like(w2)
    w2p = np.stack([np.concatenate([w2, z], 1), np.concatenate([z, w2], 1)])
    pr1 = np.stack([inp["hy_f_b1"][l], inp["hy_f_freq1"][l]], axis=1)
    b2 = np.tile(inp["hy_f_b2"][l], 2)
    f2 = np.tile(inp["hy_f_freq2"][l], 2)
    hm = [np.concatenate([np.ones(64), np.zeros(64)]), np.concatenate([np.zeros(64), np.ones(64)])]
    pr2 = np.stack([np.stack([b2, f2, hm[d]], axis=1) for d in range(2)]).astype(np.float32)
    w3 = inp["hy_f_w3"][l].reshape(64, 2, 2, 1024)
    w3s = np.stack([np.concatenate([w3[:, 0, o, ch], w3[:, 1, o, ch]], axis=0) for o in range(2)])
    mx, mn = np.log(1e-2) / 0.3, np.log(1e-2) / 1.5
    deltas = np.abs(np.linspace(mn, mx, 1024)).astype(np.float32)[ch]
    m = {"featT": featT, "tpos": tpos, "pmask": pmask, "w1": w1, "w2p": np.ascontiguousarray(w2p.astype(np.float32)),
         "pr1": np.ascontiguousarray(pr1.astype(np.float32)), "pr2": np.ascontiguousarray(pr2),
         "w3s": np.ascontiguousarray(w3s.astype(np.float32)),
         "dlt": np.ascontiguousarray(np.broadcast_to(deltas, (128, HC)))}
    m.update(fft_host)
    return m


TFP = TF + 2
CTX0 = 1
LAT0 = CTX + 2


def hy_dram_inputs(nc, with_ctx, hTp=None, tag="", pos=None):
    C = HC
    d = {}
    d["hTp"] = hTp if hTp is not None else nc.dram_tensor("hTp", [D, TFP], BF16, kind="ExternalInput").ap()
    d["wh"] = nc.dram_tensor("wh" + tag, [D, 3 * C], F32, kind="ExternalInput").ap()
    d["cw"] = nc.dram_tensor("cw" + tag, [64, 3, 3 * C], F32, kind="ExternalInput").ap()
    d["cb"] = nc.dram_tensor("cb" + tag, [64, 3 * C], F32, kind="ExternalInput").ap()
    d["hb"] = nc.dram_tensor("hb" + tag, [64, 2, C], F32, kind="ExternalInput").ap()
    d["fL"] = filt_dram_inputs(nc, "L", SEQ, 64, wtag=tag, pos=pos)
    if with_ctx:
        d["fC"] = filt_dram_inputs(nc, "C", CTX, 4, shared=d["fL"], wtag=tag, pos=pos)
    return d


def emit_hy(nc, c, psum, din, zlat, zctx, b_out, with_ctx):
    C = HC
    hT, wh, cw, cb, hb = din["hTp"], din["wh"], din["cw"], din["cb"], din["hb"]
    u = c.uid("hy")
    Hlat = nc.dram_tensor(u + "Hlat", [2, 2, NK1, 64, C], F32).ap()
    Hctx = nc.dram_tensor(u + "Hctx", [2, 2, NK1, 4, C], F32).ap() if with_ctx else None
    host = {}
    bHl, bHc = Buf(), Buf()
    dl = din["fL"]
    with phase_stack():
        fft = FFTConv(nc, c, "fL", 64, 128, psum, uniq=u)
        host.update(fft.host)
        emit_filter(nc, c, psum, fft, "L", SEQ, 64, dl, Hlat, bHl)
    c.barrier()
    if with_ctx:
        dc = din["fC"]
        with phase_stack():
            fft = FFTConv(nc, c, "fC", 4, 128, psum, uniq=u)
            host.update(fft.host)
            emit_filter(nc, c, psum, fft, "C", CTX, 4, dc, Hctx, bHc)
        c.barrier()
    H = sb_alloc(nc, "H", [128, 8, TFP], BF16)
    bHt = Buf()
    hsrc = hT.rearrange("(k p) t -> p k t", p=128)
    for k in range(8):
        c.dma("sp", H[:, k, :], hsrc[:, k, :], writes=[bHt])
    cwt = sb_alloc(nc, "cwt", [64, 3, C], F32)
    cbt = sb_alloc(nc, "cbt", [64, C], F32)
    hbt = sb_alloc(nc, "hbt", [64, 2, C], F32)
    bct = Buf()
    c.dma("sp", hbt[:], hb, writes=[bct])
    Wf = sb_alloc(nc, "Wf", [128, 8, C], BF16)
    bWf = Buf()
    ur = Rot(nc, "ur", 5, [64, C], F32)
    cr = Rot(nc, "cr", 5, [64, C], F32)
    zb = Rot(nc, "zbf", 3, [64, C], BF16)
    ob = Rot(nc, "obf", 2, [64, C], BF16)
    hl = Rot(nc, "hl", 4, [64, C], F32)
    zs = Rot(nc, "zs", 2, [64, C], F32)

    def load_family(fam):
        c.dma("pool", Wf[:], wh[:, fam * C:(fam + 1) * C].rearrange("(k p) n -> p k n", p=128), writes=[bWf])
        c.dma("sp", cwt[:], cw[:, :, fam * C:(fam + 1) * C], writes=[bct])
        c.dma("sp", cbt[:], cb[:, fam * C:(fam + 1) * C], writes=[bct])

    def project(tok0, stride):
        p, bp = psum.next()
        for k in range(8):
            c.op("pe", lambda e, p=p, k=k: e.matmul(p[0:64, 0:C], lhsT=H[:, k, tok0:tok0 + 63 * stride + 1:stride],
                                                    rhs=Wf[:, k, :], start=(k == 0), stop=(k == 7)),
                 reads=[bHt, bWf], writes=[bp])
        u, bu = ur.next()
        c.op("act", lambda e, u=u, p=p: e.copy(out=u[:], in_=p[0:64, 0:C]), reads=[bp], writes=[bu])
        return u, bu

    def conv3(fam, um, u0, up):
        a, ba = cr.next()
        c.op("dve", lambda e: e.tensor_tensor(out=a[:], in0=u0[0][:], in1=cwt[:, 1, :], op=ALU.mult),
             reads=[u0[1], bct], writes=[ba])
        c.op("dve", lambda e: e.tensor_tensor(out=a[:], in0=a[:], in1=cbt[:], op=ALU.add),
             reads=[ba, bct], writes=[ba])
        t2 = None
        if up is not None:
            t2, bt2 = cr.next()
            c.op("pool", lambda e: e.tensor_tensor(out=t2[:], in0=up[0][:], in1=cwt[:, 2, :], op=ALU.mult),
                 reads=[up[1], bct], writes=[bt2])
        if um is not None:
            t, bt = cr.next()
            c.op("dve", lambda e: e.tensor_tensor(out=t[:], in0=um[0][:], in1=cwt[:, 0, :], op=ALU.mult),
                 reads=[um[1], bct], writes=[bt])
            c.op("dve", lambda e: e.tensor_tensor(out=a[:], in0=a[:], in1=t[:], op=ALU.add),
                 reads=[ba, bt], writes=[ba])
        if t2 is not None:
            c.op("dve", lambda e: e.tensor_tensor(out=a[:], in0=a[:], in1=t2[:], op=ALU.add),
                 reads=[ba, bt2], writes=[ba])
        return a, ba

    class ColStream:
        def __init__(s2, fam, NC, base, plain_seq):
            s2.fam, s2.NC, s2.base, s2.plain = fam, NC, base, plain_seq
            s2.cache = {}

        def u(s2, n2):
            if n2 not in s2.cache:
                s2.cache[n2] = project(s2.base + n2, s2.NC)
                for k in [k for k in s2.cache if k < n2 - 2]:
                    del s2.cache[k]
            return s2.cache[n2]

        def col(s2, n2):
            if s2.plain:
                um = project(s2.base + n2 - 1, s2.NC)
                u0 = project(s2.base + n2, s2.NC)
                up = project(s2.base + n2 + 1, s2.NC)
                return conv3(s2.fam, um, u0, up)
            um = s2.u(n2 - 1) if n2 > 0 else None
            u0 = s2.u(n2)
            up = s2.u(n2 + 1) if n2 < s2.NC - 1 else None
            return conv3(s2.fam, um, u0, up)

    shared_fft = [None]

    def run_conv(tag, NC, base, plain, Hs, bHs, zout):
        fftc = FFTConv(nc, c, "d" + tag, NC, 64, psum, share=shared_fft[0], uniq=u)
        if shared_fft[0] is None:
            shared_fft[0] = fftc
        host.update(fftc.host)
        zst = [nc.dram_tensor(f"{u}zst{tag}{o}", [NC, 64, C], F32).ap() for o in range(2)]
        bzst = [Buf(), Buf()]
        sc = [fftc.scratch(f"c{o}") for o in range(2)]
        load_family(0)
        st = ColStream(0, NC, base, plain)
        for n2 in range(NC):
            a, ba = st.col(n2)
            c.dma("sp", zst[0][n2], a[:], reads=[ba], writes=[bzst[0]])
            z, bz = zb.next()
            c.op("act", lambda e, z=z, a=a: e.copy(out=z[:], in_=a[:]), reads=[ba], writes=[bz])
            fftc.stage_a(sc[0], n2, z[:], bz)
        for o in range(2):
            for k1 in range(NK1):
                hr, bhr = hl.next()
                hi, bhi = hl.next()
                c.dma("sp", hr[0:NC, :], Hs[o, 0, k1], reads=[bHs], writes=[bhr])
                c.dma("sp", hi[0:NC, :], Hs[o, 1, k1], reads=[bHs], writes=[bhi])
                fftc.stage_b_h(sc[o], k1, hr, bhr, hi, bhi)
            load_family(1 + o)
            st = ColStream(1 + o, NC, base, plain)
            for m2 in range(NC):
                p, bp = fftc.stage_ainv(sc[o], m2)
                g, bg = st.col(m2)
                zp, bzp = zs.next()
                c.dma("sp", zp[:], zst[o][m2], reads=[bzst[o]], writes=[bzp])
                t, bt = cr.next()
                c.op("dve", lambda e, t=t, zp=zp, o=o: e.tensor_tensor(out=t[:], in0=zp[:], in1=hbt[:, o, :], op=ALU.mult),
                     reads=[bzp, bct], writes=[bt])
                c.op("dve", lambda e, t=t, p=p: e.tensor_tensor(out=t[:], in0=t[:], in1=p[0:64, 0:C], op=ALU.add),
                     reads=[bt, bp], writes=[bt])
                c.op("dve", lambda e, t=t, g=g: e.tensor_tensor(out=t[:], in0=t[:], in1=g[:], op=ALU.mult),
                     reads=[bt, bg], writes=[bt])
                if o == 0:
                    c.dma("sp", zst[1][m2], t[:], reads=[bt], writes=[bzst[1]])
                    z, bz = zb.next()
                    c.op("act", lambda e, z=z, t=t: e.copy(out=z[:], in_=t[:]), reads=[bt], writes=[bz])
                    fftc.stage_a(sc[1], m2, z[:], bz)
                else:
                    oo, boo = ob.next()
                    c.op("act", lambda e, oo=oo, t=t: e.copy(out=oo[:], in_=t[:]), reads=[bt], writes=[boo])
                    c.dma("st", zout[:, m2, :], oo[:], reads=[boo], writes=[b_out])

    run_conv("L", 64, LAT0, False, Hlat, bHl, zlat)
    if with_ctx:
        run_conv("C", 4, CTX0, True, Hctx, bHc, zctx)
    return host


def build_hy_prog(with_ctx):
    nc = bass.Bass("TRN2", target_bir_lowering=False)
    c = Ctx(nc)
    psum = make_psum(nc)
    din = hy_dram_inputs(nc, with_ctx)
    zlat = nc.dram_tensor("zlat", [64, 64, HC], BF16, kind="ExternalOutput").ap()
    zctx = nc.dram_tensor("zctx", [64, 4, HC], BF16, kind="ExternalOutput").ap() if with_ctx else None
    b_out = Buf()
    with phase_stack():
        host = emit_hy(nc, c, psum, din, zlat, zctx, b_out, with_ctx)
    c.wait_all("sp", [b_out])
    c.emit()
    return nc, host


def build_mixers_prog(with_ctx):
    nc = bass.Bass("TRN2", target_bir_lowering=False)
    c = Ctx(nc)
    psum = make_psum(nc)
    dg = gla_dram_inputs(nc)
    dh = hy_dram_inputs(nc, with_ctx)
    ogT = nc.dram_tensor("ogT", [512, TF], BF16, kind="ExternalOutput").ap()
    zlat = nc.dram_tensor("zlat", [64, 64, HC], BF16, kind="ExternalOutput").ap()
    zctx = nc.dram_tensor("zctx", [64, 4, HC], BF16, kind="ExternalOutput").ap() if with_ctx else None
    b_out = Buf()
    with phase_stack():
        emit_gla(nc, c, psum, dg, ogT, b_out)
    c.barrier()
    with phase_stack():
        host = emit_hy(nc, c, psum, dh, zlat, zctx, b_out, with_ctx)
    c.wait_all("sp", [b_out])
    c.emit()
    return nc, host


def hy_inputs_for_core(inp, l, core, hT_full, with_ctx, host_consts):
    b, hf = divmod(core, 2)
    ch = slice(hf * HC, (hf + 1) * HC)
    cols = np.concatenate([np.arange(OFF_H + f * 1024 + hf * HC, OFF_H + f * 1024 + (hf + 1) * HC) for f in range(3)])
    ccols = np.concatenate([np.arange(f * 1024 + hf * HC, f * 1024 + (hf + 1) * HC) for f in range(3)])
    m = {"wh": np.ascontiguousarray(inp["w_in"][l][:, cols]),
         "cw": np.ascontiguousarray(np.broadcast_to(inp["hy_conv_w"][l][:, ccols], (64, 3, 3 * HC))),
         "cb": np.ascontiguousarray(np.broadcast_to(inp["hy_conv_b"][l][ccols], (64, 3 * HC))),
         "hb": np.ascontiguousarray(np.broadcast_to(inp["hy_bias"][l][:, ch], (64, 2, HC)))}
    if hT_full is not None:
        hp = np.zeros((D, TFP), hT_full.dtype)
        hp[:, CTX0:CTX0 + CTX] = hT_full[:, :CTX]
        hp[:, LAT0:LAT0 + SEQ] = hT_full[:, CTX:]
        m["hTp"] = hp
    fl = filt_inputs(inp, l, hf, SEQ, 64, {})
    for k in ("featT", "tpos", "pmask"):
        m[k + "L"] = fl.pop(k)
    m.update(fl)
    if with_ctx:
        fc = filt_inputs(inp, l, hf, CTX, 4, {})
        for k in ("featT", "tpos", "pmask"):
            m[k + "C"] = fc[k]
    m.update(host_consts)
    return m


def _mods_cols(mods, core, cols):
    b = core // 2
    out = np.zeros((128, 8, 2, len(cols)), np.float32)
    for sid, row in enumerate((4, b)):
        for j, (l, m) in enumerate(cols):
            out[:, :, sid, j] = mods[l, row, m].reshape(8, 128).T
    return out


def _gv(vecs):
    return np.ascontiguousarray(np.stack([fmv(v) for v in vecs], axis=-1))


def _split_tokens(resT, with_ctx):
    lat = np.zeros((NB, SEQ, resT[0].shape[0]), resT[0].dtype)
    ctx = np.zeros((NB, CTX, resT[0].shape[0]), resT[0].dtype) if with_ctx else None
    for core in range(NCORES):
        b, hf = divmod(core, 2)
        a = resT[core].T
        o = 0
        if with_ctx:
            ctx[b, hf * TC:(hf + 1) * TC] = a[:TC]
            o = TC
        lat[b, hf * TL:(hf + 1) * TL] = a[o:]
    return ctx, lat


def kernel_multilaunch(**inp):
    inp = {k: np.asarray(v) for k, v in inp.items()}
    mods = run_ada(inp)
    segs_all = [(0, TC, 0), (TC, TT, 1)]
    segs_lat = [(0, TL, 1)]
    xl, xc = inp["x"], inp["ctx"]
    nc = get_prog(("ffn", 1, "mixnorm"), lambda: build_ffn_prog(1, "mixnorm", TT, segs_all))
    maps = []
    for core in range(NCORES):
        maps.append({"xT": core_tokens_T(xl, xc, core),
                     "mods": _mods_cols(mods, core, [(0, 0), (0, 1), (0, 2), (0, 3), (0, 4)]),
                     "gv": _gv([inp["ffn1_norm_g"][0], inp["mix_norm_g"][0]]),
                     "wg0": inp["ffn1_w_gate"][0], "wu0": inp["ffn1_w_up"][0], "wd0": inp["ffn1_w_down"][0]})
    res = run(nc, maps)
    xc, xl = _split_tokens([r["xo"] for r in res], True)
    hc, hl = _split_tokens([r["ho"] for r in res], True)
    out = None
    for l in range(2):
        last = (l == 1)
        hT_full = [np.ascontiguousarray(np.concatenate([hc[b], hl[b]], axis=0).T) for b in range(NB)]
        wc = not last
        nc, hcst = get_prog(("mixers", wc), lambda: build_mixers_prog(wc))
        maps = []
        for core in range(NCORES):
            m = gla_inputs_for_core(inp, l, core, hT_full[core // 2])
            m.update(hy_inputs_for_core(inp, l, core, hT_full[core // 2], wc, hcst))
            maps.append(m)
        res = run(nc, maps)
        og = [np.concatenate([res[2 * b]["ogT"], res[2 * b + 1]["ogT"]], axis=0) for b in range(NB)]
        zz = []
        for b in range(NB):
            parts = []
            if wc:
                parts.append(np.concatenate([res[2 * b]["zctx"].reshape(CTX, HC), res[2 * b + 1]["zctx"].reshape(CTX, HC)], axis=1))
            else:
                parts.append(np.zeros((CTX, 2 * HC), res[0]["zlat"].dtype))
            parts.append(np.concatenate([res[2 * b]["zlat"].reshape(SEQ, HC), res[2 * b + 1]["zlat"].reshape(SEQ, HC)], axis=1))
            zz.append(np.ascontiguousarray(np.concatenate(parts, axis=0).T))
        T = TL if last else TT
        segs = segs_lat if last else segs_all
        nc = get_prog(("mix", T), lambda: build_mix_prog(T, segs))
        maps = []
        for core in range(NCORES):
            b, hf = divmod(core, 2)
            tok = ([] if last else [np.arange(hf * TC, (hf + 1) * TC)]) + [CTX + np.arange(hf * TL, (hf + 1) * TL)]
            tok = np.concatenate(tok)
            maps.append({"xT": core_tokens_T(xl, xc, core, with_ctx=not last),
                         "hT": np.ascontiguousarray(hT_full[b][:, tok]),
                         "ogT": np.ascontiguousarray(og[b][:, tok]), "zT": np.ascontiguousarray(zz[b][:, tok]),
                         "mods": _mods_cols(mods, core, [(l, 5)]),
                         "wgla": inp["w_gla_out"][l], "why": inp["w_hy_out"][l],
                         "wgt": np.ascontiguousarray(inp["w_in"][l][:, OFF_G:OFF_G + 2 * D]), "wout": inp["w_out"][l]})
        res = run(nc, maps)
        xc2, xl = _split_tokens([r["xo"] for r in res], not last)
        if not last:
            xc = xc2
            nc = get_prog(("ffn", 2, "mixnorm"), lambda: build_ffn_prog(2, "mixnorm", TT, segs_all))
            maps = []
            for core in range(NCORES):
                maps.append({"xT": core_tokens_T(xl, xc, core),
                             "mods": _mods_cols(mods, core, [(0, 6), (0, 7), (0, 8), (1, 0), (1, 1), (1, 2), (1, 3), (1, 4)]),
                             "gv": _gv([inp["ffn2_norm_g"][0], inp["ffn1_norm_g"][1], inp["mix_norm_g"][1]]),
                             "wg0": inp["ffn2_w_gate"][0], "wu0": inp["ffn2_w_up"][0], "wd0": inp["ffn2_w_down"][0],
                             "wg1": inp["ffn1_w_gate"][1], "wu1": inp["ffn1_w_up"][1], "wd1": inp["ffn1_w_down"][1]})
            res = run(nc, maps)
            xc, xl = _split_tokens([r["xo"] for r in res], True)
            hc, hl = _split_tokens([r["ho"] for r in res], True)
        else:
            nc = get_prog(("ffn", 1, "final"), lambda: build_ffn_prog(1, "final", TL, segs_lat))
            maps = []
            for core in range(NCORES):
                maps.append({"xT": core_tokens_T(xl, xc, core, with_ctx=False),
                             "mods": _mods_cols(mods, core, [(1, 6), (1, 7), (1, 8), (1, 0), (1, 0)]),
                             "gv": _gv([inp["ffn2_norm_g"][1], inp["final_norm_g"]]),
                             "wg0": inp["ffn2_w_gate"][1], "wu0": inp["ffn2_w_up"][1], "wd0": inp["ffn2_w_down"][1]})
            res = run(nc, maps)
            _, out = _split_tokens([r["xo"] for r in res], False)
    return np.ascontiguousarray(out.astype(np.float32))


NMODCH = 2 * 9 * 8


def emit_ada_local(nc, c, psum, cs32, adaW, adaB, MODS, b_mods):
    cst = sb_alloc(nc, "ada_cs", [128, 8, 32], F32)
    sct = sb_alloc(nc, "ada_sc", [128, 8, 32], F32)
    bt = sb_alloc(nc, "ada_b", [128, NMODCH], F32)
    b_cs, b_sc, b_bt = Buf(), Buf(), Buf()
    c.dma("sp", cst[:], cs32, writes=[b_cs])
    c.dma("sp", bt[:], adaB, writes=[b_bt])
    c.op("act", lambda e: e.activation(out=sct[:], in_=cst[:], func=AF.Silu), reads=[b_cs], writes=[b_sc])
    wrot = Rot(nc, "ada_wp", 3, [128, 8, 512], F32)
    for pc in range(NMODCH // 4):
        wt, bw = wrot.next()
        c.dma("sp", wt[:], adaW[:, pc * 512:(pc + 1) * 512].rearrange("(k p) n -> p k n", p=128), writes=[bw])
        for j in range(4):
            ch = pc * 4 + j
            p, bp = psum.next()
            for k in range(8):
                c.op("pe", lambda e, p=p, wt=wt, k=k, j=j: e.matmul(
                    p[:, 0:32], lhsT=wt[:, k, j * 128:(j + 1) * 128], rhs=sct[:, k, :], start=(k == 0), stop=(k == 7)),
                    reads=[bw, b_sc], writes=[bp])
            c.op("dve", lambda e, p=p, ch=ch: e.tensor_scalar(out=MODS[:, ch, :], in0=p[:, 0:2], scalar1=bt[:, ch:ch + 1],
                                                               scalar2=None, op0=ALU.add),
                 reads=[bp, b_bt], writes=[b_mods])


def build_fused_prog():
    nc = bass.Bass("TRN2", target_bir_lowering=False)
    c = Ctx(nc)
    psum = make_psum(nc)
    ein = lambda name, shape, dt=F32: nc.dram_tensor(name, shape, dt, kind="ExternalInput").ap()
    xin = ein("xin", [D, TF])
    cs32 = ein("cs32", [128, 8, 32])
    adaW = ein("adaW", [D, NMODCH * 128])
    adaB = ein("adaB", [128, NMODCH])
    gvd = ein("gvec", [128, 8, 7])
    identd = ein("ident", [128, 128], BF16)
    zcol = ein("zcol", [128, 8], BF16)
    W_ffn = {(l, w): ffn_dram_weights(nc, 1, tag=f"_{l}{w}")[0] for l in range(2) for w in (1, 2)}
    W_mix = [mix_dram_weights(nc, tag=f"_{l}") for l in range(2)]
    out = nc.dram_tensor("out", [D, SEQ], F32, kind="ExternalOutput").ap()
    Xs = nc.dram_tensor("Xs", [D, TF], F32).ap()
    Hs = nc.dram_tensor("Hs", [D, TF], BF16).ap()
    Hp = nc.dram_tensor("Hp", [D, TFP], BF16).ap()
    OGs = nc.dram_tensor("OGs", [2 * HC, TF], BF16).ap()
    ZL = nc.dram_tensor("ZL", [64, 64, 2 * HC], BF16).ap()
    ZC = nc.dram_tensor("ZC", [64, 4, 2 * HC], BF16).ap()
    bXs, bHs, bOG, bZ = Buf(), Buf(), Buf(), Buf()
    shared_consts, pos = {}, {}
    dG = {(l, q): gla_dram_inputs(nc, hT=Hs, tag=f"_{l}{q}", shared=shared_consts) for l in range(2) for q in range(2)}
    dH = {(l, q): hy_dram_inputs(nc, l == 0, hTp=Hp, tag=f"_{l}{q}", pos=pos) for l in range(2) for q in range(2)}
    host = {}
    MODS = sb_alloc(nc, "MODS", [128, NMODCH, 2], F32)
    b_mods = Buf()
    gt_all = sb_alloc(nc, "gvec_t", [128, 8, 7], F32)
    b_gall = Buf()
    c.dma("sp", gt_all[:], gvd, writes=[b_gall])
    zc_t = sb_alloc(nc, "zcol_t", [128, 8], BF16)
    b_zc = Buf()
    c.dma("sp", zc_t[:], zcol, writes=[b_zc])
    hp_v = Hp.rearrange("(k p) t -> p k t", p=128)
    for col in (0, CTX0 + CTX):
        c.dma("sp", hp_v[:, :, col], zc_t[:], reads=[b_zc], writes=[bHs], allow_slow_non_contiguous=True)
    with phase_stack():
        emit_ada_local(nc, c, psum, cs32, adaW, adaB, MODS, b_mods)
    c.barrier()

    def mods_table(cols):
        mt = sb_alloc(nc, "mt", [128, 8, 2, len(cols)], F32)
        b_m = Buf()
        for j, (l, m) in enumerate(cols):
            ch0 = (l * 9 + m) * 8
            for sid in range(2):
                c.op("dve", lambda e, mt=mt, j=j, sid=sid, ch0=ch0: e.tensor_copy(out=mt[:, :, sid, j],
                                                                                  in_=MODS[:, ch0:ch0 + 8, sid]),
                     reads=[b_mods], writes=[b_m])
        return mt, b_m

    def gv_table(idx):
        gt = sb_alloc(nc, "gt", [128, 8, len(idx)], F32)
        b_g = Buf()
        for j, i in enumerate(idx):
            c.op("dve", lambda e, gt=gt, j=j, i=i: e.tensor_copy(out=gt[:, :, j], in_=gt_all[:, :, i]),
                 reads=[b_gall], writes=[b_g])
        return gt, b_g

    def gcol(hh, with_ctx):
        def f(t0):
            if with_ctx:
                return hh * TC + t0 if t0 < TC else CTX + hh * TL + (t0 - TC)
            return CTX + hh * TL + t0
        return f

    xs_v = Xs.rearrange("(k p) t -> p k t", p=128)
    xin_v = xin.rearrange("(k p) t -> p k t", p=128)
    hs_v = Hs.rearrange("(k p) t -> p k t", p=128)
    og_v = OGs.rearrange("(k p) t -> p k t", p=128)
    segs_all = [(0, TC, 0), (TC, TT, 1)]
    segs_lat = [(0, TL, 1)]

    def f_phase(hh, stages, first):
        g = gcol(hh, True)
        src = xin_v if first else xs_v
        with phase_stack():
            env = setup_common(nc, c, psum)
            cols, gidx, ws = [], [], []
            for (l, w) in stages:
                base = 0 if w == 1 else 6
                cols += [(l, base), (l, base + 1), (l, base + 2)]
                gidx.append(3 * l + (0 if w == 1 else 2))
                ws.append(W_ffn[(l, w)])
            lt = stages[-1][0] + (0 if stages[-1][1] == 1 else 1)
            cols += [(lt, 3), (lt, 4)]
            gidx.append(3 * lt + 1)
            mt, b_m = mods_table(cols)
            gt, b_g = gv_table(gidx)

            def hp_of(t0, t1):
                g0 = g(t0)
                off = CTX0 if g0 < CTX else LAT0 - CTX
                return hp_v[:, :, g0 + off:g0 + off + (t1 - t0)]
            bo = emit_ffn_chain(nc, c, env, len(stages), "mixnorm", TT, segs_all,
                                lambda t0, t1: src[:, :, g(t0):g(t0) + (t1 - t0)], mt, b_m, gt, b_g, ws,
                                lambda t0, t1: xs_v[:, :, g(t0):g(t0) + (t1 - t0)],
                                [lambda t0, t1: hs_v[:, :, g(t0):g(t0) + (t1 - t0)], hp_of])
            c.wait_all("sp", [bo])
        c.barrier()

    def mixers_phase(l):
        for q in range(2):
            with phase_stack():
                emit_gla(nc, c, psum, dG[(l, q)], OGs[q * HC:(q + 1) * HC, :], bOG)
            c.barrier()
            with phase_stack():
                host.update(emit_hy(nc, c, psum, dH[(l, q)], ZL[:, :, q * HC:(q + 1) * HC],
                                    ZC[:, :, q * HC:(q + 1) * HC] if l == 0 else None, bZ, l == 0))
            c.barrier()

    zl_flat = ZL.rearrange("r c f -> (r c) f")
    zc_flat = ZC.rearrange("r c f -> (r c) f")

    def mix_phase(l, hh, with_ctx, env, xdst_of=None):
        g = gcol(hh, with_ctx)
        T = TT if with_ctx else TL
        segs = segs_all if with_ctx else segs_lat
        mt, b_m = mods_table([(l, 5)])

        def z_tm(t0, t1):
            g0 = g(t0)
            if g0 < CTX:
                return zc_flat[g0:g0 + (t1 - t0), :]
            return zl_flat[g0 - CTX:g0 - CTX + (t1 - t0), :]
        rng = lambda v: (lambda t0, t1: v[:, :, g(t0):g(t0) + (t1 - t0)])
        src_of = {"x": rng(xs_v), "h": rng(hs_v), "og": rng(og_v)}
        return emit_mix(nc, c, env, T, segs, src_of, xdst_of or rng(xs_v), mt, b_m, W_mix[l], z_tm_of=z_tm)

    for hh in range(2):
        f_phase(hh, [(0, 1)], first=True)
    mixers_phase(0)
    for hh in range(2):
        with phase_stack():
            env = setup_common(nc, c, psum)
            load_ident(nc, c, env, identd)
            bo = mix_phase(0, hh, True, env)
            c.wait_all("sp", [bo])
        c.barrier()
    for hh in range(2):
        f_phase(hh, [(0, 2), (1, 1)], first=False)
    mixers_phase(1)
    out_v = out.rearrange("(k p) t -> p k t", p=128)
    for hh in range(2):
        with phase_stack():
            env = setup_common(nc, c, psum)
            load_ident(nc, c, env, identd)
            bo = mix_phase(1, hh, False, env)
            c.wait_all("sp", [bo])
        c.barrier()
    b_fin = Buf()
    for hh in range(2):
        g = gcol(hh, False)
        with phase_stack():
            env = setup_common(nc, c, psum)
            mt, b_m = mods_table([(1, 6), (1, 7), (1, 8), (1, 0), (1, 0)])
            gt, b_g = gv_table([5, 6])
            bo = emit_ffn_chain(nc, c, env, 1, "final", TL, segs_lat,
                                lambda t0, t1, g=g: xs_v[:, :, g(t0):g(t0) + (t1 - t0)], mt, b_m, gt, b_g,
                                [W_ffn[(1, 2)]],
                                lambda t0, t1, hh=hh: out_v[:, :, hh * TL + t0:hh * TL + t1])
            c.wait_all("sp", [bo])
        c.barrier()
    c.emit()
    return nc, host


def fused_inputs_for_batch(inp, b, host_consts):
    bf = ml_dtypes.bfloat16
    m = {}
    m["xin"] = np.ascontiguousarray(np.concatenate([inp["ctx"][b], inp["x"][b]], axis=0).T)
    cs = np.zeros((32, D), np.float32)
    cs[0] = inp["c_ctx"]
    cs[1] = inp["c"][b]
    m["cs32"] = np.ascontiguousarray(cs.reshape(32, 8, 128).transpose(2, 1, 0))
    m["adaW"] = np.ascontiguousarray(np.concatenate([inp["ada_w"][0], inp["ada_w"][1]], axis=1))
    m["adaB"] = np.ascontiguousarray(np.concatenate([inp["ada_b"][0], inp["ada_b"][1]]).reshape(NMODCH, 128).T)
    m["gvec"] = _gv([inp["ffn1_norm_g"][0], inp["mix_norm_g"][0], inp["ffn2_norm_g"][0],
                     inp["ffn1_norm_g"][1], inp["mix_norm_g"][1], inp["ffn2_norm_g"][1], inp["final_norm_g"]])
    m["ident"] = np.eye(128, dtype=np.float32).astype(bf)
    m["zcol"] = np.zeros((128, 8), bf)
    for l in range(2):
        for w, pre in ((1, "ffn1"), (2, "ffn2")):
            m[f"wg0_{l}{w}"] = inp[f"{pre}_w_gate"][l]
            m[f"wu0_{l}{w}"] = inp[f"{pre}_w_up"][l]
            m[f"wd0_{l}{w}"] = inp[f"{pre}_w_down"][l]
        m[f"wgla_{l}"] = inp["w_gla_out"][l]
        m[f"why_{l}"] = inp["w_hy_out"][l]
        m[f"wgt_{l}"] = np.ascontiguousarray(inp["w_in"][l][:, OFF_G:OFF_G + 2 * D])
        m[f"wout_{l}"] = inp["w_out"][l]
        for q in range(2):
            tag = f"_{l}{q}"
            g = gla_inputs_for_core(inp, l, q, None)
            for k in ("wk", "wq", "wv", "wr", "wa", "waf", "wab", "gng"):
                m[k + tag] = g[k]
            for k in ("atmpl",) + GLA_CNAMES:
                m[k] = g[k]
            h = hy_inputs_for_core(inp, l, q, None, l == 0, {})
            for k in ("wh", "cw", "cb", "hb", "w1", "w2p", "pr1", "pr2", "w3s", "dlt"):
                m[k + tag] = h[k]
            for k in h:
                if k.startswith(("featT", "tpos", "pmask")):
                    m[k] = h[k]
    m.update(host_consts)
    return m


def kernel_fused(**inp):
    inp = {k: np.asarray(v) for k, v in inp.items()}
    nc, hcst = get_prog("fused", build_fused_prog)
    per_batch = [fused_inputs_for_batch(inp, b, hcst) for b in range(NB)]
    res = run(nc, [per_batch[core // 2] for core in range(NCORES)])
    out = np.zeros((NB, SEQ, D), np.float32)
    for core in range(NCORES):
        b, hf = divmod(core, 2)
        out[b, hf * TL:(hf + 1) * TL] = res[core]["out"][:, hf * TL:(hf + 1) * TL].T
    return out


def kernel(**inp):
    return kernel_multilaunch(**inp)
```
